# Optimizing a Trainium2 kernel written in Bass

```python
import math
import jax, jax.numpy as jnp
from jax import lax
import numpy as np

D_MODEL = 2048
BATCH = 2
SEQ = 16384
DEPTH = 2
DEC_BATCH = 4
DEC_SEQ = 8192
PAST_LEN = 128

HGRN_DIM = 128
HGRN_HEADS = (D_MODEL // 2) // HGRN_DIM
HGRN_WIDTH = HGRN_HEADS * HGRN_DIM
HGRN_CHUNK = 64
RWKV_DIM = 64
RWKV_HEADS = (D_MODEL // 2) // RWKV_DIM
RWKV_WIDTH = RWKV_HEADS * RWKV_DIM
DECAY_LORA = 64
AAA_LORA = 64
GATE_LORA = 160
RWKV_COLS = 3 * RWKV_WIDTH + 2 * DECAY_LORA + AAA_LORA + GATE_LORA
RWKV_SPLITS = [RWKV_WIDTH, 2 * RWKV_WIDTH, 3 * RWKV_WIDTH, 3 * RWKV_WIDTH + DECAY_LORA,
               3 * RWKV_WIDTH + 2 * DECAY_LORA, 3 * RWKV_WIDTH + 2 * DECAY_LORA + AAA_LORA]
MIX_COLS = 5 * HGRN_WIDTH + RWKV_COLS
ATT_DIM = 128
ATT_HEADS = D_MODEL // ATT_DIM
ATT_KV_HEADS = ATT_HEADS // 4
ATT_GROUP = ATT_HEADS // ATT_KV_HEADS
QKV_COLS = (ATT_HEADS + 2 * ATT_KV_HEADS) * ATT_DIM
WINDOW = 128
BLOCK = 128
ROPE_THETA = 10000.0
D_FF = 256 * ((8 * D_MODEL // 3 + 255) // 256)
CONV_WIDTH = 3
N_AB = (DEPTH + 1) // 2
N_ATT = DEPTH // 2
RMS_EPS = 1e-6
GN_EPS = 64e-5

kernel_name = 'hybrid_hgrn2_rwkv7_swa_convffn_encoder'


def rmsnorm(x, g):
    xf = x.astype(jnp.float32)
    y = xf * lax.rsqrt(jnp.mean(xf * xf, axis=-1, keepdims=True) + RMS_EPS)
    return (y * g.astype(jnp.float32)).astype(x.dtype)


def flip_seq(t):
    return jnp.flip(t, axis=1)


def centred_shift(p):
    pp = jnp.pad(p, ((0, 0), (1, 1), (0, 0)))
    return 0.5 * (pp[:, :-2] + pp[:, 2:])


def hgrn2_scan(q, k, v, logf):
    B, T, H, Dk = q.shape
    Dv = v.shape[-1]
    C = HGRN_CHUNK
    N = T // C

    def blk(t):
        return t.reshape(B, N, C, H, t.shape[-1]).transpose(1, 0, 3, 2, 4)

    q, k, v, logf = blk(q), blk(k), blk(v), blk(logf)
    b = jnp.cumsum(logf, axis=3)
    b_last = b[:, :, :, -1:, :]
    q_d = q * jnp.exp(b)
    k_d = k * jnp.exp(-b)
    causal = jnp.tril(jnp.ones((C, C), dtype=bool))
    att = jnp.where(causal, jnp.einsum('nbhid,nbhjd->nbhij', q_d, k_d), 0.0)
    o_intra = jnp.einsum('nbhij,nbhje->nbhie', att, v)
    upd = jnp.einsum('nbhjd,nbhje->nbhde', k * jnp.exp(b_last - b), v)
    dec = jnp.exp(b_last[:, :, :, 0, :])

    def step(S, inp):
        d, u = inp
        return S * d[..., None] + u, S

    S0 = jnp.zeros((B, H, Dk, Dv), jnp.float32)
    _, S_prev = lax.scan(step, S0, (dec, upd))
    o = o_intra + jnp.einsum('nbhid,nbhde->nbhie', q_d, S_prev)
    return o.transpose(1, 0, 3, 2, 4).reshape(B, T, H, Dv)


def rwkv7_scan(r, w, k, v, a, b):
    B, T, H, N = r.shape

    def step(S, inp):
        r_t, w_t, k_t, v_t, a_t, b_t = inp
        sa = jnp.einsum('bhij,bhj->bhi', S, a_t)
        S = S * w_t[:, :, None, :] + sa[..., None] * b_t[:, :, None, :] + v_t[..., None] * k_t[:, :, None, :]
        return S, jnp.einsum('bhij,bhj->bhi', S, r_t)

    S0 = jnp.zeros((B, H, N, N), jnp.float32)
    xs = (jnp.moveaxis(r, 1, 0), jnp.moveaxis(w, 1, 0), jnp.moveaxis(k, 1, 0),
          jnp.moveaxis(v, 1, 0), jnp.moveaxis(a, 1, 0), jnp.moveaxis(b, 1, 0))
    _, y = lax.scan(step, S0, xs)
    return jnp.moveaxis(y, 0, 1)


def hgrn_rwkv_mixer(h, layer, w_in, lb_table, o_norm, mu, w0, w2, a0, a2, g2, k_k, k_a, r_k, ln_w, ln_b, w_out):
    B, T, _ = h.shape
    f32 = jnp.float32
    proj = (h @ w_in).astype(f32)
    pa = proj[..., :5 * HGRN_WIDTH]
    pb = proj[..., 5 * HGRN_WIDTH:]

    qa, ia, zf_fwd, zf_bwd, ga = jnp.split(pa, 5, axis=-1)
    lb = jnp.cumsum(jax.nn.softmax(lb_table.astype(f32), axis=0), axis=0)[layer]

    def ha(t):
        return t.reshape(B, T, HGRN_HEADS, HGRN_DIM)

    def forget(z):
        f = lb + (1.0 - lb) * jax.nn.sigmoid(z)
        return ha(1.0 - f), ha(jnp.log(f))

    k_fwd, lf_fwd = forget(zf_fwd)
    k_bwd, lf_bwd = forget(zf_bwd)
    qh, ih = ha(qa), ha(ia)
    oa = hgrn2_scan(qh, k_fwd, ih, lf_fwd) + flip_seq(
        hgrn2_scan(flip_seq(qh), flip_seq(k_bwd), flip_seq(ih), flip_seq(lf_bwd)))
    oa = oa * lax.rsqrt(jnp.mean(oa * oa, axis=-1, keepdims=True) + RMS_EPS)
    ya = oa.reshape(B, T, HGRN_WIDTH) * o_norm.astype(f32) * jax.nn.silu(ga)

    pb = pb + mu * (centred_shift(pb) - pb)
    r, k, v, wd_fwd, wd_bwd, ad, gd = jnp.split(pb, RWKV_SPLITS, axis=-1)

    def hb(t):
        return t.reshape(B, T, RWKV_HEADS, RWKV_DIM)

    def decay(wd, w0_, w2_):
        w_log = -jax.nn.softplus(-(w0_ + jnp.tanh(wd) @ w2_)) - 0.5
        return hb(jnp.exp(-jnp.exp(w_log)))

    w_fwd = decay(wd_fwd, w0[0], w2[0])
    w_bwd = decay(wd_bwd, w0[1], w2[1])
    a = jax.nn.sigmoid(a0 + ad @ a2)
    g = jax.nn.sigmoid(gd) @ g2
    kk = hb(k * k_k)
    kk = kk / jnp.maximum(jnp.sqrt(jnp.sum(kk * kk, axis=-1, keepdims=True)), 1e-12)
    k = k * (1.0 + (a - 1.0) * k_a)
    rh, kh, vh, ah = hb(r), hb(k), hb(v), hb(a)
    a_vec = -kk
    b_vec = kk * ah
    yb = rwkv7_scan(rh, w_fwd, kh, vh, a_vec, b_vec) + flip_seq(
        rwkv7_scan(flip_seq(rh), flip_seq(w_bwd), flip_seq(kh), flip_seq(vh), flip_seq(a_vec), flip_seq(b_vec)))
    mean = jnp.mean(yb, axis=-1, keepdims=True)
    var = jnp.mean(jnp.square(yb - mean), axis=-1, keepdims=True)
    yb = ((yb - mean) * lax.rsqrt(var + GN_EPS)).reshape(B, T, RWKV_WIDTH) * ln_w + ln_b
    bonus = (jnp.sum(rh * kh * r_k, axis=-1, keepdims=True) * vh).reshape(B, T, RWKV_WIDTH)
    yb = (yb + bonus) * g

    y = jnp.concatenate([ya, yb], axis=-1).astype(h.dtype)
    return y @ w_out


def rope(x):
    T = x.shape[1]
    half = ATT_DIM // 2
    inv = ROPE_THETA ** (-jnp.arange(half, dtype=jnp.float32) / half)
    ang = jnp.arange(T, dtype=jnp.float32)[:, None] * inv[None, :]
    cos = jnp.cos(ang)[:, None, :]
    sin = jnp.sin(ang)[:, None, :]
    xf = x.astype(jnp.float32)
    x1, x2 = xf[..., :half], xf[..., half:]
    return jnp.concatenate([x1 * cos - x2 * sin, x2 * cos + x1 * sin], axis=-1).astype(x.dtype)


def band_windows(t, nb):
    B, T, KV, D = t.shape
    tp = jnp.pad(t, ((0, 0), (BLOCK, BLOCK), (0, 0), (0, 0))).reshape(B, nb + 2, BLOCK, KV, D)
    return jnp.concatenate([tp[:, :-2], tp[:, 1:-1], tp[:, 2:]], axis=2)


def window_attention_mixer(h, w_qkv, sink, w_o):
    B, T, _ = h.shape
    qkv = h @ w_qkv
    q = qkv[..., :ATT_HEADS * ATT_DIM].reshape(B, T, ATT_HEADS, ATT_DIM)
    k = qkv[..., ATT_HEADS * ATT_DIM:(ATT_HEADS + ATT_KV_HEADS) * ATT_DIM].reshape(B, T, ATT_KV_HEADS, ATT_DIM)
    v = qkv[..., (ATT_HEADS + ATT_KV_HEADS) * ATT_DIM:].reshape(B, T, ATT_KV_HEADS, ATT_DIM)
    q, k = rope(q), rope(k)
    nb = T // BLOCK
    qb = q.reshape(B, nb, BLOCK, ATT_KV_HEADS, ATT_GROUP, ATT_DIM)
    kw = band_windows(k, nb)
    vw = band_windows(v, nb)
    s = jnp.einsum('bnqkgd,bnskd->bnkgqs', qb, kw).astype(jnp.float32) * (ATT_DIM ** -0.5)
    qpos = jnp.arange(nb)[:, None, None] * BLOCK + jnp.arange(BLOCK)[None, :, None]
    kpos = (jnp.arange(nb)[:, None, None] - 1) * BLOCK + jnp.arange(3 * BLOCK)[None, None, :]
    valid = (jnp.abs(kpos - qpos) <= WINDOW) & (kpos >= 0) & (kpos < T)
    s = jnp.where(valid[None, :, None, None], s, -jnp.inf)
    sk = sink.astype(jnp.float32).reshape(ATT_KV_HEADS, ATT_GROUP)[None, None, :, :, None, None]
    m = jnp.maximum(jnp.max(s, axis=-1, keepdims=True), sk)
    p = jnp.exp(s - m)
    p = p / (jnp.sum(p, axis=-1, keepdims=True) + jnp.exp(sk - m))
    o = jnp.einsum('bnkgqs,bnskd->bnqkgd', p.astype(v.dtype), vw)
    return o.reshape(B, T, ATT_HEADS * ATT_DIM) @ w_o


def conv_ffn(h, w_up, conv_w, conv_b, w_down):
    u = h @ w_up
    gate, val = u[..., :D_FF], u[..., D_FF:]
    gp = jnp.pad(gate, ((0, 0), (1, 1), (0, 0)))
    gate = gp[:, :-2] * conv_w[0] + gp[:, 1:-1] * conv_w[1] + gp[:, 2:] * conv_w[2] + conv_b
    return (jax.nn.silu(gate) * val) @ w_down


def trunk(x, p):
    for layer in range(DEPTH):
        j = layer // 2
        h = rmsnorm(x, p['mix_norm'][layer])
        if layer % 2 == 0:
            x = x + hgrn_rwkv_mixer(h, layer, p['ab_w_in'][j], p['hgrn_lb'], p['hgrn_onorm'][j],
                                    p['rwkv_mu'][j], p['rwkv_w0'][j], p['rwkv_w2'][j], p['rwkv_a0'][j],
                                    p['rwkv_a2'][j], p['rwkv_g2'][j], p['rwkv_kk'][j], p['rwkv_ka'][j],
                                    p['rwkv_rk'][j], p['rwkv_ln_w'][j], p['rwkv_ln_b'][j], p['ab_w_out'][j])
        else:
            x = x + window_attention_mixer(h, p['att_w_qkv'][j], p['att_sink'][j], p['att_w_o'][j])
        x = x + conv_ffn(rmsnorm(x, p['ffn_norm'][layer]), p['ffn_w_up'][layer], p['ffn_conv_w'][layer],
                         p['ffn_conv_b'][layer], p['ffn_w_down'][layer])
    return rmsnorm(x, p['final_norm'])


def setup_inputs(seed: int = 0) -> dict:
    key = jax.random.key(seed)
    ks = iter(list(jax.random.split(key, 40)))
    D = D_MODEL

    def nrm(shape, scale):
        return scale * jax.random.normal(next(ks), shape, jnp.float32)

    return {
        'x_prompt': nrm((BATCH, SEQ, D), 1.0),
        'x_sample': nrm((DEC_BATCH, DEC_SEQ, D), 1.0),
        'mix_norm': 1.0 + nrm((DEPTH, D), 0.02),
        'ab_w_in': nrm((N_AB, D, MIX_COLS), D ** -0.5),
        'hgrn_lb': nrm((DEPTH + 1, HGRN_WIDTH), 0.1),
        'hgrn_onorm': 1.0 + nrm((N_AB, HGRN_WIDTH), 0.02),
        'rwkv_mu': jax.random.uniform(next(ks), (N_AB, RWKV_COLS), jnp.float32),
        'rwkv_w0': -1.0 + nrm((N_AB, 2, RWKV_WIDTH), 0.5),
        'rwkv_w2': nrm((N_AB, 2, DECAY_LORA, RWKV_WIDTH), 0.5 * DECAY_LORA ** -0.5),
        'rwkv_a0': nrm((N_AB, RWKV_WIDTH), 0.5),
        'rwkv_a2': nrm((N_AB, AAA_LORA, RWKV_WIDTH), AAA_LORA ** -0.5),
        'rwkv_g2': nrm((N_AB, GATE_LORA, RWKV_WIDTH), GATE_LORA ** -0.5),
        'rwkv_kk': 0.85 + nrm((N_AB, RWKV_WIDTH), 0.05),
        'rwkv_ka': 1.0 + nrm((N_AB, RWKV_WIDTH), 0.05),
        'rwkv_rk': nrm((N_AB, RWKV_HEADS, RWKV_DIM), 0.1),
        'rwkv_ln_w': 1.0 + nrm((N_AB, RWKV_WIDTH), 0.02),
        'rwkv_ln_b': nrm((N_AB, RWKV_WIDTH), 0.02),
        'ab_w_out': nrm((N_AB, HGRN_WIDTH + RWKV_WIDTH, D), (HGRN_WIDTH + RWKV_WIDTH) ** -0.5),
        'att_w_qkv': nrm((N_ATT, D, QKV_COLS), D ** -0.5),
        'att_sink': nrm((N_ATT, ATT_HEADS), 1.0),
        'att_w_o': nrm((N_ATT, ATT_HEADS * ATT_DIM, D), (ATT_HEADS * ATT_DIM) ** -0.5),
        'ffn_norm': 1.0 + nrm((DEPTH, D), 0.02),
        'ffn_w_up': nrm((DEPTH, D, 2 * D_FF), D ** -0.5),
        'ffn_conv_w': nrm((DEPTH, CONV_WIDTH, D_FF), 0.3) + jnp.array([0.0, 1.0, 0.0], jnp.float32)[None, :, None],
        'ffn_conv_b': nrm((DEPTH, D_FF), 0.02),
        'ffn_w_down': nrm((DEPTH, D_FF, D), D_FF ** -0.5),
        'final_norm': 1.0 + nrm((D,), 0.02),
    }


def reference(x_prompt, x_sample, mix_norm, ab_w_in, hgrn_lb, hgrn_onorm, rwkv_mu, rwkv_w0, rwkv_w2,
              rwkv_a0, rwkv_a2, rwkv_g2, rwkv_kk, rwkv_ka, rwkv_rk, rwkv_ln_w, rwkv_ln_b, ab_w_out,
              att_w_qkv, att_sink, att_w_o, ffn_norm, ffn_w_up, ffn_conv_w, ffn_conv_b, ffn_w_down, final_norm):
    p = {
        'mix_norm': mix_norm, 'ab_w_in': ab_w_in, 'hgrn_lb': hgrn_lb, 'hgrn_onorm': hgrn_onorm,
        'rwkv_mu': rwkv_mu, 'rwkv_w0': rwkv_w0, 'rwkv_w2': rwkv_w2, 'rwkv_a0': rwkv_a0, 'rwkv_a2': rwkv_a2,
        'rwkv_g2': rwkv_g2, 'rwkv_kk': rwkv_kk, 'rwkv_ka': rwkv_ka, 'rwkv_rk': rwkv_rk,
        'rwkv_ln_w': rwkv_ln_w, 'rwkv_ln_b': rwkv_ln_b, 'ab_w_out': ab_w_out,
        'att_w_qkv': att_w_qkv, 'att_sink': att_sink, 'att_w_o': att_w_o,
        'ffn_norm': ffn_norm, 'ffn_w_up': ffn_w_up, 'ffn_conv_w': ffn_conv_w, 'ffn_conv_b': ffn_conv_b,
        'ffn_w_down': ffn_w_down, 'final_norm': final_norm,
    }
    y_prompt = trunk(x_prompt, p)
    y_sample = trunk(x_sample, p)
    return (y_prompt, y_sample)
```

```python
import types
import numpy as np
from contextlib import ExitStack
import concourse.bass as bass
import concourse.mybir as mybir
from concourse.bass_utils import run_bass_kernel_spmd

F32 = mybir.dt.float32
BF16 = mybir.dt.bfloat16
AF = mybir.ActivationFunctionType
ALU = mybir.AluOpType
AX = mybir.AxisListType

P = 128
D = 2048
KC = 16
G = 512
DFF = 5632
FC = 44
HW_ = 1024
NEG = -30000.0


def freeze(fn):
    if fn.__closure__ is None:
        return fn
    cells = []
    for c in fn.__closure__:
        try:
            cells.append(types.CellType(c.cell_contents))
        except ValueError:
            cells.append(c)
    g = types.FunctionType(fn.__code__, fn.__globals__, fn.__name__, fn.__defaults__, tuple(cells))
    g.__kwdefaults__ = fn.__kwdefaults__
    return g


class Res:
    __slots__ = ("name", "last_w", "readers")

    def __init__(self, name):
        self.name = name
        self.last_w = None
        self.readers = []


class Sched:
    ENG = ("pe", "act", "dve", "pool", "sp")
    NDMA = 12

    _pool = {}

    def __init__(self, nc, stack, tag):
        self.nc = nc
        self.tag = tag
        self.ops = {e: [] for e in self.ENG}
        pool = getattr(nc, "_sched_pool", None)
        if pool is None:
            pool = {"sem": {e: nc.alloc_semaphore(name=f"s_{e}") for e in ("pe", "act", "dve", "pool")},
                    "cnt": {e: 0 for e in ("pe", "act", "dve", "pool")},
                    "dsem": {q: [nc.alloc_semaphore(name=f"d_{q}{i}") for i in range(self.NDMA)] for q in ("sp", "pool")},
                    "dcnt": {q: [0] * self.NDMA for q in ("sp", "pool")},
                    "dnext": {"sp": 0, "pool": 0},
                    "waited": {e: {} for e in self.ENG},
                    "semobj": {}}
            nc._sched_pool = pool
        self.sem = pool["sem"]
        self.cnt = pool["cnt"]
        self.dsem = pool["dsem"]
        self.dcnt = pool["dcnt"]
        self.dnext = pool["dnext"]
        self.waited = pool["waited"]
        self.semobj = pool["semobj"]
        self.nres = 0

    def res(self, name=None):
        self.nres += 1
        return Res(name or f"r{self.nres}")

    def _need(self, eng, tok, waits):
        if tok is None:
            return
        key, val = tok
        if self.waited[eng].get(key, 0) >= val:
            return
        waits[key] = max(waits.get(key, 0), val)

    def op(self, eng, fn, reads=(), writes=(), dma=False):
        import os
        self.nop = getattr(self, "nop", 0) + 1
        if os.environ.get("MAXOPS") and self.tag == os.environ.get("MAXTAG", "p2") and self.nop > int(os.environ["MAXOPS"]):
            return None
        fn = freeze(fn)
        waits = {}
        for r in list(reads) + list(writes):
            self._need(eng, r.last_w, waits)
        for w in writes:
            for tok in w.readers:
                self._need(eng, tok, waits)
        if dma:
            q = eng
            i = self.dnext[q]
            self.dnext[q] = (i + 1) % self.NDMA
            key = ("d", q, i)
            if self.dcnt[q][i] > 0:
                self._need(eng, (key, self.dcnt[q][i]), waits)
            self.dcnt[q][i] += 16
            tok = (key, self.dcnt[q][i])
            semo = self.dsem[q][i]
            inc = 16
        else:
            self.cnt[eng] += 1
            key = ("c", eng)
            tok = (key, self.cnt[eng])
            semo = self.sem[eng]
            inc = 1
        self.semobj[key] = semo
        for k, v in waits.items():
            self.waited[eng][k] = max(self.waited[eng].get(k, 0), v)
        self.ops[eng].append(([(self.semobj[k], v) for k, v in waits.items()], fn, semo, inc))
        for w in writes:
            w.last_w = tok
            w.readers = []
        for r in reads:
            r.readers.append(tok)
        return tok

    def emit(self, block):
        def run(engobj, lst, final_waits):
            for waits, fn, semo, inc in lst:
                for s, v in waits:
                    engobj.wait_ge(s, v)
                ins = fn(engobj)
                ins.then_inc(semo, inc)
            for s, v in final_waits:
                engobj.wait_ge(s, v)

        fin = []
        for q in ("sp", "pool"):
            for i in range(self.NDMA):
                if self.dcnt[q][i] > 0:
                    fin.append((self.dsem[q][i], self.dcnt[q][i]))
        for e in ("pe", "act", "dve", "pool"):
            if self.cnt[e] > 0:
                fin.append((self.sem[e], self.cnt[e]))

        @block.tensor
        def _(e):
            run(e, self.ops["pe"], [])

        @block.scalar
        def _(e):
            run(e, self.ops["act"], [])

        @block.vector
        def _(e):
            run(e, self.ops["dve"], [])

        @block.gpsimd
        def _(e):
            run(e, self.ops["pool"], [])

        @block.sync
        def _(e):
            run(e, self.ops["sp"], fin)


def acopy(e, out, in_):
    return e.activation(out=out, in_=in_, func=AF.Identity)


def rev(ap):
    apl = [list(d) for d in ap.ap]
    st, n = apl[-1]
    apl[-1] = [-st, n]
    return bass.AP(tensor=ap.tensor, offset=ap.offset + st * (n - 1), ap=apl)


def bcast_last(ap, n):
    apl = [list(d) for d in ap.ap]
    assert apl[-1][1] == 1
    apl[-1] = [0, n]
    return bass.AP(tensor=ap.tensor, offset=ap.offset, ap=apl)


class Phase:
    def __init__(self, nc, tag):
        self.nc = nc
        self.tag = tag
        self.stack = ExitStack()
        self.S = Sched(nc, self.stack, tag)
        self.n = 0

    def sb(self, shape, dt, name=None):
        self.n += 1
        return self.stack.enter_context(self.nc.sbuf_tensor(f"{self.tag}_{name or 'sb'}{self.n}", list(shape), dt))

    def ps(self, shape, dt, name=None):
        self.n += 1
        return self.stack.enter_context(self.nc.psum_tensor(f"{self.tag}_{name or 'ps'}{self.n}", list(shape), dt))

    def res(self, name=None):
        return self.S.res(name)

    def close(self):
        with self.nc.Block() as block:
            self.S.emit(block)
        self.stack.close()


class RR:
    def __init__(self, ph, n, shape, dt, name, psum=False):
        self.items = [((ph.ps if psum else ph.sb)(shape, dt, name), ph.res(name)) for _ in range(n)]
        self.i = 0

    def next(self):
        it = self.items[self.i]
        self.i = (self.i + 1) % len(self.items)
        return it


def dma(ph, q, out, in_, reads=(), writes=()):
    return ph.S.op(q, lambda e: e.dma_start(out=out, in_=in_), reads, writes, dma=True)


def col2(ap3, kc, a, b):
    v = ap3[:, kc, a:a + 1]
    apl = [list(d) for d in v.ap]
    apl[-1] = [(b - a) * apl[-1][0] if apl[-1][0] != 0 else (b - a), 2]
    return bass.AP(tensor=v.tensor, offset=v.offset, ap=apl)


class NormCtx:
    def __init__(self, ph, ident, rident, with_T=True):
        self.ph = ph
        self.junk = ph.sb([P, D], BF16, "junk")
        self.rjunk = ph.res()
        self.st = RR(ph, 2, [P, 4], F32, "nst")
        self.xn = RR(ph, 2, [P, D], BF16, "xn")
        self.ident = ident
        self.rident = rident
        if with_T:
            self.pT = RR(ph, 1, [P, D], BF16, "pT", psum=True)


def rms_stats(ph, nx, xt, rx):
    S = ph.S
    st, rst = nx.st.next()
    npart = xt.shape[0]
    S.op("act", lambda e: e.activation(out=nx.junk[0:npart, :], in_=xt, func=AF.Square, accum_out=st[0:npart, 0:1]),
         reads=[rx], writes=[nx.rjunk, rst])
    S.op("dve", lambda e: e.tensor_scalar(out=st[:, 1:2], in0=st[:, 0:1], scalar1=1.0 / D, scalar2=1e-6,
                                           op0=ALU.mult, op1=ALU.add), reads=[rst], writes=[rst])
    S.op("act", lambda e: e.activation(out=st[:, 2:3], in_=st[:, 1:2], func=AF.Sqrt), reads=[rst], writes=[rst])
    S.op("dve", lambda e: e.reciprocal(out=st[:, 3:4], in_=st[:, 2:3]), reads=[rst], writes=[rst])
    return st, rst


def norm_T(ph, nx, xt, rx, gbc, rg, hT, rhT, c0, ncol=P, npart=P):
    S = ph.S
    st, rst = rms_stats(ph, nx, xt, rx)
    xn, rxn = nx.xn.next()
    S.op("dve", lambda e: e.scalar_tensor_tensor(out=xn[0:npart, :], in0=xt, scalar=st[0:npart, 3:4], in1=gbc[0:npart, :],
                                                  op0=ALU.mult, op1=ALU.mult), reads=[rx, rst, rg], writes=[rxn])
    pT, rpT = nx.pT.next()

    def tr(e):
        ins = None
        for kc in range(KC):
            ins = e.transpose(pT[:, kc * P:kc * P + npart], xn[0:npart, kc * P:(kc + 1) * P], nx.ident[0:npart, 0:npart])
        return ins
    S.op("pe", tr, reads=[rxn, nx.rident], writes=[rpT])
    src = pT[:, :].rearrange("p (k c) -> p k c", k=KC)[:, :, 0:npart]
    S.op("act", lambda e: acopy(e, out=hT[:, :, c0:c0 + npart], in_=src), reads=[rpT], writes=[rhT])


def load_bcast(ph, vec_ap, n, name):
    t = ph.sb([P, n], F32, name)
    r = ph.res(name)
    dma(ph, "sp", t[:], vec_ap[0, :].partition_broadcast(P), writes=[r])
    return t, r


def load_plain(ph, ap, shape, dt, name, q="sp"):
    t = ph.sb(shape, dt, name)
    r = ph.res(name)
    dma(ph, q, t[:], ap, writes=[r])
    return t, r


class WStream:
    def __init__(self, ph, nbuf, shape, name):
        self.ph = ph
        self.rr = RR(ph, nbuf, shape, BF16, name)

    def load(self, src_ap, dst_slice=None):
        t, r = self.rr.next()
        dst = t[:] if dst_slice is None else dst_slice(t)
        dma(self.ph, "sp", dst, src_ap, writes=[r])
        return t, r


def ffn_phase(nc, tag, groups, xin_d, xout_d, hT_d, wup_d, wdn_d, convc_d):
    ph = Phase(nc, tag)
    S = ph.S
    cw, rcw = load_plain(ph, convc_d[:, :], [P, 4 * FC], F32, "cw")
    hxs = RR(ph, 2, [P, KC, G + 2], BF16, "hx")
    xss = RR(ph, 2, [P, 4, 256], F32, "xs")
    wus = WStream(ph, 3, [P, 2, KC, P], "wu")
    wds = WStream(ph, 2, [P, FC, 256], "wd")
    act = ph.sb([P, FC, G], BF16, "act")
    ract = [ph.res() for _ in range(FC)]
    pgs = RR(ph, 2, [P, G], F32, "pg", psum=True)
    pvs = RR(ph, 2, [P, G], F32, "pv", psum=True)
    phl = ph.ps([P, G], F32, "phalo")
    _r = ph.res()
    rphl = [_r, _r]
    pds = [ph.ps([P, G], F32, "pd"), ph.ps([P, G], F32, "pd")]
    rpd = [ph.res(), ph.res()]
    tmps = RR(ph, 2, [P, G], F32, "tmp")
    sls = RR(ph, 2, [P, G], F32, "sl")
    hT_v = hT_d.rearrange("k p t -> p k t")
    ntok = hT_d.shape[2]
    wup_v = wup_d.rearrange("o p k c -> p o k c")
    nh = 0
    nd = 0
    for g in groups:
        g0 = g * G
        hx, rhx = hxs.next()
        lo = max(g0 - 1, 0)
        hi = min(g0 + G + 1, ntok)
        dma(ph, "sp", hx[:, :, lo - (g0 - 1):hi - (g0 - 1)], hT_v[:, :, lo:hi], writes=[rhx])
        if lo > g0 - 1:
            S.op("pool", lambda e, hx=hx: e.memset(hx[:, :, 0:1], 0.0), writes=[rhx])
        if hi < g0 + G + 1:
            S.op("pool", lambda e, hx=hx: e.memset(hx[:, :, G + 1:G + 2], 0.0), writes=[rhx])
        for j in range(FC):
            wt, rwt = wus.load(wup_v[:, 2 * j:2 * j + 2, :, :])
            pg, rpg = pgs.next()
            pv, rpv = pvs.next()
            hs = nh % 2
            nh += 1

            def mm_gate(e, wt=wt, pg=pg, hx=hx, hs=hs):
                ins = None
                for kc in range(KC):
                    e.matmul(pg[:, :], lhsT=wt[:, 0, kc, :], rhs=hx[:, kc, 1:G + 1], start=(kc == 0), stop=(kc == KC - 1))
                for kc in range(KC):
                    ins = e.matmul(phl[:, hs * 2:hs * 2 + 2], lhsT=wt[:, 0, kc, :], rhs=col2(hx, kc, 0, G + 1),
                                   start=(kc == 0), stop=(kc == KC - 1))
                return ins
            S.op("pe", mm_gate, reads=[rwt, rhx], writes=[rpg, rphl[hs]])

            def mm_val(e, wt=wt, pv=pv, hx=hx):
                ins = None
                for kc in range(KC):
                    ins = e.matmul(pv[:, :], lhsT=wt[:, 1, kc, :], rhs=hx[:, kc, 1:G + 1], start=(kc == 0), stop=(kc == KC - 1))
                return ins
            S.op("pe", mm_val, reads=[rwt, rhx], writes=[rpv])
            tmp, rtmp = tmps.next()
            sl, rsl = sls.next()
            c0 = cw[:, 0 * FC + j:0 * FC + j + 1]
            c1 = cw[:, 1 * FC + j:1 * FC + j + 1]
            c2 = cw[:, 2 * FC + j:2 * FC + j + 1]
            cb = cw[:, 3 * FC + j:3 * FC + j + 1]
            S.op("act", lambda e, tmp=tmp, pg=pg, c1=c1, cb=cb: e.activation(out=tmp[:, :], in_=pg[:, :], func=AF.Identity, scale=c1, bias=cb),
                 reads=[rpg, rcw], writes=[rtmp])
            S.op("dve", lambda e, tmp=tmp, pg=pg, c0=c0: e.scalar_tensor_tensor(out=tmp[:, 1:G], in0=pg[:, 0:G - 1], scalar=c0, in1=tmp[:, 1:G],
                                                                                  op0=ALU.mult, op1=ALU.add), reads=[rpg, rcw, rtmp], writes=[rtmp])
            S.op("dve", lambda e, tmp=tmp, pg=pg, c2=c2: e.scalar_tensor_tensor(out=tmp[:, 0:G - 1], in0=pg[:, 1:G], scalar=c2, in1=tmp[:, 0:G - 1],
                                                                                  op0=ALU.mult, op1=ALU.add), reads=[rpg, rcw, rtmp], writes=[rtmp])
            S.op("dve", lambda e, tmp=tmp, c0=c0, hs=hs: e.scalar_tensor_tensor(out=tmp[:, 0:1], in0=phl[:, hs * 2:hs * 2 + 1], scalar=c0, in1=tmp[:, 0:1],
                                                                                  op0=ALU.mult, op1=ALU.add), reads=[rphl[hs], rcw, rtmp], writes=[rtmp])
            S.op("dve", lambda e, tmp=tmp, c2=c2, hs=hs: e.scalar_tensor_tensor(out=tmp[:, G - 1:G], in0=phl[:, hs * 2 + 1:hs * 2 + 2], scalar=c2, in1=tmp[:, G - 1:G],
                                                                                  op0=ALU.mult, op1=ALU.add), reads=[rphl[hs], rcw, rtmp], writes=[rtmp])
            S.op("act", lambda e, tmp=tmp, sl=sl: e.activation(out=sl[:, :], in_=tmp[:, :], func=AF.Silu), reads=[rtmp], writes=[rsl])
            S.op("dve", lambda e, sl=sl, pv=pv, j=j: e.tensor_tensor(out=act[:, j, :], in0=sl[:, :], in1=pv[:, :], op=ALU.mult),
                 reads=[rsl, rpv], writes=[ract[j]])
        for cbk in range(D // 256):
            wd, rwd = wds.load(wdn_d[cbk])
            xs, rxs = xss.next()
            cs = slice(cbk * 256, (cbk + 1) * 256)
            dma(ph, "sp", xs[:], xin_d[g0:g0 + G, cs].rearrange("(t p) d -> p t d", p=P), writes=[rxs])
            for t in range(4):
                hb = nd % 2
                nd += 1

                def mm_dn(e, wd=wd, t=t, hb=hb):
                    ins = None
                    for kc in range(FC):
                        ins = e.matmul(pds[hb][:, 0:256], lhsT=act[:, kc, t * P:(t + 1) * P], rhs=wd[:, kc, :], start=(kc == 0), stop=(kc == FC - 1))
                    return ins
                S.op("pe", mm_dn, reads=[rwd] + ract, writes=[rpd[hb]])
                S.op("dve", lambda e, xs=xs, t=t, hb=hb: e.tensor_tensor(out=xs[:, t, :], in0=pds[hb][:, 0:256], in1=xs[:, t, :], op=ALU.add),
                     reads=[rpd[hb], rxs], writes=[rxs])
            dma(ph, "sp", xout_d[g0:g0 + G, cs].rearrange("(t p) d -> p t d", p=P), xs[:], reads=[rxs])
    ph.close()


def normT_phase(nc, tag, groups, xin_d, g_d, hT_d, ident_d):
    ph = Phase(nc, tag)
    ident, rident = load_plain(ph, ident_d[:, :], [P, P], BF16, "ident", q="pool")
    gbc, rg = load_bcast(ph, g_d, D, "gbc")
    nx = NormCtx(ph, ident, rident)
    xts = RR(ph, 2, [P, 4, D], F32, "xt")
    hts = RR(ph, 2, [P, KC, G], BF16, "hT")
    hT_v = hT_d.rearrange("k p t -> p k t")
    for g in groups:
        g0 = g * G
        xt, rxt = xts.next()
        dma(ph, "sp", xt[:], xin_d[g0:g0 + G, :].rearrange("(t p) d -> p t d", p=P), writes=[rxt])
        hT, rhT = hts.next()
        for t in range(4):
            norm_T(ph, nx, xt[:, t, :], rxt, gbc, rg, hT, rhT, t * P)
        dma(ph, "sp", hT_v[:, :, g0:g0 + G], hT[:], reads=[rhT])
    ph.close()


def cast_phase(nc, tag, pairs):
    ph = Phase(nc, tag)
    for dst, src in pairs:
        rows, cols = src.shape
        step = max(1, (1 << 20) // cols)
        for r0 in range(0, rows, step):
            r1 = min(rows, r0 + step)
            dma(ph, "pool", dst[r0:r1, :], src[r0:r1, :])
    ph.close()


def lay_ws(w):
    K_, N = w.shape
    return np.ascontiguousarray(w.reshape(K_ // P, P, N // P, P).transpose(2, 1, 0, 3))


def lay_as(w, cb):
    K_, N = w.shape
    return np.ascontiguousarray(w.reshape(K_ // P, P, N // cb, cb).transpose(2, 1, 0, 3))


def lay_up(w_up):
    ws = lay_ws(w_up)
    o = np.empty_like(ws)
    o[0::2] = ws[:FC]
    o[1::2] = ws[FC:]
    return o.reshape(2 * FC * P, KC * P)


def lay_cols(v):
    return np.ascontiguousarray(v.reshape(-1, P).T)


def lay_conv(cw, cb):
    return np.ascontiguousarray(np.concatenate([lay_cols(cw[0]), lay_cols(cw[1]), lay_cols(cw[2]), lay_cols(cb)], axis=1))


def qkv_phase(nc, tag, groups, xin_d, g_d, ident_d, wqk_d, wv_d, cos_d, sin_d, qT_d, kT_d, v_d):
    ph = Phase(nc, tag)
    S = ph.S
    ident, rident = load_plain(ph, ident_d[:, :], [P, P], BF16, "ident", q="pool")
    gbc, rg = load_bcast(ph, g_d, D, "gbc")
    wv, rwv = load_plain(ph, wv_d, [P, KC, 512], BF16, "wv")
    nx = NormCtx(ph, ident, rident)
    xts = RR(ph, 1, [P, 4, D], F32, "xt")
    hts = RR(ph, 1, [P, KC, G], BF16, "hT")
    css = RR(ph, 2, [P, 2, G], F32, "cs")
    wqs = WStream(ph, 3, [P, 2, KC, P], "wq")
    pas = RR(ph, 2, [P, G], F32, "pa", psum=True)
    pbs = RR(ph, 2, [P, G], F32, "pb", psum=True)
    t1s = RR(ph, 2, [P, G], F32, "t1")
    t2s = RR(ph, 2, [P, G], F32, "t2")
    qos = RR(ph, 1, [P, 20, G], BF16, "qo")
    vos = RR(ph, 1, [P, 4, 512], BF16, "vo")
    zt = ph.sb([P, 4 * 512], BF16, "zt")
    rz = ph.res()
    S.op("pool", lambda e: e.memset(zt[:], 0.0), writes=[rz])
    Lk = kT_d.shape[2]
    kT_v = kT_d.rearrange("h p t -> p h t")
    qT_v = qT_d.rearrange("h p t -> p h t")
    wqk_v = wqk_d.rearrange("o p k c -> p o k c")
    ztk = zt[:, 0:4 * P].rearrange("p (h t) -> p h t", h=4)
    dma(ph, "sp", kT_v[:, :, 0:P], ztk, reads=[rz])
    dma(ph, "sp", kT_v[:, :, Lk - P:Lk], ztk, reads=[rz])
    dma(ph, "sp", v_d[0:P, :], zt[:, 0:512], reads=[rz])
    dma(ph, "sp", v_d[Lk - P:Lk, :], zt[:, 0:512], reads=[rz])
    for g in groups:
        g0 = g * G
        xt, rxt = xts.next()
        dma(ph, "sp", xt[:], xin_d[g0:g0 + G, :].rearrange("(t p) d -> p t d", p=P), writes=[rxt])
        cs, rcs = css.next()
        dma(ph, "sp", cs[:, 0, :], cos_d[:, g0:g0 + G], writes=[rcs])
        dma(ph, "sp", cs[:, 1, :], sin_d[:, g0:g0 + G], writes=[rcs])
        hT, rhT = hts.next()
        for t in range(4):
            norm_T(ph, nx, xt[:, t, :], rxt, gbc, rg, hT, rhT, t * P)
        qo, rqo = qos.next()
        for hh in range(20):
            wt, rwt = wqs.load(wqk_v[:, 2 * hh:2 * hh + 2, :, :])
            pa, rpa = pas.next()
            pb, rpb = pbs.next()

            def mm(e, wt=wt, pa=pa, pb=pb, hT=hT):
                ins = None
                for kc in range(KC):
                    e.matmul(pa[:, :], lhsT=wt[:, 0, kc, :], rhs=hT[:, kc, :], start=(kc == 0), stop=(kc == KC - 1))
                for kc in range(KC):
                    ins = e.matmul(pb[:, :], lhsT=wt[:, 1, kc, :], rhs=hT[:, kc, :], start=(kc == 0), stop=(kc == KC - 1))
                return ins
            S.op("pe", mm, reads=[rwt, rhT], writes=[rpa, rpb])
            t1, rt1 = t1s.next()
            t2, rt2 = t2s.next()
            S.op("dve", lambda e, t1=t1, pa=pa, cs=cs: e.tensor_tensor(out=t1[:, :], in0=pa[:, :], in1=cs[:, 0, :], op=ALU.mult),
                 reads=[rpa, rcs], writes=[rt1])
            S.op("dve", lambda e, t2=t2, pb=pb, cs=cs: e.tensor_tensor(out=t2[:, :], in0=pb[:, :], in1=cs[:, 1, :], op=ALU.mult),
                 reads=[rpb, rcs], writes=[rt2])
            S.op("pool", lambda e, t1=t1, t2=t2, qo=qo, hh=hh: e.tensor_tensor(out=qo[:, hh, :], in0=t1[:, :], in1=t2[:, :], op=ALU.add),
                 reads=[rt1, rt2], writes=[rqo])
        dma(ph, "sp", qT_v[:, :, g0:g0 + G], qo[:, 0:16, :], reads=[rqo])
        dma(ph, "sp", kT_v[:, :, P + g0:P + g0 + G], qo[:, 16:20, :], reads=[rqo])
        vo, rvo = vos.next()
        for t in range(4):
            pa, rpa = pas.next()

            def mmv(e, pa=pa, hT=hT, t=t):
                ins = None
                for kc in range(KC):
                    ins = e.matmul(pa[:, :], lhsT=hT[:, kc, t * P:(t + 1) * P], rhs=wv[:, kc, :], start=(kc == 0), stop=(kc == KC - 1))
                return ins
            S.op("pe", mmv, reads=[rwv, rhT], writes=[rpa])
            S.op("act", lambda e, vo=vo, pa=pa, t=t: acopy(e, out=vo[:, t, :], in_=pa[:, :]), reads=[rpa], writes=[rvo])
        dma(ph, "sp", v_d[P + g0:P + g0 + G, :].rearrange("(t p) c -> p t c", p=P), vo[:], reads=[rvo])
    ph.close()


def attn_phase(nc, tag, nblk, xin_d, xout_d, qT_d, kT_d, v_d, wo_d, band_d, kbias_d, sink_d, maskcol_d, ident_d):
    ph = Phase(nc, tag)
    S = ph.S
    SC = 128.0 ** -0.5
    ident, rident = load_plain(ph, ident_d[:, :], [P, P], BF16, "ident", q="pool")
    wo, rwo = load_plain(ph, wo_d.rearrange("n p k c -> p n k c"), [P, 4, KC, 512], BF16, "wo")
    band, rband = load_plain(ph, band_d[:, :], [P, 384], F32, "band")
    Lk = kbias_d.shape[1]
    kb, rkb = load_bcast(ph, kbias_d, Lk, "kbias")
    sk, rsk = load_bcast(ph, sink_d, 16, "sink")
    mc, rmc = load_plain(ph, maskcol_d[:, :], [P, maskcol_d.shape[1]], F32, "maskcol")
    qT_v = qT_d.rearrange("h p t -> p h t")
    kT_v = kT_d.rearrange("h p t -> p h t")
    qs = RR(ph, 2, [P, 16, P], BF16, "q")
    ks = RR(ph, 2, [P, 4, 384], BF16, "k")
    vs = RR(ph, 2, [P, 3, 512], BF16, "v")
    xts = RR(ph, 2, [P, D], F32, "xt")
    bns = RR(ph, 2, [P, 384], F32, "bn")
    pss = RR(ph, 2, [P, G], F32, "psc", psum=True)
    pts = RR(ph, 1, [P, 3, P], BF16, "ppt", psum=True)
    pos_ = RR(ph, 2, [P, G], F32, "po", psum=True)
    pws = RR(ph, 2, [P, G], F32, "pw", psum=True)
    ss = RR(ph, 2, [P, 384], F32, "s")
    pps = RR(ph, 2, [P, 384], F32, "p")
    pns = RR(ph, 2, [P, 384], BF16, "pn")
    pTs = RR(ph, 2, [P, 3, P], BF16, "pT")
    sts = RR(ph, 4, [P, 8], F32, "st")
    oTs = RR(ph, 2, [P, 16, P], BF16, "oT")
    for n in range(nblk):
        n0 = n * P
        q, rq = qs.next()
        dma(ph, "sp", q[:], qT_v[:, :, n0:n0 + P], writes=[rq])
        k, rk = ks.next()
        dma(ph, "sp", k[:], kT_v[:, :, n0:n0 + 384], writes=[rk])
        v, rv = vs.next()
        dma(ph, "sp", v[:], v_d[n0:n0 + 384, :].rearrange("(j p) c -> p j c", p=P), writes=[rv])
        xt, rxt = xts.next()
        dma(ph, "sp", xt[:], xin_d[n0:n0 + P, :], writes=[rxt])
        bn, rbn = bns.next()
        S.op("pool", lambda e, bn=bn, n0=n0: e.tensor_tensor(out=bn[:, :], in0=band[:, :], in1=kb[:, n0:n0 + 384], op=ALU.add),
             reads=[rband, rkb], writes=[rbn])
        oT, roT = oTs.next()
        for h in range(16):
            kv = h // 4
            psc, rpsc = pss.next()
            S.op("pe", lambda e, psc=psc, q=q, k=k, h=h, kv=kv: e.matmul(psc[:, 0:384], lhsT=q[:, h, :], rhs=k[:, kv, :], start=True, stop=True),
                 reads=[rq, rk], writes=[rpsc])
            s, rs = ss.next()
            st, rst = sts.next()
            S.op("dve", lambda e, s=s, psc=psc, bn=bn: e.scalar_tensor_tensor(out=s[:, :], in0=psc[:, 0:384], scalar=SC, in1=bn[:, :],
                                                                                op0=ALU.mult, op1=ALU.add), reads=[rpsc, rbn], writes=[rs])
            S.op("dve", lambda e, s=s, st=st: e.reduce_max(out=st[:, 0:1], in_=s[:, :], axis=AX.X), reads=[rs], writes=[rst])
            S.op("dve", lambda e, st=st, h=h: e.tensor_scalar(out=st[:, 1:2], in0=st[:, 0:1], scalar1=sk[:, h:h + 1], scalar2=-1.0,
                                                               op0=ALU.max, op1=ALU.mult), reads=[rst, rsk], writes=[rst])
            pp, rpp = pps.next()
            S.op("act", lambda e, pp=pp, s=s, st=st: e.activation(out=pp[:, :], in_=s[:, :], func=AF.Exp, bias=st[:, 1:2], accum_out=st[:, 2:3]),
                 reads=[rs, rst], writes=[rpp, rst])
            S.op("act", lambda e, st=st, h=h: e.activation(out=st[:, 3:4], in_=sk[:, h:h + 1], func=AF.Exp, bias=st[:, 1:2]),
                 reads=[rsk, rst], writes=[rst])
            S.op("dve", lambda e, st=st: e.tensor_tensor(out=st[:, 4:5], in0=st[:, 2:3], in1=st[:, 3:4], op=ALU.add), reads=[rst], writes=[rst])
            S.op("dve", lambda e, st=st: e.reciprocal(out=st[:, 5:6], in_=st[:, 4:5]), reads=[rst], writes=[rst])
            pn, rpn = pns.next()
            S.op("act", lambda e, pn=pn, pp=pp, st=st: e.activation(out=pn[:, :], in_=pp[:, :], func=AF.Identity, scale=st[:, 5:6]),
                 reads=[rpp, rst], writes=[rpn])
            ppt, rppt = pts.next()

            def trs(e, ppt=ppt, pn=pn):
                ins = None
                for j in range(3):
                    ins = e.transpose(ppt[:, j, :], pn[:, j * P:(j + 1) * P], ident[:, :])
                return ins
            S.op("pe", trs, reads=[rpn, rident], writes=[rppt])
            pT, rpT = pTs.next()
            S.op("dve", lambda e, pT=pT, ppt=ppt: e.tensor_copy(out=pT[:], in_=ppt[:]), reads=[rppt], writes=[rpT])
            po, rpo = pos_.next()

            def pv(e, po=po, v=v, pT=pT, kv=kv):
                ins = None
                for j in range(3):
                    ins = e.matmul(po[:, 0:P], lhsT=v[:, j, kv * P:(kv + 1) * P], rhs=pT[:, j, :], start=(j == 0), stop=(j == 2))
                return ins
            S.op("pe", pv, reads=[rv, rpT], writes=[rpo])
            S.op("act", lambda e, oT=oT, po=po, h=h: acopy(e, out=oT[:, h, :], in_=po[:, 0:P]), reads=[rpo], writes=[roT])
        for cbk in range(4):
            pw, rpw = pws.next()

            def mmo(e, pw=pw, oT=oT, cbk=cbk):
                ins = None
                for h in range(16):
                    ins = e.matmul(pw[:, :], lhsT=oT[:, h, :], rhs=wo[:, cbk, h, :], start=(h == 0), stop=(h == 15))
                return ins
            S.op("pe", mmo, reads=[roT, rwo], writes=[rpw])
            S.op("dve", lambda e, xt=xt, pw=pw, cbk=cbk: e.tensor_tensor(out=xt[:, cbk * 512:(cbk + 1) * 512], in0=pw[:, :],
                                                                          in1=xt[:, cbk * 512:(cbk + 1) * 512], op=ALU.add),
                 reads=[rpw, rxt], writes=[rxt])
        S.op("act", lambda e, xt=xt, n=n: e.activation(out=xt[:, :], in_=xt[:, :], func=AF.Identity, scale=mc[:, n:n + 1]),
             reads=[rxt, rmc], writes=[rxt])
        dma(ph, "sp", xout_d[n0:n0 + P, :], xt[:], reads=[rxt])
    ph.close()


def lay_qk(w_qkv):
    tiles = []
    for hh in range(20):
        blk = w_qkv[:, hh * P:(hh + 1) * P]
        perm = np.concatenate([blk[:, 64:], blk[:, :64]], axis=1)
        tiles.append(lay_ws(blk)[0])
        tiles.append(lay_ws(perm)[0])
    wqk = np.ascontiguousarray(np.stack(tiles))
    wv = lay_as(w_qkv[:, 20 * P:], 512)[0]
    return wqk.reshape(40 * P, KC * P), np.ascontiguousarray(wv.reshape(P, KC * 512))


def rope_tables(pos):
    inv = (np.float32(10000.0) ** (-(np.arange(64, dtype=np.float32) / np.float32(64)))).astype(np.float32)
    ang = (pos.astype(np.float32)[:, None] * inv[None, :]).astype(np.float32).astype(np.float64)
    c = np.cos(ang).T.astype(np.float32)
    s_ = np.sin(ang).T.astype(np.float32)
    return np.ascontiguousarray(np.concatenate([c, c], 0)), np.ascontiguousarray(np.concatenate([-s_, s_], 0))


def band_mask():
    q = np.arange(P)[:, None]
    s_ = np.arange(384)[None, :] - P
    return np.where(np.abs(s_ - q) <= 128, 0.0, NEG).astype(np.float32)


K0 = float(np.exp(-0.5))
NCH = 68


def mix_phase(nc, tag, groups, bwd, full, final, ntot, dr):
    ph = Phase(nc, tag)
    S = ph.S
    dirn = 1 if bwd else 0

    def src(ap):
        return rev(ap) if bwd else ap

    def E(eng, fn, reads=(), writes=()):
        return S.op(eng, fn, reads, writes)

    dbg = dr.get("dbg") or {}
    tapped = set()

    def tap(name, ap, r):
        if name in dbg and name not in tapped:
            tapped.add(name)
            dma(ph, "sp", dbg[name], ap, reads=[r])

    ident, rident = load_plain(ph, dr["ident"][:, :], [P, P], BF16, "ident", q="pool")
    cst, rcst = load_plain(ph, dr["mixc"][:, :], [P, dr["mixc"].shape[1]], F32, "mixc")
    C_LBT, C_ONORM, C_MU, C_W0, C_A0, C_KK, C_KA, C_RK = 0, 24, 32, 60, 76, 84, 92, 100
    rmask, rrm = load_plain(ph, dr["rmask"][:, :], [P, G], F32, "rmask")
    trim, rtrim = load_plain(ph, dr["trimask"][:, :], [P, P], F32, "trimask")
    mskM, rmskM = load_plain(ph, dr["maskM"][:, :], [64, 4 * P], F32, "maskM")
    mskX, rmskX = load_plain(ph, dr["maskX"][:, :], [64, 8 * 64], F32, "maskX")
    ones_b, rones = load_plain(ph, dr["ones"][:, :], [P, P], BF16, "ones", q="pool")
    blk64, rblk64 = load_plain(ph, dr["blk64"][:, :], [P, P], BF16, "blk64", q="pool")
    w2sb, rw2 = load_plain(ph, dr["w2"][:, :], [P, 1024], BF16, "w2", q="pool")
    a2sb, ra2 = load_plain(ph, dr["a2"][:, :], [64, 1024], BF16, "a2", q="pool")
    if final:
        g2a, rg2a = load_plain(ph, dr["g2"][0:P, :], [P, 1024], BF16, "g2a", q="pool")
        g2b, rg2b = load_plain(ph, dr["g2"][P:160, :], [32, 1024], BF16, "g2b", q="pool")
        lnc, rlnc = load_plain(ph, dr["lnc"][:, :], [64, 32], F32, "lnc")
        ones64, rones64 = load_plain(ph, dr["ones64"][:, :], [64, 64], BF16, "ones64", q="pool")
    dc = ph.sb([P, 128], F32, "dconst")
    rdc = ph.res()
    lbt = cst[:, C_LBT:C_LBT + 24].rearrange("p (r h) -> p r h", r=3)
    D_LB, D_OML, D_NOML, D_OMM, D_HM, D_T = 0, 8, 16, 24, 52, 80
    E("dve", lambda e: e.tensor_tensor(out=dc[:, D_T:D_T + 8], in0=lbt[:, 0, :], in1=lbt[:, 1, :], op=ALU.max), [rcst], [rdc])
    E("dve", lambda e: e.tensor_tensor(out=dc[:, D_T:D_T + 8], in0=dc[:, D_T:D_T + 8], in1=lbt[:, 2, :], op=ALU.max), [rcst, rdc], [rdc])
    for r_ in range(3):
        E("dve", lambda e, r_=r_: e.tensor_tensor(out=dc[:, D_T + 8 + 8 * r_:D_T + 16 + 8 * r_], in0=lbt[:, r_, :], in1=dc[:, D_T:D_T + 8], op=ALU.subtract),
          [rcst, rdc], [rdc])
    E("act", lambda e: e.activation(out=dc[:, D_T + 8:D_T + 32], in_=dc[:, D_T + 8:D_T + 32], func=AF.Exp), [rdc], [rdc])
    E("dve", lambda e: e.tensor_tensor(out=dc[:, D_T:D_T + 8], in0=dc[:, D_T + 8:D_T + 16], in1=dc[:, D_T + 16:D_T + 24], op=ALU.add), [rdc], [rdc])
    E("dve", lambda e: e.tensor_tensor(out=dc[:, D_T:D_T + 8], in0=dc[:, D_T:D_T + 8], in1=dc[:, D_T + 24:D_T + 32], op=ALU.add), [rdc], [rdc])
    E("dve", lambda e: e.reciprocal(out=dc[:, D_T:D_T + 8], in_=dc[:, D_T:D_T + 8]), [rdc], [rdc])
    E("dve", lambda e: e.tensor_tensor(out=dc[:, D_LB:D_LB + 8], in0=dc[:, D_T + 8:D_T + 16], in1=dc[:, D_T:D_T + 8], op=ALU.mult), [rdc], [rdc])
    E("dve", lambda e: e.tensor_scalar(out=dc[:, D_OML:D_OML + 8], in0=dc[:, D_LB:D_LB + 8], scalar1=-1.0, scalar2=1.0, op0=ALU.mult, op1=ALU.add), [rdc], [rdc])
    E("dve", lambda e: e.tensor_scalar(out=dc[:, D_NOML:D_NOML + 8], in0=dc[:, D_OML:D_OML + 8], scalar1=-1.0, scalar2=None, op0=ALU.mult), [rdc], [rdc])
    E("dve", lambda e: e.tensor_scalar(out=dc[:, D_OMM:D_OMM + 28], in0=cst[:, C_MU:C_MU + 28], scalar1=-1.0, scalar2=1.0, op0=ALU.mult, op1=ALU.add), [rcst], [rdc])
    E("dve", lambda e: e.tensor_scalar(out=dc[:, D_HM:D_HM + 28], in0=cst[:, C_MU:C_MU + 28], scalar1=0.5, scalar2=None, op0=ALU.mult), [rcst], [rdc])

    hTs = RR(ph, 2, [P, KC, G + 2], BF16, "hT")
    hT_v = dr["hT0"].rearrange("k p t -> p k t")
    mkg = ph.sb([P, G], F32, "maskg")
    rmkg = ph.res()
    FB = RR(ph, 4, [P, G], F32, "fb", psum=True)
    PY = [(ph.ps([P, G], F32, "py"), ph.res()) for _ in range(2)]
    TB = RR(ph, 2, [P, 8 * P], BF16, "tb", psum=True)
    wis = WStream(ph, 3, [P, KC, P], "wi")
    win_v = dr["w_in"].rearrange("o p k c -> p o k c")
    tf = RR(ph, 15, [P, G], F32, "tf")
    tb_ = RR(ph, 20, [P, G], BF16, "tbf")
    th = ph.sb([P, G], BF16, "th")
    rth = ph.res()
    adb = ph.sb([64, G], BF16, "adb")
    radb = ph.res()
    if final:
        sgd = ph.sb([P, 2, G], BF16, "sgd")
        rsgd = ph.res()
    S32 = ph.sb([P, 8, P], F32, "S32")
    rS32 = [ph.res() for _ in range(8)]
    Sbf = [ph.sb([P, 8, P], BF16, "Sbf") for _ in range(2)]
    rSbf = [[ph.res() for _ in range(8)] for _ in range(2)]
    sbp = [0] * 8
    H32 = ph.sb([64, 16, 64], F32, "H32")
    rH32 = [ph.res() for _ in range(16)]
    Hbf = [ph.sb([64, 16, 64], BF16, "Hbf") for _ in range(2)]
    rHbf = [[ph.res() for _ in range(16)] for _ in range(2)]
    hbp = [0] * 16
    if dr.get("st_in") is not None:
        dma(ph, "sp", S32[:].rearrange("p h e -> p (h e)"), dr["st_in"][0][:, :], writes=rS32)
        dma(ph, "sp", H32[:].rearrange("p h e -> p (h e)"), dr["st_in"][1][:, :], writes=rH32)
    else:
        E("pool", lambda e: e.memset(S32[:], 0.0), [], rS32)
        E("pool", lambda e: e.memset(H32[:], 0.0), [], rH32)
    E("act", lambda e: acopy(e, out=Sbf[0][:], in_=S32[:]), rS32, rSbf[0])
    E("act", lambda e: acopy(e, out=Hbf[0][:], in_=H32[:]), rH32, rHbf[0])
    if full:
        yst = RR(ph, 2, [P, G], F32, "yst")
    if full:
        ysrs = RR(ph, 1, [64, 2, G], F32, "ysr")
    if final:
        yaTs = RR(ph, 2, [P, G], BF16, "yaT")
        ybTs = RR(ph, 2, [64, 2, G], BF16, "ybT")
        parts = RR(ph, 2, [P, G], F32, "part")
        bon1 = ph.sb([64, G], BF16, "bon1")
        rbon1 = ph.res()
    AR = ph.sb([P, 8, 2, 64], BF16, "AR")
    rAR = ph.res()
    tmq = {nm: (ph.sb([64, 8, P], BF16, nm), ph.res()) for nm in ("ATM", "BHM", "KHM", "VTM")}
    M1 = ph.sb([64, 16, P], BF16, "M1")
    rM1 = ph.res()
    M2 = ph.sb([64, 16, P], BF16, "M2")
    rM2 = ph.res()
    Xs = [ph.sb([64, 16, 64], BF16, "X") for _ in range(2)]
    rXs = [ph.res(), ph.res()]
    XTs = [ph.sb([64, 16, 64], BF16, "XT") for _ in range(2)]
    rXTs = [ph.res(), ph.res()]
    Z = ph.sb([64, 16, P], BF16, "Z")
    rZ = ph.res()
    QE = ph.sb([64, 16, 64], BF16, "QE")
    rQE = ph.res()
    GT = ph.sb([64, 16, 64], BF16, "GT")
    rGT = ph.res()
    HI = ph.sb([64, 16, 64], BF16, "HI")
    rHI = ph.res()
    PC = ph.sb([64, 2, 8], F32, "PC")
    rPC = ph.res()
    PCt = ph.sb([P, 8], F32, "PCt")
    rPCt = ph.res()
    AR1 = ph.sb([64, 8, 2, 64], BF16, "AR1")
    rAR1 = ph.res()
    BK1 = ph.sb([64, 2, G], BF16, "BK1")
    rBK1 = ph.res()
    RT0 = ph.sb([64, 2, G], BF16, "RT0")
    rRT0 = ph.res()

    cur_h = {}

    def proj(chunk, halo):
        hT, rhT = cur_h["hT"]
        wt, rwt = wis.load(win_v[:, chunk, :, :])
        pp, rpp = FB.next()
        if halo:
            phh, rph = FB.next()
        else:
            phh, rph = None, None

        def mm(e):
            ins = None
            for kc in range(KC):
                ins = e.matmul(pp[:, :], lhsT=wt[:, kc, :], rhs=hT[:, kc, 1:G + 1], start=(kc == 0), stop=(kc == KC - 1))
            if halo:
                for kc in range(KC):
                    ins = e.matmul(phh[:, 0:2], lhsT=wt[:, kc, :], rhs=col2(hT, kc, 0, G + 1), start=(kc == 0), stop=(kc == KC - 1))
            return ins
        E("pe", mm, [rwt, rhT], [rpp] + ([rph] if halo else []))
        return pp, rpp, phh, rph

    def shift(pp, rpp, phh, rph, mi, np_=P):
        o, ro = tf.next()
        omm = dc[0:np_, D_OMM + mi:D_OMM + mi + 1]
        hm = dc[0:np_, D_HM + mi:D_HM + mi + 1]
        sp_ = src(pp[0:np_, :])
        E("act", lambda e: e.activation(out=o[0:np_, :], in_=sp_, func=AF.Identity, scale=omm), [rpp, rdc], [ro])
        if bwd:
            a_ = rev(pp[0:np_, 1:G])
            b_ = rev(pp[0:np_, 0:G - 1])
            hf, hl = phh[0:np_, 1:2], phh[0:np_, 0:1]
        else:
            a_ = pp[0:np_, 0:G - 1]
            b_ = pp[0:np_, 1:G]
            hf, hl = phh[0:np_, 0:1], phh[0:np_, 1:2]
        E("dve", lambda e: e.scalar_tensor_tensor(out=o[0:np_, 1:G], in0=a_, scalar=hm, in1=o[0:np_, 1:G], op0=ALU.mult, op1=ALU.add), [rpp, rdc, ro], [ro])
        E("dve", lambda e: e.scalar_tensor_tensor(out=o[0:np_, 0:G - 1], in0=b_, scalar=hm, in1=o[0:np_, 0:G - 1], op0=ALU.mult, op1=ALU.add), [rpp, rdc, ro], [ro])
        E("dve", lambda e: e.scalar_tensor_tensor(out=o[0:np_, 0:1], in0=hf, scalar=hm, in1=o[0:np_, 0:1], op0=ALU.mult, op1=ALU.add), [rph, rdc, ro], [ro])
        E("dve", lambda e: e.scalar_tensor_tensor(out=o[0:np_, G - 1:G], in0=hl, scalar=hm, in1=o[0:np_, G - 1:G], op0=ALU.mult, op1=ALU.add), [rph, rdc, ro], [ro])
        return o, ro

    for g in groups:
        g0 = g * G
        hT, rhT = hTs.next()
        cur_h["hT"] = (hT, rhT)
        lo = max(g0 - 1, 0)
        hi = min(g0 + G + 1, ntot)
        dma(ph, "sp", hT[:, :, lo - (g0 - 1):hi - (g0 - 1)], hT_v[:, :, lo:hi], writes=[rhT])
        if lo > g0 - 1:
            E("pool", lambda e, hT=hT: e.memset(hT[:, :, 0:1], 0.0), [], [rhT])
        if hi < g0 + G + 1:
            E("pool", lambda e, hT=hT: e.memset(hT[:, :, G + 1:G + 2], 0.0), [], [rhT])
        dma(ph, "sp", mkg[:], dr["maskrow"][0, g0:g0 + G].partition_broadcast(P), writes=[rmkg])

        pp, rpp, phh, rph = proj(64, True)
        o, ro = shift(pp, rpp, phh, rph, 24)
        E("act", lambda e, o=o: e.activation(out=th[:, :], in_=o[:, :], func=AF.Tanh), [ro], [rth])
        pp, rpp, phh, rph = proj(65, True)
        o, ro = shift(pp, rpp, phh, rph, 25, np_=64)
        E("act", lambda e, o=o: acopy(e, out=adb[:, :], in_=o[0:64, :]), [ro], [radb])
        if final:
            pp, rpp, phh, rph = proj(66, True)
            o, ro = shift(pp, rpp, phh, rph, 26)
            E("act", lambda e, o=o: e.activation(out=sgd[:, 0, :], in_=o[:, :], func=AF.Sigmoid), [ro], [rsgd])
            pp, rpp, phh, rph = proj(67, True)
            o, ro = shift(pp, rpp, phh, rph, 27, np_=32)
            E("act", lambda e, o=o: e.activation(out=sgd[0:32, 1, :], in_=o[0:32, :], func=AF.Sigmoid), [ro], [rsgd])

        def hg_A(h):
            lb = dc[:, D_LB + h:D_LB + h + 1]
            oml = dc[:, D_OML + h:D_OML + h + 1]
            noml = dc[:, D_NOML + h:D_NOML + h + 1]
            if full:
                pq, rpq, _, _ = proj(h, False)
            pi, rpi, _, _ = proj(8 + h, False)
            pz, rpz, _, _ = proj(16 + dirn * 8 + h, False)
            if final:
                pgt, rpgt, _, _ = proj(32 + h, False)
            sg, rsg = tf.next()
            E("act", lambda e, sg=sg, pz=pz: e.activation(out=sg[:, :], in_=src(pz[:, :]), func=AF.Sigmoid), [rpz], [rsg])
            f_, rf = tf.next()
            E("dve", lambda e, f_=f_, sg=sg, oml=oml, lb=lb: e.tensor_scalar(out=f_[:, :], in0=sg[:, :], scalar1=oml, scalar2=lb, op0=ALU.mult, op1=ALU.add), [rsg, rdc], [rf])
            kk, rkk = tf.next()
            E("dve", lambda e, kk=kk, sg=sg, oml=oml, noml=noml: e.tensor_scalar(out=kk[:, :], in0=sg[:, :], scalar1=noml, scalar2=oml, op0=ALU.mult, op1=ALU.add), [rsg, rdc], [rkk])
            lf, rlf = tf.next()
            E("act", lambda e, lf=lf, f_=f_: e.activation(out=lf[:, :], in_=f_[:, :], func=AF.Ln), [rf], [rlf])
            bcs, rbcs = tf.next()
            E("dve", lambda e, bcs=bcs, lf=lf: e.tensor_tensor_scan(out=bcs[:, :], data0=rmask[:, :], data1=lf[:, :], initial=0.0, op0=ALU.mult, op1=ALU.add), [rlf, rrm], [rbcs])
            eb, reb = tf.next()
            E("act", lambda e, eb=eb, bcs=bcs: e.activation(out=eb[:, :], in_=bcs[:, :], func=AF.Exp), [rbcs], [reb])
            enb, renb = tf.next()
            E("act", lambda e, enb=enb, bcs=bcs: e.activation(out=enb[:, :], in_=bcs[:, :], func=AF.Exp, scale=-1.0), [rbcs], [renb])
            if full:
                QD, rQD = tb_.next()
                E("dve", lambda e, QD=QD, pq=pq, eb=eb: e.tensor_tensor(out=QD[:, :], in0=src(pq[:, :]), in1=eb[:, :], op=ALU.mult), [rpq, reb], [rQD])
                KD, rKD = tb_.next()
                E("pool", lambda e, KD=KD, kk=kk, enb=enb: e.tensor_tensor(out=KD[:, :], in0=kk[:, :], in1=enb[:, :], op=ALU.mult), [rkk, renb], [rKD])
            khf, rkhf = tf.next()
            E("dve", lambda e, khf=khf, kk=kk, enb=enb: e.tensor_tensor(out=khf[:, :], in0=kk[:, :], in1=enb[:, :], op=ALU.mult), [rkk, renb], [rkhf])
            KH, rKH = tb_.next()
            ebl = bcast_last(eb[:, :].rearrange("p (n c) -> p n c", c=64)[:, :, 63:64], 64)
            E("dve", lambda e, KH=KH, khf=khf, ebl=ebl: e.tensor_tensor(out=KH[:, :].rearrange("p (n c) -> p n c", c=64), in0=khf[:, :].rearrange("p (n c) -> p n c", c=64),
                                                                          in1=ebl, op=ALU.mult), [rkhf, reb], [rKH])
            VT, rVT = tb_.next()
            E("act", lambda e, VT=VT, pi=pi: acopy(e, out=VT[:, :], in_=src(pi[:, :])), [rpi], [rVT])
            if final:
                GSh, rGSh = tb_.next()
                E("act", lambda e, pgt=pgt, GSh=GSh: e.activation(out=GSh[:, :], in_=src(pgt[:, :]), func=AF.Silu), [rpgt], [rGSh])
            ptb, rptb = TB.next()

            def trh(e, ptb=ptb, VT=VT, KH=KH):
                ins = None
                for b in range(4):
                    e.transpose(ptb[:, b * P:(b + 1) * P], VT[:, b * P:(b + 1) * P], ident[:, :])
                for b in range(4):
                    ins = e.transpose(ptb[:, (4 + b) * P:(5 + b) * P], KH[:, b * P:(b + 1) * P], ident[:, :])
                return ins
            E("pe", trh, [rVT, rKH, rident], [rptb])
            VK, rVK = tb_.next()
            E("dve", lambda e, VK=VK, ptb=ptb: e.tensor_copy(out=VK[:, :], in_=ptb[:, 0:4 * P]), [rptb], [rVK])
            VK2, rVK2 = tb_.next()
            E("dve", lambda e, VK2=VK2, ptb=ptb: e.tensor_copy(out=VK2[:, :], in_=ptb[:, 4 * P:8 * P]), [rptb], [rVK2])
            return dict(locals())

        def hg_B(h, V):
            QD, rQD, KD, rKD = V.get("QD"), V.get("rQD"), V.get("KD"), V.get("rKD")
            VK, rVK, VK2, rVK2 = V["VK"], V["rVK"], V["VK2"], V["rVK2"]
            eb, reb = V["eb"], V["reb"]
            GSh, rGSh = V.get("GSh"), V.get("rGSh")
            if full:
                ys, rys = yst.next()
            for b in range(4):
                bs = slice(b * P, (b + 1) * P)
                if full:
                    pat, rpat = FB.next()
                    E("pe", lambda e, pat=pat, KD=KD, QD=QD, bs=bs: e.matmul(pat[:, 0:P], lhsT=KD[:, bs], rhs=QD[:, bs], start=True, stop=True), [rKD, rQD], [rpat])
                    ATT, rATT = tb_.next()
                    E("dve", lambda e, ATT=ATT, pat=pat: e.tensor_tensor(out=ATT[:, 0:P], in0=pat[:, 0:P], in1=trim[:, :], op=ALU.mult), [rpat, rtrim], [rATT])
                    po, rpo = FB.next()
                    E("pe", lambda e, po=po, VK=VK, ATT=ATT, b=b: e.matmul(po[:, 0:P], lhsT=VK[:, b * P:(b + 1) * P], rhs=ATT[:, 0:P], start=True, stop=False), [rVK, rATT], [rpo])
                for c in range(2):
                    cp = slice(64 * c, 64 * c + 64)
                    cur = sbp[h]
                    if full:
                        E("pe", lambda e, po=po, cur=cur, h=h, QD=QD, b=b, c=c: e.matmul(po[:, 64 * c:64 * c + 64], lhsT=Sbf[cur][:, h, :], rhs=QD[:, b * P + 64 * c:b * P + 64 * c + 64],
                                                                                         start=False, stop=(c == 1)), [rSbf[cur][h], rQD], [rpo])
                    pu, rpu = FB.next()
                    E("pe", lambda e, pu=pu, VK=VK, VK2=VK2, b=b, cp=cp: e.matmul(pu[:, 0:P], lhsT=VK2[cp, b * P:(b + 1) * P], rhs=VK[cp, b * P:(b + 1) * P], start=True, stop=True),
                      [rVK, rVK2], [rpu])
                    dcol = eb[:, b * P + 64 * c + 63:b * P + 64 * c + 64]
                    E("dve", lambda e, pu=pu, h=h, dcol=dcol: e.scalar_tensor_tensor(out=S32[:, h, :], in0=S32[:, h, :], scalar=dcol, in1=pu[:, 0:P], op0=ALU.mult, op1=ALU.add),
                      [rS32[h], reb, rpu], [rS32[h]])
                    nxt = 1 - cur
                    E("act", lambda e, nxt=nxt, h=h: acopy(e, out=Sbf[nxt][:, h, :], in_=S32[:, h, :]), [rS32[h]], [rSbf[nxt][h]])
                    sbp[h] = nxt
                if full:
                    if final:
                        pt_, rpt_ = parts.next()
                        dma(ph, "sp", pt_[:, 0:P], dr["ya_part"][h][:, g0 + (3 - b) * P:g0 + (4 - b) * P] if bwd else dr["ya_part"][h][:, g0 + b * P:g0 + (b + 1) * P], writes=[rpt_])
                        E("dve", lambda e, ys=ys, po=po, pt_=pt_, bs=bs: e.tensor_tensor(out=ys[:, bs], in0=po[:, 0:P], in1=src(pt_[:, 0:P]), op=ALU.add), [rpo, rpt_], [rys])
                    else:
                        E("act", lambda e, ys=ys, po=po, bs=bs: acopy(e, out=ys[:, bs], in_=po[:, 0:P]), [rpo], [rys])
            if full and not final:
                dma(ph, "sp", dr["ya_part"][h][:, g0:g0 + G], src(ys[:, :]) if bwd else ys[:, :], reads=[rys])
            if final:
                sq, rsq = tb_.next()
                E("act", lambda e, sq=sq, ys=ys: e.activation(out=sq[:, :], in_=ys[:, :], func=AF.Square), [rys], [rsq])
                pm, rpm = FB.next()
                E("pe", lambda e, pm=pm, sq=sq: e.matmul(pm[:, :], lhsT=ones_b[:, :], rhs=sq[:, :], start=True, stop=True), [rones, rsq], [rpm])
                vr, rvr = tf.next()
                E("dve", lambda e, vr=vr, pm=pm: e.tensor_scalar(out=vr[:, :], in0=pm[:, :], scalar1=1.0 / 128, scalar2=1e-6, op0=ALU.mult, op1=ALU.add), [rpm], [rvr])
                E("act", lambda e, vr=vr: e.activation(out=vr[:, :], in_=vr[:, :], func=AF.Sqrt), [rvr], [rvr])
                E("dve", lambda e, vr=vr: e.reciprocal(out=vr[:, :], in_=vr[:, :]), [rvr], [rvr])
                on = cst[:, C_ONORM + h:C_ONORM + h + 1]
                t3, rt3 = tf.next()
                E("dve", lambda e, t3=t3, ys=ys, on=on, vr=vr: e.scalar_tensor_tensor(out=t3[:, :], in0=ys[:, :], scalar=on, in1=vr[:, :], op0=ALU.mult, op1=ALU.mult), [rys, rcst, rvr], [rt3])
                yaT, ryaT = yaTs.next()
                E("pool", lambda e, t3=t3, yaT=yaT, GSh=GSh: e.tensor_tensor(out=src(yaT[:, :]), in0=t3[:, :], in1=GSh[:, :], op=ALU.mult), [rt3, rGSh], [ryaT])
                dma(ph, "sp", dr["yaT"][h][:, g0:g0 + G], yaT[:, :], reads=[ryaT])


        nh = 0 if dr.get("skip_hgrn") else 8
        VA = {}
        if nh:
            VA[0] = hg_A(0)
        for h in range(nh):
            if h + 1 < nh:
                VA[h + 1] = hg_A(h + 1)
            hg_B(h, VA.pop(h))

        for c in range(0 if dr.get("skip_rwkv") else 8):
            if full:
                pr, rpr, phr, rphr = proj(40 + c, True)
                rs_, rrs = shift(pr, rpr, phr, rphr, c)
            pk, rpk, phk, rphk = proj(48 + c, True)
            ks_, rks = shift(pk, rpk, phk, rphk, 8 + c)
            pv, rpv, phv, rphv = proj(56 + c, True)
            vs_, rvs = shift(pv, rpv, phv, rphv, 16 + c)
            cs128 = slice(c * P, (c + 1) * P)
            pzw, rpzw = FB.next()
            d0 = dirn * 64
            E("pe", lambda e, pzw=pzw, cs128=cs128: e.matmul(pzw[:, :], lhsT=w2sb[d0:d0 + 64, cs128], rhs=th[d0:d0 + 64, :], start=True, stop=True), [rw2, rth], [rpzw])
            sgw, rsgw = tf.next()
            E("act", lambda e, sgw=sgw, pzw=pzw, c=c: e.activation(out=sgw[:, :], in_=pzw[:, :], func=AF.Sigmoid, bias=cst[:, C_W0 + dirn * 8 + c:C_W0 + dirn * 8 + c + 1]), [rpzw, rcst], [rsgw])
            css, rcss = tf.next()
            E("dve", lambda e, css=css, sgw=sgw: e.tensor_tensor_scan(out=css[:, :], data0=rmask[:, :], data1=sgw[:, :], initial=0.0, op0=ALU.mult, op1=ALU.add), [rsgw, rrm], [rcss])
            ec, rec = tf.next()
            E("act", lambda e, ec=ec, css=css: e.activation(out=ec[:, :], in_=css[:, :], func=AF.Exp, scale=-K0), [rcss], [rec])
            enc, renc = tf.next()
            E("act", lambda e, enc=enc, css=css: e.activation(out=enc[:, :], in_=css[:, :], func=AF.Exp, scale=K0), [rcss], [renc])
            dlt, rdlt = tf.next()
            E("pool", lambda e, dlt=dlt, css=css, sgw=sgw: e.tensor_tensor(out=dlt[:, :], in0=css[:, :], in1=sgw[:, :], op=ALU.subtract), [rcss, rsgw], [rdlt])
            E("act", lambda e, dlt=dlt: e.activation(out=dlt[:, :], in_=dlt[:, :], func=AF.Exp, scale=-K0), [rdlt], [rdlt])
            pza, rpza = FB.next()
            E("pe", lambda e, pza=pza, cs128=cs128: e.matmul(pza[:, :], lhsT=a2sb[0:64, cs128], rhs=adb[0:64, :], start=True, stop=True), [ra2, radb], [rpza])
            asg, rasg = tf.next()
            E("act", lambda e, asg=asg, pza=pza, c=c: e.activation(out=asg[:, :], in_=pza[:, :], func=AF.Sigmoid, bias=cst[:, C_A0 + c:C_A0 + c + 1]), [rpza, rcst], [rasg])
            kk0, rkk0 = tf.next()
            E("act", lambda e, kk0=kk0, ks_=ks_, c=c: e.activation(out=kk0[:, :], in_=ks_[:, :], func=AF.Identity, scale=cst[:, C_KK + c:C_KK + c + 1]), [rks, rcst], [rkk0])
            sq, rsq = tb_.next()
            E("act", lambda e, sq=sq, kk0=kk0: e.activation(out=sq[:, :], in_=kk0[:, :], func=AF.Square), [rkk0], [rsq])
            pss, rpss = FB.next()
            E("pe", lambda e, pss=pss, sq=sq: e.matmul(pss[:, :], lhsT=blk64[:, :], rhs=sq[:, :], start=True, stop=True), [rblk64, rsq], [rpss])
            rn, rrn = tf.next()
            E("act", lambda e, rn=rn, pss=pss: e.activation(out=rn[:, :], in_=pss[:, :], func=AF.Sqrt), [rpss], [rrn])
            E("dve", lambda e, rn=rn: e.tensor_scalar(out=rn[:, :], in0=rn[:, :], scalar1=1e-12, scalar2=None, op0=ALU.max), [rrn], [rrn])
            E("dve", lambda e, rn=rn: e.reciprocal(out=rn[:, :], in_=rn[:, :]), [rrn], [rrn])
            E("dve", lambda e, kk0=kk0, rn=rn: e.tensor_tensor(out=kk0[:, :], in0=kk0[:, :], in1=rn[:, :], op=ALU.mult), [rkk0, rrn], [rkk0])
            km, rkm = tf.next()
            E("dve", lambda e, km=km, asg=asg, c=c: e.tensor_scalar(out=km[:, :], in0=asg[:, :], scalar1=-1.0, scalar2=cst[:, C_KA + c:C_KA + c + 1], op0=ALU.add, op1=ALU.mult), [rasg, rcst], [rkm])
            E("dve", lambda e, km=km, ks_=ks_: e.scalar_tensor_tensor(out=km[:, :], in0=km[:, :], scalar=1.0, in1=ks_[:, :], op0=ALU.add, op1=ALU.mult), [rkm, rks], [rkm])
            AR4 = AR[:, :, 0, :]
            E("dve", lambda e, kk0=kk0, dlt=dlt: e.scalar_tensor_tensor(out=AR[:, :, 0, :], in0=kk0[:, :].rearrange("p (n c) -> p n c", c=64), scalar=-1.0,
                                                                          in1=dlt[:, :].rearrange("p (n c) -> p n c", c=64), op0=ALU.mult, op1=ALU.mult), [rkk0, rdlt], [rAR])
            if full:
                E("dve", lambda e, rs_=rs_, ec=ec: e.tensor_tensor(out=AR[:, :, 1, :], in0=rs_[:, :].rearrange("p (n c) -> p n c", c=64),
                                                                    in1=ec[:, :].rearrange("p (n c) -> p n c", c=64), op=ALU.mult), [rrs, rec], [rAR])
            bv, rbv = tf.next()
            E("pool", lambda e, bv=bv, kk0=kk0, asg=asg: e.tensor_tensor(out=bv[:, :], in0=kk0[:, :], in1=asg[:, :], op=ALU.mult), [rkk0, rasg], [rbv])
            BT, rBT = tb_.next()
            E("pool", lambda e, BT=BT, bv=bv, enc=enc: e.tensor_tensor(out=BT[:, :], in0=bv[:, :], in1=enc[:, :], op=ALU.mult), [rbv, renc], [rBT])
            KT, rKT = tb_.next()
            E("pool", lambda e, KT=KT, km=km, enc=enc: e.tensor_tensor(out=KT[:, :], in0=km[:, :], in1=enc[:, :], op=ALU.mult), [rkm, renc], [rKT])
            ecl = bcast_last(ec[:, :].rearrange("p (n c) -> p n c", c=64)[:, :, 63:64], 64)
            BH, rBH = tb_.next()
            E("dve", lambda e, BH=BH, BT=BT, ecl=ecl: e.tensor_tensor(out=BH[:, :].rearrange("p (n c) -> p n c", c=64), in0=BT[:, :].rearrange("p (n c) -> p n c", c=64), in1=ecl, op=ALU.mult), [rBT, rec], [rBH])
            KH_, rKH_ = tb_.next()
            E("dve", lambda e, KH_=KH_, KT=KT, ecl=ecl: e.tensor_tensor(out=KH_[:, :].rearrange("p (n c) -> p n c", c=64), in0=KT[:, :].rearrange("p (n c) -> p n c", c=64), in1=ecl, op=ALU.mult), [rKT, rec], [rKH_])
            VTr, rVTr = tb_.next()
            E("pool", lambda e, VTr=VTr, vs_=vs_: e.tensor_tensor(out=VTr[:, :], in0=vs_[:, :], in1=src(mkg[:, :]), op=ALU.mult), [rvs, rmkg], [rVTr])
            tap("AR", AR[:].rearrange("p n a c -> p (n a c)"), rAR)
            tap("BT", BT[:, :], rBT)
            tap("KT", KT[:, :], rKT)
            tap("ec", ec[:, :], rec)
            tap("enc", enc[:, :], renc)
            tap("kkn", kk0[:, :], rkk0)
            tap("asg", asg[:, :], rasg)
            tap("ks", ks_[:, :], rks)
            E("act", lambda e, ec=ec: acopy(e, out=PCt[:, :], in_=ec[:, 63:G:64]), [rec], [rPCt])
            E("act", lambda e: acopy(e, out=PC[:, 0, :], in_=PCt[0:64, :]), [rPCt], [rPC])
            dma(ph, "sp", PC[:, 1, :], PCt[64:128, :], reads=[rPCt], writes=[rPC])
            if full:
                E("act", lambda e: acopy(e, out=RT0[:, 0, :].rearrange("p (n c) -> p n c", c=64), in_=AR[0:64, :, 1, :]), [rAR], [rRT0])
                dma(ph, "sp", RT0[:, 1, :].rearrange("p (n c) -> p n c", c=64), AR[64:128, :, 1, :], reads=[rAR], writes=[rRT0])
            if final:
                tq, rtq = tb_.next()
                E("dve", lambda e, tq=tq, rs_=rs_, km=km, c=c: e.scalar_tensor_tensor(out=tq[:, :], in0=rs_[:, :], scalar=cst[:, C_RK + c:C_RK + c + 1], in1=km[:, :], op0=ALU.mult, op1=ALU.mult), [rrs, rkm, rcst], [rtq])
                pbn, rpbn = FB.next()
                E("pe", lambda e, pbn=pbn, tq=tq: e.matmul(pbn[:, :], lhsT=blk64[:, :], rhs=tq[:, :], start=True, stop=True), [rblk64, rtq], [rpbn])
                bon, rbon = tb_.next()
                E("dve", lambda e, bon=bon, pbn=pbn, vs_=vs_: e.tensor_tensor(out=bon[:, :], in0=pbn[:, :], in1=vs_[:, :], op=ALU.mult), [rpbn, rvs], [rbon])
            for nm, srcT, rsrc in (("ATM", None, rAR), ("BHM", BH, rBH), ("KHM", KH_, rKH_), ("VTM", VTr, rVTr)):
                ptb, rptb = TB.next()

                def trr(e, ptb=ptb, srcT=srcT):
                    ins = None
                    for n in range(8):
                        in_ = AR[:, n, 0, :] if srcT is None else srcT[:, n * 64:(n + 1) * 64]
                        ins = e.transpose(ptb[0:64, n * P:(n + 1) * P], in_, ident[:, :])
                    return ins
                E("pe", trr, [rsrc, rident], [rptb])
                dst, rdst = tmq[nm]
                E("act" if nm in ("ATM", "KHM") else "dve",
                  (lambda e, dst=dst, ptb=ptb: acopy(e, out=dst[:, :, :], in_=ptb[0:64, :].rearrange("p (n c) -> p n c", c=P))) if nm in ("ATM", "KHM") else
                  (lambda e, dst=dst, ptb=ptb: e.tensor_copy(out=dst[:, :, :], in_=ptb[0:64, :].rearrange("p (n c) -> p n c", c=P))),
                  [rptb], [rdst])
            ATM, rATM = tmq["ATM"]
            BHM, rBHM = tmq["BHM"]
            KHM, rKHM = tmq["KHM"]
            VTM, rVTM = tmq["VTM"]
            if full:
                dma(ph, "sp", AR1[:, :, :, :], AR[64:128, :, :, :], reads=[rAR], writes=[rAR1])
            else:
                dma(ph, "sp", AR1[:, :, 0, :], AR[64:128, :, 0, :], reads=[rAR], writes=[rAR1])
            dma(ph, "sp", BK1[:, 0, :], BT[64:128, :], reads=[rBT], writes=[rBK1])
            dma(ph, "sp", BK1[:, 1, :], KT[64:128, :], reads=[rKT], writes=[rBK1])
            BT3 = [BT[0:64, :].rearrange("p (n c) -> p n c", c=64), BK1[:, 0, :].rearrange("p (n c) -> p n c", c=64)]
            KT3 = [KT[0:64, :].rearrange("p (n c) -> p n c", c=64), BK1[:, 1, :].rearrange("p (n c) -> p n c", c=64)]
            ARh = [AR[0:64, :, :, :], AR1[:, :, :, :]]
            ncol = P if full else 64

            def hs_(hh):
                return slice(hh * 64, hh * 64 + 64)
            for (lhs3, rl, Mdst, rMd) in ((BT3, rBT, M1, rM1), (KT3, rKT, M2, rM2)):
                rl = [rl, rBK1, rAR1]
                for q4 in range(4):
                    pm_, rpm_ = FB.next()

                    def mmM(e, pm_=pm_, lhs3=lhs3, q4=q4):
                        ins = None
                        for j in range(4):
                            b = q4 * 4 + j
                            n, hh = b // 2, b % 2
                            rhs = ARh[hh][:, n, :, :] if full else ARh[hh][:, n, 0, :]
                            ins = e.matmul(pm_[0:64, j * P:j * P + ncol], lhsT=lhs3[hh][:, n, :], rhs=rhs, start=True, stop=True)
                        return ins
                    E("pe", mmM, rl + [rAR], [rpm_])
                    E("dve", lambda e, pm_=pm_, Mdst=Mdst, q4=q4: e.tensor_tensor(out=Mdst[:, q4 * 4:q4 * 4 + 4, 0:ncol], in0=pm_[0:64, :].rearrange("p (j c) -> p j c", c=P)[:, :, 0:ncol],
                                                                                   in1=mskM[:, :].rearrange("p (j c) -> p j c", c=P)[:, :, 0:ncol], op=ALU.mult), [rpm_, rmskM], [rMd])
            for q8 in range(2):
                px, rpx = FB.next()

                def mmX(e, px=px, q8=q8):
                    ins = None
                    for j in range(8):
                        b = q8 * 8 + j
                        n, hh = b // 2, b % 2
                        ins = e.matmul(px[0:64, j * 64:(j + 1) * 64], lhsT=ARh[hh][:, n, 0, :], rhs=BT3[hh][:, n, :], start=True, stop=True)
                    return ins
                E("pe", mmX, [rAR, rBT, rAR1, rBK1], [rpx])
                E("dve", lambda e, px=px, q8=q8: e.tensor_tensor(out=Xs[0][:, q8 * 8:q8 * 8 + 8, :], in0=px[0:64, :].rearrange("p (j c) -> p j c", c=64),
                                                                   in1=mskX[:, :].rearrange("p (j c) -> p j c", c=64), op=ALU.mult), [rpx, rmskX], [rXs[0]])
            E("pool", lambda e: e.tensor_copy(out=XTs[0][:, :, :], in_=M1[:, :, 0:64]), [rM1], [rXTs[0]])
            E("pool", lambda e: e.tensor_copy(out=Z[:, :, 0:64].rearrange("p (n h) c -> p n h c", h=2), in_=ATM[:, :, :].rearrange("p n (h c) -> p n h c", h=2)), [rATM], [rZ])
            for q8 in range(2):
                pz0, rpz0 = FB.next()

                def mmZ0(e, pz0=pz0, q8=q8):
                    ins = None
                    for j in range(8):
                        b = q8 * 8 + j
                        n, hh = b // 2, b % 2
                        ins = e.matmul(pz0[0:64, j * 64:(j + 1) * 64], lhsT=M2[:, b, 0:64], rhs=VTM[:, n, hs_(hh)], start=True, stop=True)
                    return ins
                E("pe", mmZ0, [rM2, rVTM], [rpz0])
                E("act", lambda e, pz0=pz0, q8=q8: acopy(e, out=Z[:, q8 * 8:q8 * 8 + 8, 64:P], in_=pz0[0:64, :].rearrange("p (j c) -> p j c", c=64)), [rpz0], [rZ])
            tap("M1", M1[:].rearrange("p b c -> p (b c)"), rM1)
            tap("M2", M2[:].rearrange("p b c -> p (b c)"), rM2)
            tap("X0", Xs[0][:].rearrange("p b c -> p (b c)"), rXs[0])
            tap("Z0", Z[:].rearrange("p b c -> p (b c)"), rZ)
            cur = 0
            for it in range(6):
                for q4 in range(4):
                    pa_, rpa_ = FB.next()

                    def mmA(e, pa_=pa_, q4=q4, cur=cur):
                        ins = None
                        for j in range(4):
                            b = q4 * 4 + j
                            ins = e.matmul(pa_[0:64, j * P:(j + 1) * P], lhsT=XTs[cur][:, b, :], rhs=Z[:, b, :], start=True, stop=True)
                        return ins
                    E("pe", mmA, [rXTs[cur], rZ], [rpa_])
                    E("dve", lambda e, pa_=pa_, q4=q4: e.tensor_tensor(out=Z[:, q4 * 4:q4 * 4 + 4, :], in0=pa_[0:64, :].rearrange("p (j c) -> p j c", c=P),
                                                                        in1=Z[:, q4 * 4:q4 * 4 + 4, :], op=ALU.add), [rpa_, rZ], [rZ])
                if it < 5:
                    nxt = 1 - cur
                    for q8 in range(2):
                        p1_, rp1_ = FB.next()
                        p2_, rp2_ = FB.next()

                        def mmS(e, p1_=p1_, p2_=p2_, q8=q8, cur=cur):
                            ins = None
                            for j in range(8):
                                b = q8 * 8 + j
                                e.matmul(p1_[0:64, j * 64:(j + 1) * 64], lhsT=Xs[cur][:, b, :], rhs=XTs[cur][:, b, :], start=True, stop=True)
                                ins = e.matmul(p2_[0:64, j * 64:(j + 1) * 64], lhsT=XTs[cur][:, b, :], rhs=Xs[cur][:, b, :], start=True, stop=True)
                            return ins
                        E("pe", mmS, [rXs[cur], rXTs[cur]], [rp1_, rp2_])
                        E("act", lambda e, p1_=p1_, q8=q8, nxt=nxt: acopy(e, out=XTs[nxt][:, q8 * 8:q8 * 8 + 8, :], in_=p1_[0:64, :].rearrange("p (j c) -> p j c", c=64)), [rp1_], [rXTs[nxt]])
                        E("act", lambda e, p2_=p2_, q8=q8, nxt=nxt: acopy(e, out=Xs[nxt][:, q8 * 8:q8 * 8 + 8, :], in_=p2_[0:64, :].rearrange("p (j c) -> p j c", c=64)), [rp2_], [rXs[nxt]])
                    cur = nxt
            if full:
                for q8 in range(2):
                    pq_, rpq_ = FB.next()

                    def mmQ(e, pq_=pq_, q8=q8):
                        ins = None
                        for j in range(8):
                            b = q8 * 8 + j
                            ins = e.matmul(pq_[0:64, j * 64:(j + 1) * 64], lhsT=Z[:, b, 0:64], rhs=M1[:, b, 64:P], start=True, stop=True)
                        return ins
                    E("pe", mmQ, [rZ, rM1], [rpq_])
                    E("dve", lambda e, pq_=pq_, q8=q8: e.tensor_tensor(out=QE[:, q8 * 8:q8 * 8 + 8, :].rearrange("p (n h) c -> p n h c", h=2),
                                                                        in0=pq_[0:64, :].rearrange("p (n h c) -> p n h c", h=2, c=64),
                                                                        in1=RT0[:, :, q8 * 256:(q8 + 1) * 256].rearrange("p h (n c) -> p n h c", c=64), op=ALU.add), [rpq_, rRT0], [rQE])
            for q8 in range(2):
                pg_, rpg_ = FB.next()
                ph2, rph2 = FB.next()

                def mmG(e, pg_=pg_, ph2=ph2, q8=q8):
                    ins = None
                    for j in range(8):
                        b = q8 * 8 + j
                        n, hh = b // 2, b % 2
                        e.matmul(pg_[0:64, j * 64:(j + 1) * 64], lhsT=Z[:, b, 0:64], rhs=BHM[:, n, hs_(hh)], start=True, stop=True)
                        e.matmul(ph2[0:64, j * 64:(j + 1) * 64], lhsT=BHM[:, n, hs_(hh)], rhs=Z[:, b, 64:P], start=True, stop=False)
                        ins = e.matmul(ph2[0:64, j * 64:(j + 1) * 64], lhsT=KHM[:, n, hs_(hh)], rhs=VTM[:, n, hs_(hh)], start=False, stop=True)
                    return ins
                E("pe", mmG, [rZ, rBHM, rKHM, rVTM], [rpg_, rph2])
                E("act", lambda e, pg_=pg_, q8=q8: acopy(e, out=GT[:, q8 * 8:q8 * 8 + 8, :], in_=pg_[0:64, :].rearrange("p (j c) -> p j c", c=64)), [rpg_], [rGT])
                E("dve", lambda e, ph2=ph2, q8=q8: e.tensor_copy(out=HI[:, q8 * 8:q8 * 8 + 8, :], in_=ph2[0:64, :].rearrange("p (j c) -> p j c", c=64)), [rph2], [rHI])
            if full:
                pys = PY
            for n in range(8):
                for hh in range(2):
                    b = n * 2 + hh
                    h = 2 * c + hh
                    curh = hbp[h]
                    if full:
                        py, rpy = pys[b // 8]
                        j = b % 8

                        def mmY(e, py=py, j=j, b=b, n=n, hh=hh, h=h, curh=curh):
                            e.matmul(py[0:64, j * 64:(j + 1) * 64], lhsT=Z[:, b, 64:P], rhs=M1[:, b, 64:P], start=True, stop=False)
                            e.matmul(py[0:64, j * 64:(j + 1) * 64], lhsT=VTM[:, n, hs_(hh)], rhs=M2[:, b, 64:P], start=False, stop=False)
                            return e.matmul(py[0:64, j * 64:(j + 1) * 64], lhsT=Hbf[curh][:, h, :], rhs=QE[:, b, :], start=False, stop=True)
                        E("pe", mmY, [rZ, rM1, rM2, rVTM, rQE, rHbf[curh][h]], [rpy])
                    pn_, rpn_ = FB.next()

                    def mmH(e, pn_=pn_, b=b, h=h, curh=curh):
                        e.matmul(pn_[0:64, 0:64], lhsT=GT[:, b, :], rhs=Hbf[curh][:, h, :], start=True, stop=False)
                        return e.matmul(pn_[0:64, 0:64], lhsT=ident[0:64, 0:64], rhs=HI[:, b, :], start=False, stop=True)
                    E("pe", mmH, [rGT, rHI, rident, rHbf[curh][h]], [rpn_])
                    E("dve", lambda e, pn_=pn_, h=h, hh=hh, n=n: e.scalar_tensor_tensor(out=H32[:, h, :], in0=H32[:, h, :], scalar=PC[:, hh, n:n + 1], in1=pn_[0:64, 0:64], op0=ALU.mult, op1=ALU.add),
                      [rH32[h], rPC, rpn_], [rH32[h]])
                    nxh = 1 - curh
                    E("act", lambda e, nxh=nxh, h=h: acopy(e, out=Hbf[nxh][:, h, :], in_=H32[:, h, :]), [rH32[h]], [rHbf[nxh][h]])
                    hbp[h] = nxh
            if full:
                ysr, rysr = ysrs.next()

                def ysr_view(hh, q8, ysr=ysr):
                    return ysr[:, hh, q8 * 256:(q8 + 1) * 256].rearrange("p (n c) -> p n c", c=64)
                for q8 in range(2):
                    py, rpy = pys[q8]
                    for hh in range(2):
                        pyv = py[0:64, :].rearrange("p (n h c) -> p n h c", h=2, c=64)[:, :, hh, :]
                        if final:
                            pbuf, rpbuf = parts.next()
                            lo = g0 + (1 - q8) * 256 if bwd else g0 + q8 * 256
                            dma(ph, "sp", pbuf[0:64, 0:256], dr["yb_part"][2 * c + hh][:, lo:lo + 256], writes=[rpbuf])
                            E("dve", lambda e, pyv=pyv, pbuf=pbuf, hh=hh, q8=q8, ysr_view=ysr_view: e.tensor_tensor(
                                out=ysr_view(hh, q8), in0=pyv, in1=src(pbuf[0:64, 0:256]).rearrange("p (n c) -> p n c", c=64), op=ALU.add), [rpy, rpbuf], [rysr])
                        else:
                            E("act", lambda e, pyv=pyv, hh=hh, q8=q8, ysr_view=ysr_view: acopy(e, out=ysr_view(hh, q8), in_=pyv), [rpy], [rysr])
                if not final:
                    for hh in range(2):
                        dma(ph, "sp", dr["yb_part"][2 * c + hh][:, g0:g0 + G], src(ysr[:, hh, :]) if bwd else ysr[:, hh, :], reads=[rysr])
                if final:
                    dma(ph, "sp", bon1[:, :], bon[64:128, :], reads=[rbon], writes=[rbon1])
                    ybT, rybT = ybTs.next()
                    for hh in range(2):
                        h = 2 * c + hh
                        yv = ysr[:, hh, :]
                        ybf, rybf = tb_.next()
                        E("act", lambda e, ybf=ybf, yv=yv: acopy(e, out=ybf[0:64, :], in_=yv), [rysr], [rybf])
                        pm, rpm = FB.next()
                        E("pe", lambda e, pm=pm, ybf=ybf: e.matmul(pm[0:64, :], lhsT=ones64[:, :], rhs=ybf[0:64, :], start=True, stop=True), [rones64, rybf], [rpm])
                        dd, rdd = tf.next()
                        E("dve", lambda e, dd=dd, yv=yv, pm=pm: e.tensor_tensor(out=dd[0:64, :], in0=yv, in1=pm[0:64, :], op=ALU.subtract), [rysr, rpm], [rdd])
                        dsq, rdsq = tb_.next()
                        E("act", lambda e, dsq=dsq, dd=dd: e.activation(out=dsq[0:64, :], in_=dd[0:64, :], func=AF.Square), [rdd], [rdsq])
                        pv2, rpv2 = FB.next()
                        E("pe", lambda e, pv2=pv2, dsq=dsq: e.matmul(pv2[0:64, :], lhsT=ones64[:, :], rhs=dsq[0:64, :], start=True, stop=True), [rones64, rdsq], [rpv2])
                        vr, rvr = tf.next()
                        E("dve", lambda e, vr=vr, pv2=pv2: e.tensor_scalar(out=vr[0:64, :], in0=pv2[0:64, :], scalar1=64e-5, scalar2=None, op0=ALU.add), [rpv2], [rvr])
                        E("act", lambda e, vr=vr: e.activation(out=vr[0:64, :], in_=vr[0:64, :], func=AF.Sqrt), [rvr], [rvr])
                        E("dve", lambda e, vr=vr: e.reciprocal(out=vr[0:64, :], in_=vr[0:64, :]), [rvr], [rvr])
                        E("dve", lambda e, dd=dd, vr=vr: e.tensor_tensor(out=dd[0:64, :], in0=dd[0:64, :], in1=vr[0:64, :], op=ALU.mult), [rdd, rvr], [rdd])
                        E("act", lambda e, dd=dd, h=h: e.activation(out=dd[0:64, :], in_=dd[0:64, :], func=AF.Identity, scale=lnc[:, h:h + 1], bias=lnc[:, 16 + h:17 + h]), [rdd, rlnc], [rdd])
                        if hh == 0:
                            E("dve", lambda e, dd=dd, bon=bon: e.tensor_tensor(out=dd[0:64, :], in0=dd[0:64, :], in1=bon[0:64, :], op=ALU.add), [rdd, rbon], [rdd])
                        else:
                            E("dve", lambda e, dd=dd: e.tensor_tensor(out=dd[0:64, :], in0=dd[0:64, :], in1=bon1[:, :], op=ALU.add), [rdd, rbon1], [rdd])
                        pgg, rpgg = FB.next()

                        def mmg(e, pgg=pgg, h=h):
                            e.matmul(pgg[0:64, :], lhsT=g2a[:, h * 64:(h + 1) * 64], rhs=sgd[:, 0, :], start=True, stop=False)
                            return e.matmul(pgg[0:64, :], lhsT=g2b[0:32, h * 64:(h + 1) * 64], rhs=sgd[0:32, 1, :], start=False, stop=True)
                        E("pe", mmg, [rg2a, rg2b, rsgd], [rpgg])
                        E("dve", lambda e, dd=dd, pgg=pgg, hh=hh, ybT=ybT: e.tensor_tensor(out=src(ybT[:, hh, :]), in0=dd[0:64, :], in1=pgg[0:64, :], op=ALU.mult), [rdd, rpgg], [rybT])
                    dma(ph, "sp", dr["ybT"].rearrange("h p t -> p h t")[:, 2 * c:2 * c + 2, g0:g0 + G], ybT[:, :, :], reads=[rybT])
    if dr.get("st_out") is not None:
        dma(ph, "sp", dr["st_out"][0][:, :], S32[:].rearrange("p h e -> p (h e)"), reads=rS32)
        dma(ph, "sp", dr["st_out"][1][:, :], H32[:].rearrange("p h e -> p (h e)"), reads=rH32)
    ph.close()


def outproj_phase(nc, tag, groups, x_d, maskcol_d, yaT_d, ybT_d, woA_d, woB_d, xout_d):
    ph = Phase(nc, tag)
    S = ph.S
    mc, rmc = load_plain(ph, maskcol_d[:, :], [P, maskcol_d.shape[1]], F32, "maskcol")
    yas = RR(ph, 2, [P, 8, G], BF16, "ya")
    ybs = RR(ph, 2, [64, 16, G], BF16, "yb")
    was = RR(ph, 2, [P, 8, 512], BF16, "woA")
    wbs = RR(ph, 2, [64, 16, 512], BF16, "woB")
    xss = RR(ph, 2, [P, 4, 512], F32, "xs")
    pws = RR(ph, 2, [P, G], F32, "pw", psum=True)
    ya_v = yaT_d.rearrange("h p t -> p h t")
    yb_v = ybT_d.rearrange("h p t -> p h t")
    for g in groups:
        g0 = g * G
        ya, rya = yas.next()
        dma(ph, "sp", ya[:], ya_v[:, :, g0:g0 + G], writes=[rya])
        yb, ryb = ybs.next()
        dma(ph, "sp", yb[:], yb_v[:, :, g0:g0 + G], writes=[ryb])
        for cbk in range(4):
            wa, rwa = was.next()
            dma(ph, "sp", wa[:], woA_d[cbk], writes=[rwa])
            wb, rwb = wbs.next()
            dma(ph, "sp", wb[:], woB_d[cbk], writes=[rwb])
            xs, rxs = xss.next()
            cs = slice(cbk * 512, (cbk + 1) * 512)
            dma(ph, "sp", xs[:], x_d[g0:g0 + G, cs].rearrange("(t p) d -> p t d", p=P), writes=[rxs])
            for t in range(4):
                pw, rpw = pws.next()

                def mmo(e, pw=pw, ya=ya, yb=yb, wa=wa, wb=wb, t=t):
                    for kc in range(8):
                        e.matmul(pw[:, :], lhsT=ya[:, kc, t * P:(t + 1) * P], rhs=wa[:, kc, :], start=(kc == 0), stop=False)
                    ins = None
                    for h in range(16):
                        ins = e.matmul(pw[:, :], lhsT=yb[0:64, h, t * P:(t + 1) * P], rhs=wb[0:64, h, :], start=False, stop=(h == 15))
                    return ins
                S.op("pe", mmo, reads=[rya, ryb, rwa, rwb], writes=[rpw])
                S.op("dve", lambda e, xs=xs, pw=pw, t=t: e.tensor_tensor(out=xs[:, t, :], in0=pw[:, :], in1=xs[:, t, :], op=ALU.add), reads=[rpw, rxs], writes=[rxs])
                S.op("act", lambda e, xs=xs, t=t, g=g: e.activation(out=xs[:, t, :], in_=xs[:, t, :], func=AF.Identity, scale=mc[:, g * 4 + t:g * 4 + t + 1]),
                     reads=[rxs, rmc], writes=[rxs])
            dma(ph, "sp", xout_d[g0:g0 + G, cs].rearrange("(t p) d -> p t d", p=P), xs[:], reads=[rxs])
    ph.close()


def final_phase(nc, tag, groups, xin_d, g_d, y_d):
    ph = Phase(nc, tag)
    S = ph.S
    gbc, rg = load_bcast(ph, g_d, D, "gbc")
    nx = NormCtx(ph, None, None, with_T=False)
    xts = RR(ph, 2, [P, 4, D], F32, "xt")
    for g in groups:
        g0 = g * G
        xt, rxt = xts.next()
        dma(ph, "sp", xt[:], xin_d[g0:g0 + G, :].rearrange("(t p) d -> p t d", p=P), writes=[rxt])
        for t in range(4):
            st, rst = rms_stats(ph, nx, xt[:, t, :], rxt)
            S.op("dve", lambda e, xt=xt, t=t, st=st: e.scalar_tensor_tensor(out=xt[:, t, :], in0=xt[:, t, :], scalar=st[:, 3:4], in1=gbc[:, :],
                                                                            op0=ALU.mult, op1=ALU.mult), reads=[rxt, rst, rg], writes=[rxt])
        dma(ph, "sp", y_d[g0:g0 + G, :].rearrange("(t p) d -> p t d", p=P), xt[:], reads=[rxt])
    ph.close()


def mix_masks():
    u = np.arange(G)
    rmask = np.broadcast_to((u % 64 != 0).astype(np.float32)[None, :], (P, G)).copy()
    j = np.arange(P)[:, None]
    i = np.arange(P)[None, :]
    trimask = ((j // 64 == i // 64) & (j <= i)).astype(np.float32)
    s_ = np.arange(64)[:, None]
    t_ = np.arange(64)[None, :]
    mM = np.concatenate([(s_ < t_), (s_ <= t_)], axis=1).astype(np.float32)
    maskM = np.tile(mM, (1, 4))
    mX = (t_.T > s_.T).astype(np.float32)
    mX = (np.arange(64)[:, None] > np.arange(64)[None, :]).astype(np.float32)
    maskX = np.tile(mX, (1, 8))
    ones = np.ones((P, P), np.float32)
    blk64 = (np.arange(P)[:, None] // 64 == np.arange(P)[None, :] // 64).astype(np.float32)
    shift = np.zeros((P, 64), np.float32)
    shift[64 + np.arange(64), np.arange(64)] = 1.0
    ones64 = np.full((64, 64), 1.0 / 64, np.float32)
    return dict(rmask=rmask, trimask=trimask, maskM=maskM, maskX=maskX, ones=ones, blk64=blk64, shift=shift, ones64=ones64,
                ident=np.eye(P, dtype=np.float32))


def lay_mix(inp, d0, d1):
    w = inp["ab_w_in"][0]
    HWd = 1024
    pa = [w[:, i * HWd:(i + 1) * HWd] for i in range(5)]
    pb = w[:, 5 * HWd:]
    r_, k_, v_ = pb[:, 0:1024], pb[:, 1024:2048], pb[:, 2048:3072]
    wd = [pb[:, 3072:3136], pb[:, 3136:3200]]
    ad = pb[:, 3200:3264]
    gd = pb[:, 3264:3424]
    z64 = np.zeros((D, 64), np.float32)
    cols = [pa[0], pa[1], pa[2 + d0], pa[2 + d1], pa[4], r_, k_, v_,
            np.concatenate([wd[d0], wd[d1]], 1), np.concatenate([ad, z64], 1), gd[:, 0:128],
            np.concatenate([gd[:, 128:160], np.zeros((D, 96), np.float32)], 1)]
    w_in = lay_ws(np.concatenate(cols, axis=1)).reshape(NCH * P, KC * P)
    mu = inp["rwkv_mu"][0]
    z = np.zeros(64, np.float32)
    mus = [mu[0:1024], mu[1024:2048], mu[2048:3072]]
    mul = [np.concatenate([mu[3072 + d0 * 64:3136 + d0 * 64], mu[3072 + d1 * 64:3136 + d1 * 64]]), np.concatenate([mu[3200:3264], z]),
           mu[3264:3392], np.concatenate([mu[3392:3424], np.zeros(96, np.float32)])]
    lb = inp["hgrn_lb"]
    lbt = np.concatenate([lay_cols(lb[r]) for r in range(3)], axis=1)
    mixc = np.concatenate([lbt, lay_cols(inp["hgrn_onorm"][0])] + [lay_cols(m) for m in mus] + [m.reshape(P, 1) for m in mul] +
                          [lay_cols(inp["rwkv_w0"][0][d0]), lay_cols(inp["rwkv_w0"][0][d1]), lay_cols(inp["rwkv_a0"][0]),
                           lay_cols(inp["rwkv_kk"][0]), lay_cols(inp["rwkv_ka"][0]), lay_cols(inp["rwkv_rk"][0].reshape(-1))], axis=1)
    assert mixc.shape == (P, 108), mixc.shape
    w2 = np.concatenate([inp["rwkv_w2"][0][d0], inp["rwkv_w2"][0][d1]], axis=0)
    lnc = np.concatenate([inp["rwkv_ln_w"][0].reshape(16, 64).T, inp["rwkv_ln_b"][0].reshape(16, 64).T], axis=1)
    wo = inp["ab_w_out"][0]
    woA = lay_as(wo[:1024], 512)
    woB = np.ascontiguousarray(wo[1024:].reshape(16, 64, 4, 512).transpose(2, 1, 0, 3))
    return dict(w_in=w_in, mixc=np.ascontiguousarray(mixc), w2=np.ascontiguousarray(w2), a2=np.ascontiguousarray(inp["rwkv_a2"][0]),
                g2=np.ascontiguousarray(inp["rwkv_g2"][0]), lnc=np.ascontiguousarray(lnc),
                woA=woA.reshape(4 * P, 8 * 512), woB=woB.reshape(4 * 64, 16 * 512))


IN_SHAPES = None


def input_shapes(NTOT, L):
    return {
        "xs": [NTOT, D], "maskrow": [1, NTOT], "maskcol": [P, NTOT // P], "cos": [P, L], "sin": [P, L], "kbias": [1, L + 256],
        "mixn0": [1, D], "ffnn0": [1, D], "mixn1": [1, D], "ffnn1": [1, D], "finn": [1, D],
        "w_in": [NCH * P, KC * P], "mixc": [P, 108], "w2": [P, 1024], "a2": [64, 1024], "g2": [160, 1024], "lnc": [64, 32],
        "woA": [4 * P, 8 * 512], "woB": [4 * 64, 16 * 512],
        "rmask": [P, G], "trimask": [P, P], "maskM": [64, 512], "maskX": [64, 512], "ones": [P, P], "blk64": [P, P], "ones64": [64, 64], "ident": [P, P],
        "wqk": [40 * P, KC * P], "wv": [P, KC * 512], "wo": [4 * P, KC * 512], "band": [P, 384], "sink": [1, 16],
        "up0": [2 * FC * P, KC * P], "up1": [2 * FC * P, KC * P], "dn0": [8 * P, FC * 256], "dn1": [8 * P, FC * 256],
        "conv0": [P, 4 * FC], "conv1": [P, 4 * FC],
    }


def build(NTOT, L, NOUT):
    nc = bass.Bass("TRN2", target_bir_lowering=False)
    dr = {k: nc.dram_tensor(k, list(v), F32, kind="ExternalInput").ap() for k, v in input_shapes(NTOT, L).items()}
    y = nc.dram_tensor("y", [NOUT, D], F32, kind="ExternalOutput").ap()

    def scr(name, shape, dt=BF16):
        return nc.dram_tensor(name, list(shape), dt, kind="Internal").ap()
    wb = {k: scr(k + "_b", dr[k].shape) for k in ("w_in", "woA", "woB", "wqk", "wv", "wo", "up0", "up1", "dn0", "dn1")}
    hT0 = scr("hT0", [KC, P, NTOT])
    ya_part = scr("ya_part", [8, P, L], F32)
    yb_part = scr("yb_part", [16, 64, L], F32)
    yaT = scr("yaT", [8, P, L])
    ybT = scr("ybT", [16, 64, L])
    st_hg = scr("st_hg", [P, 1024], F32)
    st_rw = scr("st_rw", [64, 1024], F32)
    x1 = scr("x1", [L, D], F32)
    x2 = scr("x2", [L, D], F32)
    x3 = scr("x3", [L, D], F32)
    x4 = scr("x4", [L, D], F32)
    hT1 = scr("hT1", [KC, P, L])
    hT3 = scr("hT3", [KC, P, L])
    qT = scr("qT", [16, P, L])
    kT = scr("kT", [4, P, L + 256])
    vv = scr("vv", [L + 256, 512])
    NG, NGL, NGO = NTOT // G, L // G, NOUT // G

    cast_phase(nc, "pw", [(wb[k], dr[k]) for k in wb])
    normT_phase(nc, "p0", range(NG), dr["xs"], dr["mixn0"], hT0, dr["ident"])
    d2 = dict(dr)
    d2.update(hT0=hT0, w_in=wb["w_in"].rearrange("(o p) (k c) -> o p k c", p=P, c=P),
              ya_part=[ya_part[h] for h in range(8)], yb_part=[yb_part[h] for h in range(16)], yaT=yaT, ybT=ybT)
    if NG > NGL:
        d1 = dict(d2)
        d1["st_out"] = (st_hg, st_rw)
        mix_phase(nc, "p1", list(range(NGL, NG))[::-1], True, False, False, NTOT, d1)
    mix_phase(nc, "p2", list(range(NGL)), False, True, False, NTOT, d2)
    d3 = dict(d2)
    if NG > NGL:
        d3["st_in"] = (st_hg, st_rw)
    mix_phase(nc, "p3", list(range(NGL))[::-1], True, True, True, NTOT, d3)
    outproj_phase(nc, "p3b", range(NGL), dr["xs"], dr["maskcol"], yaT, ybT,
                  wb["woA"].rearrange("(n p) (k c) -> n p k c", p=P, c=512), wb["woB"].rearrange("(n p) (k c) -> n p k c", p=64, c=512), x1)
    normT_phase(nc, "p3c", range(NGL), x1, dr["ffnn0"], hT1, dr["ident"])
    ffn_phase(nc, "p4", range(NGL), x1, x2, hT1, wb["up0"].rearrange("(o p) (k c) -> o p k c", p=P, c=P),
              wb["dn0"].rearrange("(o p) (k c) -> o p k c", p=P, c=256), dr["conv0"])
    qkv_phase(nc, "p4b", range(NGL), x2, dr["mixn1"], dr["ident"], wb["wqk"].rearrange("(o p) (k c) -> o p k c", p=P, c=P),
              wb["wv"].rearrange("p (k c) -> p k c", c=512), dr["cos"], dr["sin"], qT, kT, vv)
    attn_phase(nc, "p5", L // P, x2, x3, qT, kT, vv, wb["wo"].rearrange("(n p) (k c) -> n p k c", p=P, c=512), dr["band"], dr["kbias"],
               dr["sink"], dr["maskcol"], dr["ident"])
    normT_phase(nc, "p5b", range(NGL), x3, dr["ffnn1"], hT3, dr["ident"])
    ffn_phase(nc, "p6", range(NGO), x3, x4, hT3, wb["up1"].rearrange("(o p) (k c) -> o p k c", p=P, c=P),
              wb["dn1"].rearrange("(o p) (k c) -> o p k c", p=P, c=256), dr["conv1"])
    final_phase(nc, "p7", range(NGO), x4, dr["finn"], y)
    return nc


def host_shared(inp):
    sh = dict(mix_masks())
    sh.pop("shift", None)
    sh["band"] = band_mask()
    sh["sink"] = np.ascontiguousarray(inp["att_sink"][0][None, :])
    for nm, key, i in (("mixn0", "mix_norm", 0), ("ffnn0", "ffn_norm", 0), ("mixn1", "mix_norm", 1), ("ffnn1", "ffn_norm", 1)):
        sh[nm] = np.ascontiguousarray(inp[key][i][None, :])
    sh["finn"] = np.ascontiguousarray(inp["final_norm"][None, :])
    wqk, wv = lay_qk(inp["att_w_qkv"][0])
    sh["wqk"], sh["wv"] = wqk, wv
    sh["wo"] = lay_as(inp["att_w_o"][0], 512).reshape(4 * P, KC * 512)
    for l in range(2):
        sh[f"up{l}"] = lay_up(inp["ffn_w_up"][l])
        sh[f"dn{l}"] = lay_as(inp["ffn_w_down"][l], 256).reshape(8 * P, FC * 256)
    orient = {}
    for flip in (0, 1):
        o = dict(lay_mix(inp, flip, 1 - flip))
        for l in range(2):
            cw = inp["ffn_conv_w"][l]
            o[f"conv{l}"] = lay_conv(cw[::-1] if flip else cw, inp["ffn_conv_b"][l])
        orient[flip] = o
    return sh, orient


def core_inputs(sh, orient, x_local, mask, pos, flip, NTOT, L):
    m = dict(sh)
    m.update(orient[flip])
    m["xs"] = np.ascontiguousarray(x_local, dtype=np.float32)
    m["maskrow"] = np.ascontiguousarray(mask[None, :], dtype=np.float32)
    m["maskcol"] = np.ascontiguousarray(mask.reshape(-1, P).T, dtype=np.float32)
    c, s_ = rope_tables(pos[:L])
    m["cos"], m["sin"] = c, s_
    kb = np.full((1, L + 256), NEG, np.float32)
    kb[0, P:P + L] = np.where(mask[:L] > 0, 0.0, NEG)
    m["kbias"] = kb
    return m


_NC_CACHE = {}


def run_cores(inp_w, jobs, NTOT, L, NOUT):
    sh, orient = host_shared(inp_w)
    in_maps = [core_inputs(sh, orient, x, mk, pos, fl, NTOT, L) for (x, mk, pos, fl) in jobs]
    key = (NTOT, L, NOUT)
    nc = build(NTOT, L, NOUT)
    res = run_bass_kernel_spmd(nc, in_maps, core_ids=list(range(len(jobs))))
    return [r["y"] for r in res.results]


def kernel(x_prompt, x_sample, **w):
    NTOT, L, NOUT = 16384, 8704, 8192
    inp_w = {k: np.asarray(v, dtype=np.float32) for k, v in w.items()}
    x_prompt = np.asarray(x_prompt, dtype=np.float32)
    x_sample = np.asarray(x_sample, dtype=np.float32)
    jobs = []
    ar = np.arange(NTOT)
    ones = np.ones(NTOT, np.float32)
    for b in range(2):
        jobs.append((x_prompt[b], ones, ar, 0))
        jobs.append((x_prompt[b][::-1], ones, NTOT - 1 - ar, 1))
    smask = np.concatenate([np.ones(NOUT, np.float32), np.zeros(NTOT - NOUT, np.float32)])
    for i in range(4):
        xl = np.zeros((NTOT, D), np.float32)
        xl[:NOUT] = x_sample[i]
        jobs.append((xl, smask, ar, 0))
    outs = run_cores(inp_w, jobs, NTOT, L, NOUT)
    y_prompt = np.empty((2, 2 * NOUT, D), np.float32)
    y_sample = np.empty((4, NOUT, D), np.float32)
    for b in range(2):
        y_prompt[b, :NOUT] = outs[2 * b]
        y_prompt[b, NOUT:] = outs[2 * b + 1][::-1]
    for i in range(4):
        y_sample[i] = outs[4 + i]
    return (y_prompt, y_sample)
```

```python
import types
import numpy as np
from contextlib import ExitStack
import concourse.bass as bass
import concourse.mybir as mybir
from concourse.bass_utils import run_bass_kernel_spmd

F32 = mybir.dt.float32
BF16 = mybir.dt.bfloat16
AF = mybir.ActivationFunctionType
ALU = mybir.AluOpType
AX = mybir.AxisListType

P = 128
D = 2048
KC = 16
G = 512
DFF = 5632
FC = 44
HW_ = 1024
NEG = -30000.0


def freeze(fn):
    if fn.__closure__ is None:
        return fn
    cells = []
    for c in fn.__closure__:
        try:
            cells.append(types.CellType(c.cell_contents))
        except ValueError:
            cells.append(c)
    g = types.FunctionType(fn.__code__, fn.__globals__, fn.__name__, fn.__defaults__, tuple(cells))
    g.__kwdefaults__ = fn.__kwdefaults__
    return g


class Res:
    __slots__ = ("name", "last_w", "readers")

    def __init__(self, name):
        self.name = name
        self.last_w = None
        self.readers = []


class Sched:
    ENG = ("pe", "act", "dve", "pool", "sp")
    NDMA = 12

    _pool = {}

    def __init__(self, nc, stack, tag):
        self.nc = nc
        self.tag = tag
        self.ops = {e: [] for e in self.ENG}
        pool = getattr(nc, "_sched_pool", None)
        if pool is None:
            pool = {"sem": {e: nc.alloc_semaphore(name=f"s_{e}") for e in ("pe", "act", "dve", "pool")},
                    "cnt": {e: 0 for e in ("pe", "act", "dve", "pool")},
                    "dsem": {q: [nc.alloc_semaphore(name=f"d_{q}{i}") for i in range(self.NDMA)] for q in ("sp", "pool")},
                    "dcnt": {q: [0] * self.NDMA for q in ("sp", "pool")},
                    "dnext": {"sp": 0, "pool": 0},
                    "waited": {e: {} for e in self.ENG},
                    "semobj": {}}
            nc._sched_pool = pool
        self.sem = pool["sem"]
        self.cnt = pool["cnt"]
        self.dsem = pool["dsem"]
        self.dcnt = pool["dcnt"]
        self.dnext = pool["dnext"]
        self.waited = pool["waited"]
        self.semobj = pool["semobj"]
        self.nres = 0

    def res(self, name=None):
        self.nres += 1
        return Res(name or f"r{self.nres}")

    def _need(self, eng, tok, waits):
        if tok is None:
            return
        key, val = tok
        if eng == "pe" and key == ("c", "pe"):
            return
        if self.waited[eng].get(key, 0) >= val:
            return
        waits[key] = max(waits.get(key, 0), val)

    def op(self, eng, fn, reads=(), writes=(), dma=False):
        import os
        self.nop = getattr(self, "nop", 0) + 1
        if os.environ.get("MAXOPS") and self.tag == os.environ.get("MAXTAG", "p2") and self.nop > int(os.environ["MAXOPS"]):
            return None
        fn = freeze(fn)
        waits = {}
        for r in list(reads) + list(writes):
            self._need(eng, r.last_w, waits)
        for w in writes:
            for tok in w.readers:
                self._need(eng, tok, waits)
        if dma:
            q = eng
            i = self.dnext[q]
            self.dnext[q] = (i + 1) % self.NDMA
            key = ("d", q, i)
            if self.dcnt[q][i] > 0:
                self._need(eng, (key, self.dcnt[q][i]), waits)
            self.dcnt[q][i] += 16
            tok = (key, self.dcnt[q][i])
            semo = self.dsem[q][i]
            inc = 16
        else:
            self.cnt[eng] += 1
            key = ("c", eng)
            tok = (key, self.cnt[eng])
            semo = self.sem[eng]
            inc = 1
        self.semobj[key] = semo
        for k, v in waits.items():
            self.waited[eng][k] = max(self.waited[eng].get(k, 0), v)
        self.ops[eng].append(([(self.semobj[k], v) for k, v in waits.items()], fn, semo, inc))
        for w in writes:
            w.last_w = tok
            w.readers = []
        for r in reads:
            r.readers.append(tok)
        return tok

    def emit(self, block):
        def run(engobj, lst, final_waits):
            for waits, fn, semo, inc in lst:
                for s, v in waits:
                    engobj.wait_ge(s, v)
                ins = fn(engobj)
                ins.then_inc(semo, inc)
            for s, v in final_waits:
                engobj.wait_ge(s, v)

        fin = []
        for q in ("sp", "pool"):
            for i in range(self.NDMA):
                if self.dcnt[q][i] > 0:
                    fin.append((self.dsem[q][i], self.dcnt[q][i]))
        for e in ("pe", "act", "dve", "pool"):
            if self.cnt[e] > 0:
                fin.append((self.sem[e], self.cnt[e]))

        @block.tensor
        def _(e):
            run(e, self.ops["pe"], [])

        @block.scalar
        def _(e):
            run(e, self.ops["act"], [])

        @block.vector
        def _(e):
            run(e, self.ops["dve"], [])

        @block.gpsimd
        def _(e):
            run(e, self.ops["pool"], [])

        @block.sync
        def _(e):
            run(e, self.ops["sp"], fin)


def acopy(e, out, in_):
    return e.activation(out=out, in_=in_, func=AF.Identity)


def rev(ap):
    apl = [list(d) for d in ap.ap]
    st, n = apl[-1]
    apl[-1] = [-st, n]
    return bass.AP(tensor=ap.tensor, offset=ap.offset + st * (n - 1), ap=apl)


def bcast_last(ap, n):
    apl = [list(d) for d in ap.ap]
    assert apl[-1][1] == 1
    apl[-1] = [0, n]
    return bass.AP(tensor=ap.tensor, offset=ap.offset, ap=apl)


class Phase:
    def __init__(self, nc, tag):
        self.nc = nc
        self.tag = tag
        self.stack = ExitStack()
        self.S = Sched(nc, self.stack, tag)
        self.n = 0

    def sb(self, shape, dt, name=None):
        self.n += 1
        return self.stack.enter_context(self.nc.sbuf_tensor(f"{self.tag}_{name or 'sb'}{self.n}", list(shape), dt))

    def ps(self, shape, dt, name=None):
        self.n += 1
        return self.stack.enter_context(self.nc.psum_tensor(f"{self.tag}_{name or 'ps'}{self.n}", list(shape), dt))

    def res(self, name=None):
        return self.S.res(name)

    def close(self):
        with self.nc.Block() as block:
            self.S.emit(block)
        self.stack.close()


class RR:
    def __init__(self, ph, n, shape, dt, name, psum=False):
        self.items = [((ph.ps if psum else ph.sb)(shape, dt, name), ph.res(name)) for _ in range(n)]
        self.i = 0

    def next(self):
        it = self.items[self.i]
        self.i = (self.i + 1) % len(self.items)
        return it


def dma(ph, q, out, in_, reads=(), writes=()):
    return ph.S.op(q, lambda e: e.dma_start(out=out, in_=in_), reads, writes, dma=True)


def col2(ap3, kc, a, b):
    v = ap3[:, kc, a:a + 1]
    apl = [list(d) for d in v.ap]
    apl[-1] = [(b - a) * apl[-1][0] if apl[-1][0] != 0 else (b - a), 2]
    return bass.AP(tensor=v.tensor, offset=v.offset, ap=apl)


class NormCtx:
    def __init__(self, ph, ident, rident, with_T=True):
        self.ph = ph
        self.junk = ph.sb([P, D], BF16, "junk")
        self.rjunk = ph.res()
        self.st = RR(ph, 2, [P, 4], F32, "nst")
        self.xn = RR(ph, 2, [P, D], BF16, "xn")
        self.ident = ident
        self.rident = rident
        if with_T:
            self.pT = RR(ph, 1, [P, D], BF16, "pT", psum=True)


def rms_stats(ph, nx, xt, rx):
    S = ph.S
    st, rst = nx.st.next()
    npart = xt.shape[0]
    S.op("act", lambda e: e.activation(out=nx.junk[0:npart, :], in_=xt, func=AF.Square, accum_out=st[0:npart, 0:1]),
         reads=[rx], writes=[nx.rjunk, rst])
    S.op("dve", lambda e: e.tensor_scalar(out=st[:, 1:2], in0=st[:, 0:1], scalar1=1.0 / D, scalar2=1e-6,
                                           op0=ALU.mult, op1=ALU.add), reads=[rst], writes=[rst])
    S.op("act", lambda e: e.activation(out=st[:, 2:3], in_=st[:, 1:2], func=AF.Sqrt), reads=[rst], writes=[rst])
    S.op("dve", lambda e: e.reciprocal(out=st[:, 3:4], in_=st[:, 2:3]), reads=[rst], writes=[rst])
    return st, rst


def norm_T(ph, nx, xt, rx, gbc, rg, hT, rhT, c0, ncol=P, npart=P):
    S = ph.S
    st, rst = rms_stats(ph, nx, xt, rx)
    xn, rxn = nx.xn.next()
    S.op("dve", lambda e: e.scalar_tensor_tensor(out=xn[0:npart, :], in0=xt, scalar=st[0:npart, 3:4], in1=gbc[0:npart, :],
                                                  op0=ALU.mult, op1=ALU.mult), reads=[rx, rst, rg], writes=[rxn])
    pT, rpT = nx.pT.next()

    def tr(e):
        ins = None
        for kc in range(KC):
            ins = e.transpose(pT[:, kc * P:kc * P + npart], xn[0:npart, kc * P:(kc + 1) * P], nx.ident[0:npart, 0:npart])
        return ins
    S.op("pe", tr, reads=[rxn, nx.rident], writes=[rpT])
    src = pT[:, :].rearrange("p (k c) -> p k c", k=KC)[:, :, 0:npart]
    S.op("act", lambda e: acopy(e, out=hT[:, :, c0:c0 + npart], in_=src), reads=[rpT], writes=[rhT])


def load_bcast(ph, vec_ap, n, name):
    t = ph.sb([P, n], F32, name)
    r = ph.res(name)
    dma(ph, "sp", t[:], vec_ap[0, :].partition_broadcast(P), writes=[r])
    return t, r


def load_plain(ph, ap, shape, dt, name, q="sp"):
    t = ph.sb(shape, dt, name)
    r = ph.res(name)
    dma(ph, q, t[:], ap, writes=[r])
    return t, r


class WStream:
    def __init__(self, ph, nbuf, shape, name):
        self.ph = ph
        self.rr = RR(ph, nbuf, shape, BF16, name)

    def load(self, src_ap, dst_slice=None):
        t, r = self.rr.next()
        dst = t[:] if dst_slice is None else dst_slice(t)
        dma(self.ph, "sp", dst, src_ap, writes=[r])
        return t, r


def ffn_phase(nc, tag, groups, xin_d, xout_d, hT_d, wup_d, wdn_d, convc_d):
    ph = Phase(nc, tag)
    S = ph.S
    cw, rcw = load_plain(ph, convc_d[:, :], [P, 4 * FC], F32, "cw")
    hxs = RR(ph, 2, [P, KC, G + 2], BF16, "hx")
    xss = RR(ph, 2, [P, 4, 256], F32, "xs")
    wus = WStream(ph, 3, [P, 2, KC, P], "wu")
    wds = WStream(ph, 2, [P, FC, 256], "wd")
    act = ph.sb([P, FC, G], BF16, "act")
    ract = [ph.res() for _ in range(FC)]
    pgs = RR(ph, 2, [P, G], F32, "pg", psum=True)
    pvs = RR(ph, 2, [P, G], F32, "pv", psum=True)
    phl = ph.ps([P, G], F32, "phalo")
    _r = ph.res()
    rphl = [_r, _r]
    pds = [ph.ps([P, G], F32, "pd"), ph.ps([P, G], F32, "pd")]
    rpd = [ph.res(), ph.res()]
    tmps = RR(ph, 2, [P, G], F32, "tmp")
    sls = RR(ph, 2, [P, G], F32, "sl")
    hT_v = hT_d.rearrange("k p t -> p k t")
    ntok = hT_d.shape[2]
    wup_v = wup_d.rearrange("o p k c -> p o k c")
    nh = 0
    nd = 0
    for g in groups:
        g0 = g * G
        hx, rhx = hxs.next()
        lo = max(g0 - 1, 0)
        hi = min(g0 + G + 1, ntok)
        dma(ph, "sp", hx[:, :, lo - (g0 - 1):hi - (g0 - 1)], hT_v[:, :, lo:hi], writes=[rhx])
        if lo > g0 - 1:
            S.op("pool", lambda e, hx=hx: e.memset(hx[:, :, 0:1], 0.0), writes=[rhx])
        if hi < g0 + G + 1:
            S.op("pool", lambda e, hx=hx: e.memset(hx[:, :, G + 1:G + 2], 0.0), writes=[rhx])
        for j in range(FC):
            wt, rwt = wus.load(wup_v[:, 2 * j:2 * j + 2, :, :])
            pg, rpg = pgs.next()
            pv, rpv = pvs.next()
            hs = nh % 2
            nh += 1

            def mm_gate(e, wt=wt, pg=pg, hx=hx, hs=hs):
                ins = None
                for kc in range(KC):
                    e.matmul(pg[:, :], lhsT=wt[:, 0, kc, :], rhs=hx[:, kc, 1:G + 1], start=(kc == 0), stop=(kc == KC - 1))
                for kc in range(KC):
                    ins = e.matmul(phl[:, hs * 2:hs * 2 + 2], lhsT=wt[:, 0, kc, :], rhs=col2(hx, kc, 0, G + 1),
                                   start=(kc == 0), stop=(kc == KC - 1))
                return ins
            S.op("pe", mm_gate, reads=[rwt, rhx], writes=[rpg, rphl[hs]])

            def mm_val(e, wt=wt, pv=pv, hx=hx):
                ins = None
                for kc in range(KC):
                    ins = e.matmul(pv[:, :], lhsT=wt[:, 1, kc, :], rhs=hx[:, kc, 1:G + 1], start=(kc == 0), stop=(kc == KC - 1))
                return ins
            S.op("pe", mm_val, reads=[rwt, rhx], writes=[rpv])
            tmp, rtmp = tmps.next()
            sl, rsl = sls.next()
            c0 = cw[:, 0 * FC + j:0 * FC + j + 1]
            c1 = cw[:, 1 * FC + j:1 * FC + j + 1]
            c2 = cw[:, 2 * FC + j:2 * FC + j + 1]
            cb = cw[:, 3 * FC + j:3 * FC + j + 1]
            S.op("act", lambda e, tmp=tmp, pg=pg, c1=c1, cb=cb: e.activation(out=tmp[:, :], in_=pg[:, :], func=AF.Identity, scale=c1, bias=cb),
                 reads=[rpg, rcw], writes=[rtmp])
            S.op("dve", lambda e, tmp=tmp, pg=pg, c0=c0: e.scalar_tensor_tensor(out=tmp[:, 1:G], in0=pg[:, 0:G - 1], scalar=c0, in1=tmp[:, 1:G],
                                                                                  op0=ALU.mult, op1=ALU.add), reads=[rpg, rcw, rtmp], writes=[rtmp])
            S.op("dve", lambda e, tmp=tmp, pg=pg, c2=c2: e.scalar_tensor_tensor(out=tmp[:, 0:G - 1], in0=pg[:, 1:G], scalar=c2, in1=tmp[:, 0:G - 1],
                                                                                  op0=ALU.mult, op1=ALU.add), reads=[rpg, rcw, rtmp], writes=[rtmp])
            S.op("dve", lambda e, tmp=tmp, c0=c0, hs=hs: e.scalar_tensor_tensor(out=tmp[:, 0:1], in0=phl[:, hs * 2:hs * 2 + 1], scalar=c0, in1=tmp[:, 0:1],
                                                                                  op0=ALU.mult, op1=ALU.add), reads=[rphl[hs], rcw, rtmp], writes=[rtmp])
            S.op("dve", lambda e, tmp=tmp, c2=c2, hs=hs: e.scalar_tensor_tensor(out=tmp[:, G - 1:G], in0=phl[:, hs * 2 + 1:hs * 2 + 2], scalar=c2, in1=tmp[:, G - 1:G],
                                                                                  op0=ALU.mult, op1=ALU.add), reads=[rphl[hs], rcw, rtmp], writes=[rtmp])
            S.op("act", lambda e, tmp=tmp, sl=sl: e.activation(out=sl[:, :], in_=tmp[:, :], func=AF.Silu), reads=[rtmp], writes=[rsl])
            S.op("dve", lambda e, sl=sl, pv=pv, j=j: e.tensor_tensor(out=act[:, j, :], in0=sl[:, :], in1=pv[:, :], op=ALU.mult),
                 reads=[rsl, rpv], writes=[ract[j]])
        for cbk in range(D // 256):
            wd, rwd = wds.load(wdn_d[cbk])
            xs, rxs = xss.next()
            cs = slice(cbk * 256, (cbk + 1) * 256)
            dma(ph, "sp", xs[:], xin_d[g0:g0 + G, cs].rearrange("(t p) d -> p t d", p=P), writes=[rxs])
            for t in range(4):
                hb = nd % 2
                nd += 1

                def mm_dn(e, wd=wd, t=t, hb=hb):
                    ins = None
                    for kc in range(FC):
                        ins = e.matmul(pds[hb][:, 0:256], lhsT=act[:, kc, t * P:(t + 1) * P], rhs=wd[:, kc, :], start=(kc == 0), stop=(kc == FC - 1))
                    return ins
                S.op("pe", mm_dn, reads=[rwd] + ract, writes=[rpd[hb]])
                S.op("dve", lambda e, xs=xs, t=t, hb=hb: e.tensor_tensor(out=xs[:, t, :], in0=pds[hb][:, 0:256], in1=xs[:, t, :], op=ALU.add),
                     reads=[rpd[hb], rxs], writes=[rxs])
            dma(ph, "sp", xout_d[g0:g0 + G, cs].rearrange("(t p) d -> p t d", p=P), xs[:], reads=[rxs])
    ph.close()


def normT_phase(nc, tag, groups, xin_d, g_d, hT_d, ident_d):
    ph = Phase(nc, tag)
    ident, rident = load_plain(ph, ident_d[:, :], [P, P], BF16, "ident", q="pool")
    gbc, rg = load_bcast(ph, g_d, D, "gbc")
    nx = NormCtx(ph, ident, rident)
    xts = RR(ph, 2, [P, 4, D], F32, "xt")
    hts = RR(ph, 2, [P, KC, G], BF16, "hT")
    hT_v = hT_d.rearrange("k p t -> p k t")
    for g in groups:
        g0 = g * G
        xt, rxt = xts.next()
        dma(ph, "sp", xt[:], xin_d[g0:g0 + G, :].rearrange("(t p) d -> p t d", p=P), writes=[rxt])
        hT, rhT = hts.next()
        for t in range(4):
            norm_T(ph, nx, xt[:, t, :], rxt, gbc, rg, hT, rhT, t * P)
        dma(ph, "sp", hT_v[:, :, g0:g0 + G], hT[:], reads=[rhT])
    ph.close()


def cast_phase(nc, tag, pairs):
    ph = Phase(nc, tag)
    for dst, src in pairs:
        rows, cols = src.shape
        step = max(1, (1 << 20) // cols)
        for r0 in range(0, rows, step):
            r1 = min(rows, r0 + step)
            dma(ph, "pool", dst[r0:r1, :], src[r0:r1, :])
    ph.close()


def lay_ws(w):
    K_, N = w.shape
    return np.ascontiguousarray(w.reshape(K_ // P, P, N // P, P).transpose(2, 1, 0, 3))


def lay_as(w, cb):
    K_, N = w.shape
    return np.ascontiguousarray(w.reshape(K_ // P, P, N // cb, cb).transpose(2, 1, 0, 3))


def lay_up(w_up):
    ws = lay_ws(w_up)
    o = np.empty_like(ws)
    o[0::2] = ws[:FC]
    o[1::2] = ws[FC:]
    return o.reshape(2 * FC * P, KC * P)


def lay_cols(v):
    return np.ascontiguousarray(v.reshape(-1, P).T)


def lay_conv(cw, cb):
    return np.ascontiguousarray(np.concatenate([lay_cols(cw[0]), lay_cols(cw[1]), lay_cols(cw[2]), lay_cols(cb)], axis=1))


def qkv_phase(nc, tag, groups, xin_d, g_d, ident_d, wqk_d, wv_d, cos_d, sin_d, qT_d, kT_d, v_d):
    ph = Phase(nc, tag)
    S = ph.S
    ident, rident = load_plain(ph, ident_d[:, :], [P, P], BF16, "ident", q="pool")
    gbc, rg = load_bcast(ph, g_d, D, "gbc")
    wv, rwv = load_plain(ph, wv_d, [P, KC, 512], BF16, "wv")
    nx = NormCtx(ph, ident, rident)
    xts = RR(ph, 1, [P, 4, D], F32, "xt")
    hts = RR(ph, 1, [P, KC, G], BF16, "hT")
    css = RR(ph, 2, [P, 2, G], F32, "cs")
    wqs = WStream(ph, 3, [P, 2, KC, P], "wq")
    pas = RR(ph, 2, [P, G], F32, "pa", psum=True)
    pbs = RR(ph, 2, [P, G], F32, "pb", psum=True)
    t1s = RR(ph, 2, [P, G], F32, "t1")
    t2s = RR(ph, 2, [P, G], F32, "t2")
    qos = RR(ph, 1, [P, 20, G], BF16, "qo")
    vos = RR(ph, 1, [P, 4, 512], BF16, "vo")
    zt = ph.sb([P, 4 * 512], BF16, "zt")
    rz = ph.res()
    S.op("pool", lambda e: e.memset(zt[:], 0.0), writes=[rz])
    Lk = kT_d.shape[2]
    kT_v = kT_d.rearrange("h p t -> p h t")
    qT_v = qT_d.rearrange("h p t -> p h t")
    wqk_v = wqk_d.rearrange("o p k c -> p o k c")
    ztk = zt[:, 0:4 * P].rearrange("p (h t) -> p h t", h=4)
    dma(ph, "sp", kT_v[:, :, 0:P], ztk, reads=[rz])
    dma(ph, "sp", kT_v[:, :, Lk - P:Lk], ztk, reads=[rz])
    dma(ph, "sp", v_d[0:P, :], zt[:, 0:512], reads=[rz])
    dma(ph, "sp", v_d[Lk - P:Lk, :], zt[:, 0:512], reads=[rz])
    for g in groups:
        g0 = g * G
        xt, rxt = xts.next()
        dma(ph, "sp", xt[:], xin_d[g0:g0 + G, :].rearrange("(t p) d -> p t d", p=P), writes=[rxt])
        cs, rcs = css.next()
        dma(ph, "sp", cs[:, 0, :], cos_d[:, g0:g0 + G], writes=[rcs])
        dma(ph, "sp", cs[:, 1, :], sin_d[:, g0:g0 + G], writes=[rcs])
        hT, rhT = hts.next()
        for t in range(4):
            norm_T(ph, nx, xt[:, t, :], rxt, gbc, rg, hT, rhT, t * P)
        qo, rqo = qos.next()
        for hh in range(20):
            wt, rwt = wqs.load(wqk_v[:, 2 * hh:2 * hh + 2, :, :])
            pa, rpa = pas.next()
            pb, rpb = pbs.next()

            def mm(e, wt=wt, pa=pa, pb=pb, hT=hT):
                ins = None
                for kc in range(KC):
                    e.matmul(pa[:, :], lhsT=wt[:, 0, kc, :], rhs=hT[:, kc, :], start=(kc == 0), stop=(kc == KC - 1))
                for kc in range(KC):
                    ins = e.matmul(pb[:, :], lhsT=wt[:, 1, kc, :], rhs=hT[:, kc, :], start=(kc == 0), stop=(kc == KC - 1))
                return ins
            S.op("pe", mm, reads=[rwt, rhT], writes=[rpa, rpb])
            t1, rt1 = t1s.next()
            t2, rt2 = t2s.next()
            S.op("dve", lambda e, t1=t1, pa=pa, cs=cs: e.tensor_tensor(out=t1[:, :], in0=pa[:, :], in1=cs[:, 0, :], op=ALU.mult),
                 reads=[rpa, rcs], writes=[rt1])
            S.op("dve", lambda e, t2=t2, pb=pb, cs=cs: e.tensor_tensor(out=t2[:, :], in0=pb[:, :], in1=cs[:, 1, :], op=ALU.mult),
                 reads=[rpb, rcs], writes=[rt2])
            S.op("pool", lambda e, t1=t1, t2=t2, qo=qo, hh=hh: e.tensor_tensor(out=qo[:, hh, :], in0=t1[:, :], in1=t2[:, :], op=ALU.add),
                 reads=[rt1, rt2], writes=[rqo])
        dma(ph, "sp", qT_v[:, :, g0:g0 + G], qo[:, 0:16, :], reads=[rqo])
        dma(ph, "sp", kT_v[:, :, P + g0:P + g0 + G], qo[:, 16:20, :], reads=[rqo])
        vo, rvo = vos.next()
        for t in range(4):
            pa, rpa = pas.next()

            def mmv(e, pa=pa, hT=hT, t=t):
                ins = None
                for kc in range(KC):
                    ins = e.matmul(pa[:, :], lhsT=hT[:, kc, t * P:(t + 1) * P], rhs=wv[:, kc, :], start=(kc == 0), stop=(kc == KC - 1))
                return ins
            S.op("pe", mmv, reads=[rwv, rhT], writes=[rpa])
            S.op("act", lambda e, vo=vo, pa=pa, t=t: acopy(e, out=vo[:, t, :], in_=pa[:, :]), reads=[rpa], writes=[rvo])
        dma(ph, "sp", v_d[P + g0:P + g0 + G, :].rearrange("(t p) c -> p t c", p=P), vo[:], reads=[rvo])
    ph.close()


def attn_phase(nc, tag, nblk, xin_d, xout_d, qT_d, kT_d, v_d, wo_d, band_d, kbias_d, sink_d, maskcol_d, ident_d):
    ph = Phase(nc, tag)
    S = ph.S
    SC = 128.0 ** -0.5
    ident, rident = load_plain(ph, ident_d[:, :], [P, P], BF16, "ident", q="pool")
    wo, rwo = load_plain(ph, wo_d.rearrange("n p k c -> p n k c"), [P, 4, KC, 512], BF16, "wo")
    band, rband = load_plain(ph, band_d[:, :], [P, 384], F32, "band")
    Lk = kbias_d.shape[1]
    kb, rkb = load_bcast(ph, kbias_d, Lk, "kbias")
    sk, rsk = load_bcast(ph, sink_d, 16, "sink")
    mc, rmc = load_plain(ph, maskcol_d[:, :], [P, maskcol_d.shape[1]], F32, "maskcol")
    qT_v = qT_d.rearrange("h p t -> p h t")
    kT_v = kT_d.rearrange("h p t -> p h t")
    qs = RR(ph, 2, [P, 16, P], BF16, "q")
    ks = RR(ph, 2, [P, 4, 384], BF16, "k")
    vs = RR(ph, 2, [P, 3, 512], BF16, "v")
    xts = RR(ph, 2, [P, D], F32, "xt")
    bns = RR(ph, 2, [P, 384], F32, "bn")
    pss = RR(ph, 2, [P, G], F32, "psc", psum=True)
    pts = RR(ph, 1, [P, 3, P], BF16, "ppt", psum=True)
    pos_ = RR(ph, 2, [P, G], F32, "po", psum=True)
    pws = RR(ph, 2, [P, G], F32, "pw", psum=True)
    ss = RR(ph, 2, [P, 384], F32, "s")
    pps = RR(ph, 2, [P, 384], F32, "p")
    pns = RR(ph, 2, [P, 384], BF16, "pn")
    pTs = RR(ph, 2, [P, 3, P], BF16, "pT")
    sts = RR(ph, 4, [P, 8], F32, "st")
    oTs = RR(ph, 2, [P, 16, P], BF16, "oT")
    for n in range(nblk):
        n0 = n * P
        q, rq = qs.next()
        dma(ph, "sp", q[:], qT_v[:, :, n0:n0 + P], writes=[rq])
        k, rk = ks.next()
        dma(ph, "sp", k[:], kT_v[:, :, n0:n0 + 384], writes=[rk])
        v, rv = vs.next()
        dma(ph, "sp", v[:], v_d[n0:n0 + 384, :].rearrange("(j p) c -> p j c", p=P), writes=[rv])
        xt, rxt = xts.next()
        dma(ph, "sp", xt[:], xin_d[n0:n0 + P, :], writes=[rxt])
        bn, rbn = bns.next()
        S.op("pool", lambda e, bn=bn, n0=n0: e.tensor_tensor(out=bn[:, :], in0=band[:, :], in1=kb[:, n0:n0 + 384], op=ALU.add),
             reads=[rband, rkb], writes=[rbn])
        oT, roT = oTs.next()
        for h in range(16):
            kv = h // 4
            psc, rpsc = pss.next()
            S.op("pe", lambda e, psc=psc, q=q, k=k, h=h, kv=kv: e.matmul(psc[:, 0:384], lhsT=q[:, h, :], rhs=k[:, kv, :], start=True, stop=True),
                 reads=[rq, rk], writes=[rpsc])
            s, rs = ss.next()
            st, rst = sts.next()
            S.op("dve", lambda e, s=s, psc=psc, bn=bn: e.scalar_tensor_tensor(out=s[:, :], in0=psc[:, 0:384], scalar=SC, in1=bn[:, :],
                                                                                op0=ALU.mult, op1=ALU.add), reads=[rpsc, rbn], writes=[rs])
            S.op("dve", lambda e, s=s, st=st: e.reduce_max(out=st[:, 0:1], in_=s[:, :], axis=AX.X), reads=[rs], writes=[rst])
            S.op("dve", lambda e, st=st, h=h: e.tensor_scalar(out=st[:, 1:2], in0=st[:, 0:1], scalar1=sk[:, h:h + 1], scalar2=-1.0,
                                                               op0=ALU.max, op1=ALU.mult), reads=[rst, rsk], writes=[rst])
            pp, rpp = pps.next()
            S.op("act", lambda e, pp=pp, s=s, st=st: e.activation(out=pp[:, :], in_=s[:, :], func=AF.Exp, bias=st[:, 1:2], accum_out=st[:, 2:3]),
                 reads=[rs, rst], writes=[rpp, rst])
            S.op("act", lambda e, st=st, h=h: e.activation(out=st[:, 3:4], in_=sk[:, h:h + 1], func=AF.Exp, bias=st[:, 1:2]),
                 reads=[rsk, rst], writes=[rst])
            S.op("dve", lambda e, st=st: e.tensor_tensor(out=st[:, 4:5], in0=st[:, 2:3], in1=st[:, 3:4], op=ALU.add), reads=[rst], writes=[rst])
            S.op("dve", lambda e, st=st: e.reciprocal(out=st[:, 5:6], in_=st[:, 4:5]), reads=[rst], writes=[rst])
            pn, rpn = pns.next()
            S.op("act", lambda e, pn=pn, pp=pp, st=st: e.activation(out=pn[:, :], in_=pp[:, :], func=AF.Identity, scale=st[:, 5:6]),
                 reads=[rpp, rst], writes=[rpn])
            ppt, rppt = pts.next()

            def trs(e, ppt=ppt, pn=pn):
                ins = None
                for j in range(3):
                    ins = e.transpose(ppt[:, j, :], pn[:, j * P:(j + 1) * P], ident[:, :])
                return ins
            S.op("pe", trs, reads=[rpn, rident], writes=[rppt])
            pT, rpT = pTs.next()
            S.op("dve", lambda e, pT=pT, ppt=ppt: e.tensor_copy(out=pT[:], in_=ppt[:]), reads=[rppt], writes=[rpT])
            po, rpo = pos_.next()

            def pv(e, po=po, v=v, pT=pT, kv=kv):
                ins = None
                for j in range(3):
                    ins = e.matmul(po[:, 0:P], lhsT=v[:, j, kv * P:(kv + 1) * P], rhs=pT[:, j, :], start=(j == 0), stop=(j == 2))
                return ins
            S.op("pe", pv, reads=[rv, rpT], writes=[rpo])
            S.op("act", lambda e, oT=oT, po=po, h=h: acopy(e, out=oT[:, h, :], in_=po[:, 0:P]), reads=[rpo], writes=[roT])
        for cbk in range(4):
            pw, rpw = pws.next()

            def mmo(e, pw=pw, oT=oT, cbk=cbk):
                ins = None
                for h in range(16):
                    ins = e.matmul(pw[:, :], lhsT=oT[:, h, :], rhs=wo[:, cbk, h, :], start=(h == 0), stop=(h == 15))
                return ins
            S.op("pe", mmo, reads=[roT, rwo], writes=[rpw])
            S.op("dve", lambda e, xt=xt, pw=pw, cbk=cbk: e.tensor_tensor(out=xt[:, cbk * 512:(cbk + 1) * 512], in0=pw[:, :],
                                                                          in1=xt[:, cbk * 512:(cbk + 1) * 512], op=ALU.add),
                 reads=[rpw, rxt], writes=[rxt])
        S.op("act", lambda e, xt=xt, n=n: e.activation(out=xt[:, :], in_=xt[:, :], func=AF.Identity, scale=mc[:, n:n + 1]),
             reads=[rxt, rmc], writes=[rxt])
        dma(ph, "sp", xout_d[n0:n0 + P, :], xt[:], reads=[rxt])
    ph.close()


def lay_qk(w_qkv):
    tiles = []
    for hh in range(20):
        blk = w_qkv[:, hh * P:(hh + 1) * P]
        perm = np.concatenate([blk[:, 64:], blk[:, :64]], axis=1)
        tiles.append(lay_ws(blk)[0])
        tiles.append(lay_ws(perm)[0])
    wqk = np.ascontiguousarray(np.stack(tiles))
    wv = lay_as(w_qkv[:, 20 * P:], 512)[0]
    return wqk.reshape(40 * P, KC * P), np.ascontiguousarray(wv.reshape(P, KC * 512))


def rope_tables(pos):
    inv = (np.float32(10000.0) ** (-(np.arange(64, dtype=np.float32) / np.float32(64)))).astype(np.float32)
    ang = (pos.astype(np.float32)[:, None] * inv[None, :]).astype(np.float32).astype(np.float64)
    c = np.cos(ang).T.astype(np.float32)
    s_ = np.sin(ang).T.astype(np.float32)
    return np.ascontiguousarray(np.concatenate([c, c], 0)), np.ascontiguousarray(np.concatenate([-s_, s_], 0))


def band_mask():
    q = np.arange(P)[:, None]
    s_ = np.arange(384)[None, :] - P
    return np.where(np.abs(s_ - q) <= 128, 0.0, NEG).astype(np.float32)


K0 = float(np.exp(-0.5))
NCH = 68


def mix_phase(nc, tag, groups, bwd, full, final, ntot, dr):
    ph = Phase(nc, tag)
    S = ph.S
    dirn = 1 if bwd else 0

    def src(ap):
        return rev(ap) if bwd else ap

    def E(eng, fn, reads=(), writes=()):
        return S.op(eng, fn, reads, writes)

    dbg = dr.get("dbg") or {}
    tapped = set()

    def tap(name, ap, r):
        if name in dbg and name not in tapped:
            tapped.add(name)
            dma(ph, "sp", dbg[name], ap, reads=[r])

    ident, rident = load_plain(ph, dr["ident"][:, :], [P, P], BF16, "ident", q="pool")
    cst, rcst = load_plain(ph, dr["mixc"][:, :], [P, dr["mixc"].shape[1]], F32, "mixc")
    C_LBT, C_ONORM, C_MU, C_W0, C_A0, C_KK, C_KA, C_RK = 0, 24, 32, 60, 76, 84, 92, 100
    rmask, rrm = load_plain(ph, dr["rmask"][:, :], [P, G], F32, "rmask")
    trim, rtrim = load_plain(ph, dr["trimask"][:, :], [P, P], F32, "trimask")
    mskM, rmskM = load_plain(ph, dr["maskM"][:, :], [64, 4 * P], F32, "maskM")
    mskX, rmskX = load_plain(ph, dr["maskX"][:, :], [64, 8 * 64], F32, "maskX")
    ones_b, rones = load_plain(ph, dr["ones"][:, :], [P, P], BF16, "ones", q="pool")
    blk64, rblk64 = load_plain(ph, dr["blk64"][:, :], [P, P], BF16, "blk64", q="pool")
    w2sb, rw2 = load_plain(ph, dr["w2"][:, :], [P, 1024], BF16, "w2", q="pool")
    a2sb, ra2 = load_plain(ph, dr["a2"][:, :], [64, 1024], BF16, "a2", q="pool")
    if final:
        g2a, rg2a = load_plain(ph, dr["g2"][0:P, :], [P, 1024], BF16, "g2a", q="pool")
        g2b, rg2b = load_plain(ph, dr["g2"][P:160, :], [32, 1024], BF16, "g2b", q="pool")
        lnc, rlnc = load_plain(ph, dr["lnc"][:, :], [64, 32], F32, "lnc")
        ones64, rones64 = load_plain(ph, dr["ones64"][:, :], [64, 64], BF16, "ones64", q="pool")
    dc = ph.sb([P, 128], F32, "dconst")
    rdc = ph.res()
    lbt = cst[:, C_LBT:C_LBT + 24].rearrange("p (r h) -> p r h", r=3)
    D_LB, D_OML, D_NOML, D_OMM, D_HM, D_T = 0, 8, 16, 24, 52, 80
    E("dve", lambda e: e.tensor_tensor(out=dc[:, D_T:D_T + 8], in0=lbt[:, 0, :], in1=lbt[:, 1, :], op=ALU.max), [rcst], [rdc])
    E("dve", lambda e: e.tensor_tensor(out=dc[:, D_T:D_T + 8], in0=dc[:, D_T:D_T + 8], in1=lbt[:, 2, :], op=ALU.max), [rcst, rdc], [rdc])
    for r_ in range(3):
        E("dve", lambda e, r_=r_: e.tensor_tensor(out=dc[:, D_T + 8 + 8 * r_:D_T + 16 + 8 * r_], in0=lbt[:, r_, :], in1=dc[:, D_T:D_T + 8], op=ALU.subtract),
          [rcst, rdc], [rdc])
    E("act", lambda e: e.activation(out=dc[:, D_T + 8:D_T + 32], in_=dc[:, D_T + 8:D_T + 32], func=AF.Exp), [rdc], [rdc])
    E("dve", lambda e: e.tensor_tensor(out=dc[:, D_T:D_T + 8], in0=dc[:, D_T + 8:D_T + 16], in1=dc[:, D_T + 16:D_T + 24], op=ALU.add), [rdc], [rdc])
    E("dve", lambda e: e.tensor_tensor(out=dc[:, D_T:D_T + 8], in0=dc[:, D_T:D_T + 8], in1=dc[:, D_T + 24:D_T + 32], op=ALU.add), [rdc], [rdc])
    E("dve", lambda e: e.reciprocal(out=dc[:, D_T:D_T + 8], in_=dc[:, D_T:D_T + 8]), [rdc], [rdc])
    E("dve", lambda e: e.tensor_tensor(out=dc[:, D_LB:D_LB + 8], in0=dc[:, D_T + 8:D_T + 16], in1=dc[:, D_T:D_T + 8], op=ALU.mult), [rdc], [rdc])
    E("dve", lambda e: e.tensor_scalar(out=dc[:, D_OML:D_OML + 8], in0=dc[:, D_LB:D_LB + 8], scalar1=-1.0, scalar2=1.0, op0=ALU.mult, op1=ALU.add), [rdc], [rdc])
    E("dve", lambda e: e.tensor_scalar(out=dc[:, D_NOML:D_NOML + 8], in0=dc[:, D_OML:D_OML + 8], scalar1=-1.0, scalar2=None, op0=ALU.mult), [rdc], [rdc])
    E("dve", lambda e: e.tensor_scalar(out=dc[:, D_OMM:D_OMM + 28], in0=cst[:, C_MU:C_MU + 28], scalar1=-1.0, scalar2=1.0, op0=ALU.mult, op1=ALU.add), [rcst], [rdc])
    E("dve", lambda e: e.tensor_scalar(out=dc[:, D_HM:D_HM + 28], in0=cst[:, C_MU:C_MU + 28], scalar1=0.5, scalar2=None, op0=ALU.mult), [rcst], [rdc])

    hTs = RR(ph, 2, [P, KC, G + 2], BF16, "hT")
    hT_v = dr["hT0"].rearrange("k p t -> p k t")
    mkg = ph.sb([P, G], F32, "maskg")
    rmkg = ph.res()
    FB = RR(ph, 4, [P, G], F32, "fb", psum=True)
    PY = [(ph.ps([P, G], F32, "py"), ph.res()) for _ in range(2)]
    TB = RR(ph, 2, [P, 8 * P], BF16, "tb", psum=True)
    wis = WStream(ph, 3, [P, KC, P], "wi")
    win_v = dr["w_in"].rearrange("o p k c -> p o k c")
    tf = RR(ph, 15, [P, G], F32, "tf")
    tb_ = RR(ph, 20, [P, G], BF16, "tbf")
    th = ph.sb([P, G], BF16, "th")
    rth = ph.res()
    adb = ph.sb([64, G], BF16, "adb")
    radb = ph.res()
    if final:
        sgd = ph.sb([P, 2, G], BF16, "sgd")
        rsgd = ph.res()
    S32 = ph.sb([P, 8, P], F32, "S32")
    rS32 = [ph.res() for _ in range(8)]
    Sbf = [ph.sb([P, 8, P], BF16, "Sbf") for _ in range(2)]
    rSbf = [[ph.res() for _ in range(8)] for _ in range(2)]
    sbp = [0] * 8
    H32 = ph.sb([64, 16, 64], F32, "H32")
    rH32 = [ph.res() for _ in range(16)]
    Hbf = [ph.sb([64, 16, 64], BF16, "Hbf") for _ in range(2)]
    rHbf = [[ph.res() for _ in range(16)] for _ in range(2)]
    hbp = [0] * 16
    if dr.get("st_in") is not None:
        dma(ph, "sp", S32[:].rearrange("p h e -> p (h e)"), dr["st_in"][0][:, :], writes=rS32)
        dma(ph, "sp", H32[:].rearrange("p h e -> p (h e)"), dr["st_in"][1][:, :], writes=rH32)
    else:
        E("pool", lambda e: e.memset(S32[:], 0.0), [], rS32)
        E("pool", lambda e: e.memset(H32[:], 0.0), [], rH32)
    E("act", lambda e: acopy(e, out=Sbf[0][:], in_=S32[:]), rS32, rSbf[0])
    E("act", lambda e: acopy(e, out=Hbf[0][:], in_=H32[:]), rH32, rHbf[0])
    if full:
        yst = RR(ph, 2, [P, G], F32, "yst")
    if full:
        ysrs = RR(ph, 1, [64, 2, G], F32, "ysr")
    if final:
        yaTs = RR(ph, 2, [P, G], BF16, "yaT")
        ybTs = RR(ph, 2, [64, 2, G], BF16, "ybT")
        parts = RR(ph, 2, [P, G], F32, "part")
        bon1 = ph.sb([64, G], BF16, "bon1")
        rbon1 = ph.res()
    AR = ph.sb([P, 8, 2, 64], BF16, "AR")
    rAR = ph.res()
    tmq = {nm: (ph.sb([64, 8, P], BF16, nm), ph.res()) for nm in ("ATM", "BHM", "KHM", "VTM")}
    M1 = ph.sb([64, 16, P], BF16, "M1")
    rM1 = ph.res()
    M2 = ph.sb([64, 16, P], BF16, "M2")
    rM2 = ph.res()
    Xs = [ph.sb([64, 16, 64], BF16, "X") for _ in range(2)]
    rXs = [ph.res(), ph.res()]
    XTs = [ph.sb([64, 16, 64], BF16, "XT") for _ in range(2)]
    rXTs = [ph.res(), ph.res()]
    Z = ph.sb([64, 16, P], BF16, "Z")
    rZ = ph.res()
    QE = ph.sb([64, 16, 64], BF16, "QE")
    rQE = ph.res()
    GT = ph.sb([64, 16, 64], BF16, "GT")
    rGT = ph.res()
    HI = ph.sb([64, 16, 64], BF16, "HI")
    rHI = ph.res()
    PC = ph.sb([64, 2, 8], F32, "PC")
    rPC = ph.res()
    PCt = ph.sb([P, 8], F32, "PCt")
    rPCt = ph.res()
    AR1 = ph.sb([64, 8, 2, 64], BF16, "AR1")
    rAR1 = ph.res()
    BK1 = ph.sb([64, 2, G], BF16, "BK1")
    rBK1 = ph.res()
    RT0 = ph.sb([64, 2, G], BF16, "RT0")
    rRT0 = ph.res()

    cur_h = {}

    def proj(chunk, halo):
        hT, rhT = cur_h["hT"]
        wt, rwt = wis.load(win_v[:, chunk, :, :])
        pp, rpp = FB.next()
        if halo:
            phh, rph = FB.next()
        else:
            phh, rph = None, None

        def mm(e):
            ins = None
            for kc in range(KC):
                ins = e.matmul(pp[:, :], lhsT=wt[:, kc, :], rhs=hT[:, kc, 1:G + 1], start=(kc == 0), stop=(kc == KC - 1))
            if halo:
                for kc in range(KC):
                    ins = e.matmul(phh[:, 0:2], lhsT=wt[:, kc, :], rhs=col2(hT, kc, 0, G + 1), start=(kc == 0), stop=(kc == KC - 1))
            return ins
        E("pe", mm, [rwt, rhT], [rpp] + ([rph] if halo else []))
        return pp, rpp, phh, rph

    def shift(pp, rpp, phh, rph, mi, np_=P):
        o, ro = tf.next()
        omm = dc[0:np_, D_OMM + mi:D_OMM + mi + 1]
        hm = dc[0:np_, D_HM + mi:D_HM + mi + 1]
        sp_ = src(pp[0:np_, :])
        E("act", lambda e: e.activation(out=o[0:np_, :], in_=sp_, func=AF.Identity, scale=omm), [rpp, rdc], [ro])
        if bwd:
            a_ = rev(pp[0:np_, 1:G])
            b_ = rev(pp[0:np_, 0:G - 1])
            hf, hl = phh[0:np_, 1:2], phh[0:np_, 0:1]
        else:
            a_ = pp[0:np_, 0:G - 1]
            b_ = pp[0:np_, 1:G]
            hf, hl = phh[0:np_, 0:1], phh[0:np_, 1:2]
        E("dve", lambda e: e.scalar_tensor_tensor(out=o[0:np_, 1:G], in0=a_, scalar=hm, in1=o[0:np_, 1:G], op0=ALU.mult, op1=ALU.add), [rpp, rdc, ro], [ro])
        E("dve", lambda e: e.scalar_tensor_tensor(out=o[0:np_, 0:G - 1], in0=b_, scalar=hm, in1=o[0:np_, 0:G - 1], op0=ALU.mult, op1=ALU.add), [rpp, rdc, ro], [ro])
        E("dve", lambda e: e.scalar_tensor_tensor(out=o[0:np_, 0:1], in0=hf, scalar=hm, in1=o[0:np_, 0:1], op0=ALU.mult, op1=ALU.add), [rph, rdc, ro], [ro])
        E("dve", lambda e: e.scalar_tensor_tensor(out=o[0:np_, G - 1:G], in0=hl, scalar=hm, in1=o[0:np_, G - 1:G], op0=ALU.mult, op1=ALU.add), [rph, rdc, ro], [ro])
        return o, ro

    for g in groups:
        g0 = g * G
        hT, rhT = hTs.next()
        cur_h["hT"] = (hT, rhT)
        lo = max(g0 - 1, 0)
        hi = min(g0 + G + 1, ntot)
        dma(ph, "sp", hT[:, :, lo - (g0 - 1):hi - (g0 - 1)], hT_v[:, :, lo:hi], writes=[rhT])
        if lo > g0 - 1:
            E("pool", lambda e, hT=hT: e.memset(hT[:, :, 0:1], 0.0), [], [rhT])
        if hi < g0 + G + 1:
            E("pool", lambda e, hT=hT: e.memset(hT[:, :, G + 1:G + 2], 0.0), [], [rhT])
        dma(ph, "sp", mkg[:], dr["maskrow"][0, g0:g0 + G].partition_broadcast(P), writes=[rmkg])

        pp, rpp, phh, rph = proj(64, True)
        o, ro = shift(pp, rpp, phh, rph, 24)
        E("act", lambda e, o=o: e.activation(out=th[:, :], in_=o[:, :], func=AF.Tanh), [ro], [rth])
        pp, rpp, phh, rph = proj(65, True)
        o, ro = shift(pp, rpp, phh, rph, 25, np_=64)
        E("act", lambda e, o=o: acopy(e, out=adb[:, :], in_=o[0:64, :]), [ro], [radb])
        if final:
            pp, rpp, phh, rph = proj(66, True)
            o, ro = shift(pp, rpp, phh, rph, 26)
            E("act", lambda e, o=o: e.activation(out=sgd[:, 0, :], in_=o[:, :], func=AF.Sigmoid), [ro], [rsgd])
            pp, rpp, phh, rph = proj(67, True)
            o, ro = shift(pp, rpp, phh, rph, 27, np_=32)
            E("act", lambda e, o=o: e.activation(out=sgd[0:32, 1, :], in_=o[0:32, :], func=AF.Sigmoid), [ro], [rsgd])

        def hg_A(h):
            lb = dc[:, D_LB + h:D_LB + h + 1]
            oml = dc[:, D_OML + h:D_OML + h + 1]
            noml = dc[:, D_NOML + h:D_NOML + h + 1]
            if full:
                pq, rpq, _, _ = proj(h, False)
            pi, rpi, _, _ = proj(8 + h, False)
            pz, rpz, _, _ = proj(16 + dirn * 8 + h, False)
            if final:
                pgt, rpgt, _, _ = proj(32 + h, False)
            sg, rsg = tf.next()
            E("act", lambda e, sg=sg, pz=pz: e.activation(out=sg[:, :], in_=src(pz[:, :]), func=AF.Sigmoid), [rpz], [rsg])
            f_, rf = tf.next()
            E("dve", lambda e, f_=f_, sg=sg, oml=oml, lb=lb: e.tensor_scalar(out=f_[:, :], in0=sg[:, :], scalar1=oml, scalar2=lb, op0=ALU.mult, op1=ALU.add), [rsg, rdc], [rf])
            kk, rkk = tf.next()
            E("dve", lambda e, kk=kk, sg=sg, oml=oml, noml=noml: e.tensor_scalar(out=kk[:, :], in0=sg[:, :], scalar1=noml, scalar2=oml, op0=ALU.mult, op1=ALU.add), [rsg, rdc], [rkk])
            lf, rlf = tf.next()
            E("act", lambda e, lf=lf, f_=f_: e.activation(out=lf[:, :], in_=f_[:, :], func=AF.Ln), [rf], [rlf])
            bcs, rbcs = tf.next()
            E("dve", lambda e, bcs=bcs, lf=lf: e.tensor_tensor_scan(out=bcs[:, :], data0=rmask[:, :], data1=lf[:, :], initial=0.0, op0=ALU.mult, op1=ALU.add), [rlf, rrm], [rbcs])
            eb, reb = tf.next()
            E("act", lambda e, eb=eb, bcs=bcs: e.activation(out=eb[:, :], in_=bcs[:, :], func=AF.Exp), [rbcs], [reb])
            enb, renb = tf.next()
            E("act", lambda e, enb=enb, bcs=bcs: e.activation(out=enb[:, :], in_=bcs[:, :], func=AF.Exp, scale=-1.0), [rbcs], [renb])
            if full:
                QD, rQD = tb_.next()
                E("dve", lambda e, QD=QD, pq=pq, eb=eb: e.tensor_tensor(out=QD[:, :], in0=src(pq[:, :]), in1=eb[:, :], op=ALU.mult), [rpq, reb], [rQD])
                KD, rKD = tb_.next()
                E("pool", lambda e, KD=KD, kk=kk, enb=enb: e.tensor_tensor(out=KD[:, :], in0=kk[:, :], in1=enb[:, :], op=ALU.mult), [rkk, renb], [rKD])
            khf, rkhf = tf.next()
            E("dve", lambda e, khf=khf, kk=kk, enb=enb: e.tensor_tensor(out=khf[:, :], in0=kk[:, :], in1=enb[:, :], op=ALU.mult), [rkk, renb], [rkhf])
            KH, rKH = tb_.next()
            ebl = bcast_last(eb[:, :].rearrange("p (n c) -> p n c", c=64)[:, :, 63:64], 64)
            E("dve", lambda e, KH=KH, khf=khf, ebl=ebl: e.tensor_tensor(out=KH[:, :].rearrange("p (n c) -> p n c", c=64), in0=khf[:, :].rearrange("p (n c) -> p n c", c=64),
                                                                          in1=ebl, op=ALU.mult), [rkhf, reb], [rKH])
            VT, rVT = tb_.next()
            E("act", lambda e, VT=VT, pi=pi: acopy(e, out=VT[:, :], in_=src(pi[:, :])), [rpi], [rVT])
            if final:
                GSh, rGSh = tb_.next()
                E("act", lambda e, pgt=pgt, GSh=GSh: e.activation(out=GSh[:, :], in_=src(pgt[:, :]), func=AF.Silu), [rpgt], [rGSh])
            ptb, rptb = TB.next()

            def trh(e, ptb=ptb, VT=VT, KH=KH):
                ins = None
                for b in range(4):
                    e.transpose(ptb[:, b * P:(b + 1) * P], VT[:, b * P:(b + 1) * P], ident[:, :])
                for b in range(4):
                    ins = e.transpose(ptb[:, (4 + b) * P:(5 + b) * P], KH[:, b * P:(b + 1) * P], ident[:, :])
                return ins
            E("pe", trh, [rVT, rKH, rident], [rptb])
            VK, rVK = tb_.next()
            E("dve", lambda e, VK=VK, ptb=ptb: e.tensor_copy(out=VK[:, :], in_=ptb[:, 0:4 * P]), [rptb], [rVK])
            VK2, rVK2 = tb_.next()
            E("dve", lambda e, VK2=VK2, ptb=ptb: e.tensor_copy(out=VK2[:, :], in_=ptb[:, 4 * P:8 * P]), [rptb], [rVK2])
            return dict(locals())

        def hg_B(h, V):
            QD, rQD, KD, rKD = V.get("QD"), V.get("rQD"), V.get("KD"), V.get("rKD")
            VK, rVK, VK2, rVK2 = V["VK"], V["rVK"], V["VK2"], V["rVK2"]
            eb, reb = V["eb"], V["reb"]
            GSh, rGSh = V.get("GSh"), V.get("rGSh")
            if full:
                ys, rys = yst.next()
            for b in range(4):
                bs = slice(b * P, (b + 1) * P)
                if full:
                    pat, rpat = FB.next()
                    E("pe", lambda e, pat=pat, KD=KD, QD=QD, bs=bs: e.matmul(pat[:, 0:P], lhsT=KD[:, bs], rhs=QD[:, bs], start=True, stop=True), [rKD, rQD], [rpat])
                    ATT, rATT = tb_.next()
                    E("dve", lambda e, ATT=ATT, pat=pat: e.tensor_tensor(out=ATT[:, 0:P], in0=pat[:, 0:P], in1=trim[:, :], op=ALU.mult), [rpat, rtrim], [rATT])
                    po, rpo = FB.next()
                    E("pe", lambda e, po=po, VK=VK, ATT=ATT, b=b: e.matmul(po[:, 0:P], lhsT=VK[:, b * P:(b + 1) * P], rhs=ATT[:, 0:P], start=True, stop=False), [rVK, rATT], [rpo])
                for c in range(2):
                    cp = slice(64 * c, 64 * c + 64)
                    cur = sbp[h]
                    if full:
                        E("pe", lambda e, po=po, cur=cur, h=h, QD=QD, b=b, c=c: e.matmul(po[:, 64 * c:64 * c + 64], lhsT=Sbf[cur][:, h, :], rhs=QD[:, b * P + 64 * c:b * P + 64 * c + 64],
                                                                                         start=False, stop=(c == 1)), [rSbf[cur][h], rQD], [rpo])
                    pu, rpu = FB.next()
                    E("pe", lambda e, pu=pu, VK=VK, VK2=VK2, b=b, cp=cp: e.matmul(pu[:, 0:P], lhsT=VK2[cp, b * P:(b + 1) * P], rhs=VK[cp, b * P:(b + 1) * P], start=True, stop=True),
                      [rVK, rVK2], [rpu])
                    dcol = eb[:, b * P + 64 * c + 63:b * P + 64 * c + 64]
                    E("dve", lambda e, pu=pu, h=h, dcol=dcol: e.scalar_tensor_tensor(out=S32[:, h, :], in0=S32[:, h, :], scalar=dcol, in1=pu[:, 0:P], op0=ALU.mult, op1=ALU.add),
                      [rS32[h], reb, rpu], [rS32[h]])
                    nxt = 1 - cur
                    E("act", lambda e, nxt=nxt, h=h: acopy(e, out=Sbf[nxt][:, h, :], in_=S32[:, h, :]), [rS32[h]], [rSbf[nxt][h]])
                    sbp[h] = nxt
                if full:
                    if final:
                        pt_, rpt_ = parts.next()
                        dma(ph, "sp", pt_[:, 0:P], dr["ya_part"][h][:, g0 + (3 - b) * P:g0 + (4 - b) * P] if bwd else dr["ya_part"][h][:, g0 + b * P:g0 + (b + 1) * P], writes=[rpt_])
                        E("dve", lambda e, ys=ys, po=po, pt_=pt_, bs=bs: e.tensor_tensor(out=ys[:, bs], in0=po[:, 0:P], in1=src(pt_[:, 0:P]), op=ALU.add), [rpo, rpt_], [rys])
                    else:
                        E("act", lambda e, ys=ys, po=po, bs=bs: acopy(e, out=ys[:, bs], in_=po[:, 0:P]), [rpo], [rys])
            if full and not final:
                dma(ph, "sp", dr["ya_part"][h][:, g0:g0 + G], src(ys[:, :]) if bwd else ys[:, :], reads=[rys])
            if final:
                sq, rsq = tb_.next()
                E("act", lambda e, sq=sq, ys=ys: e.activation(out=sq[:, :], in_=ys[:, :], func=AF.Square), [rys], [rsq])
                pm, rpm = FB.next()
                E("pe", lambda e, pm=pm, sq=sq: e.matmul(pm[:, :], lhsT=ones_b[:, :], rhs=sq[:, :], start=True, stop=True), [rones, rsq], [rpm])
                vr, rvr = tf.next()
                E("dve", lambda e, vr=vr, pm=pm: e.tensor_scalar(out=vr[:, :], in0=pm[:, :], scalar1=1.0 / 128, scalar2=1e-6, op0=ALU.mult, op1=ALU.add), [rpm], [rvr])
                E("act", lambda e, vr=vr: e.activation(out=vr[:, :], in_=vr[:, :], func=AF.Sqrt), [rvr], [rvr])
                E("dve", lambda e, vr=vr: e.reciprocal(out=vr[:, :], in_=vr[:, :]), [rvr], [rvr])
                on = cst[:, C_ONORM + h:C_ONORM + h + 1]
                t3, rt3 = tf.next()
                E("dve", lambda e, t3=t3, ys=ys, on=on, vr=vr: e.scalar_tensor_tensor(out=t3[:, :], in0=ys[:, :], scalar=on, in1=vr[:, :], op0=ALU.mult, op1=ALU.mult), [rys, rcst, rvr], [rt3])
                yaT, ryaT = yaTs.next()
                E("pool", lambda e, t3=t3, yaT=yaT, GSh=GSh: e.tensor_tensor(out=src(yaT[:, :]), in0=t3[:, :], in1=GSh[:, :], op=ALU.mult), [rt3, rGSh], [ryaT])
                dma(ph, "sp", dr["yaT"][h][:, g0:g0 + G], yaT[:, :], reads=[ryaT])


        nh = 0 if dr.get("skip_hgrn") else 8
        VA = {}
        if nh:
            VA[0] = hg_A(0)
        for h in range(nh):
            if h + 1 < nh:
                VA[h + 1] = hg_A(h + 1)
            hg_B(h, VA.pop(h))

        for c in range(0 if dr.get("skip_rwkv") else 8):
            if full:
                pr, rpr, phr, rphr = proj(40 + c, True)
                rs_, rrs = shift(pr, rpr, phr, rphr, c)
            pk, rpk, phk, rphk = proj(48 + c, True)
            ks_, rks = shift(pk, rpk, phk, rphk, 8 + c)
            pv, rpv, phv, rphv = proj(56 + c, True)
            vs_, rvs = shift(pv, rpv, phv, rphv, 16 + c)
            cs128 = slice(c * P, (c + 1) * P)
            pzw, rpzw = FB.next()
            d0 = dirn * 64
            E("pe", lambda e, pzw=pzw, cs128=cs128: e.matmul(pzw[:, :], lhsT=w2sb[d0:d0 + 64, cs128], rhs=th[d0:d0 + 64, :], start=True, stop=True), [rw2, rth], [rpzw])
            sgw, rsgw = tf.next()
            E("act", lambda e, sgw=sgw, pzw=pzw, c=c: e.activation(out=sgw[:, :], in_=pzw[:, :], func=AF.Sigmoid, bias=cst[:, C_W0 + dirn * 8 + c:C_W0 + dirn * 8 + c + 1]), [rpzw, rcst], [rsgw])
            css, rcss = tf.next()
            E("dve", lambda e, css=css, sgw=sgw: e.tensor_tensor_scan(out=css[:, :], data0=rmask[:, :], data1=sgw[:, :], initial=0.0, op0=ALU.mult, op1=ALU.add), [rsgw, rrm], [rcss])
            ec, rec = tf.next()
            E("act", lambda e, ec=ec, css=css: e.activation(out=ec[:, :], in_=css[:, :], func=AF.Exp, scale=-K0), [rcss], [rec])
            enc, renc = tf.next()
            E("act", lambda e, enc=enc, css=css: e.activation(out=enc[:, :], in_=css[:, :], func=AF.Exp, scale=K0), [rcss], [renc])
            dlt, rdlt = tf.next()
            E("dve", lambda e, dlt=dlt, css=css, sgw=sgw: e.tensor_tensor(out=dlt[:, :], in0=css[:, :], in1=sgw[:, :], op=ALU.subtract), [rcss, rsgw], [rdlt])
            E("act", lambda e, dlt=dlt: e.activation(out=dlt[:, :], in_=dlt[:, :], func=AF.Exp, scale=-K0), [rdlt], [rdlt])
            pza, rpza = FB.next()
            E("pe", lambda e, pza=pza, cs128=cs128: e.matmul(pza[:, :], lhsT=a2sb[0:64, cs128], rhs=adb[0:64, :], start=True, stop=True), [ra2, radb], [rpza])
            asg, rasg = tf.next()
            E("act", lambda e, asg=asg, pza=pza, c=c: e.activation(out=asg[:, :], in_=pza[:, :], func=AF.Sigmoid, bias=cst[:, C_A0 + c:C_A0 + c + 1]), [rpza, rcst], [rasg])
            kk0, rkk0 = tf.next()
            E("act", lambda e, kk0=kk0, ks_=ks_, c=c: e.activation(out=kk0[:, :], in_=ks_[:, :], func=AF.Identity, scale=cst[:, C_KK + c:C_KK + c + 1]), [rks, rcst], [rkk0])
            sq, rsq = tb_.next()
            E("act", lambda e, sq=sq, kk0=kk0: e.activation(out=sq[:, :], in_=kk0[:, :], func=AF.Square), [rkk0], [rsq])
            pss, rpss = FB.next()
            E("pe", lambda e, pss=pss, sq=sq: e.matmul(pss[:, :], lhsT=blk64[:, :], rhs=sq[:, :], start=True, stop=True), [rblk64, rsq], [rpss])
            rn, rrn = tf.next()
            E("act", lambda e, rn=rn, pss=pss: e.activation(out=rn[:, :], in_=pss[:, :], func=AF.Sqrt), [rpss], [rrn])
            E("dve", lambda e, rn=rn: e.tensor_scalar(out=rn[:, :], in0=rn[:, :], scalar1=1e-12, scalar2=None, op0=ALU.max), [rrn], [rrn])
            E("dve", lambda e, rn=rn: e.reciprocal(out=rn[:, :], in_=rn[:, :]), [rrn], [rrn])
            E("dve", lambda e, kk0=kk0, rn=rn: e.tensor_tensor(out=kk0[:, :], in0=kk0[:, :], in1=rn[:, :], op=ALU.mult), [rkk0, rrn], [rkk0])
            km, rkm = tf.next()
            E("dve", lambda e, km=km, asg=asg, c=c: e.tensor_scalar(out=km[:, :], in0=asg[:, :], scalar1=-1.0, scalar2=cst[:, C_KA + c:C_KA + c + 1], op0=ALU.add, op1=ALU.mult), [rasg, rcst], [rkm])
            E("dve", lambda e, km=km, ks_=ks_: e.scalar_tensor_tensor(out=km[:, :], in0=km[:, :], scalar=1.0, in1=ks_[:, :], op0=ALU.add, op1=ALU.mult), [rkm, rks], [rkm])
            AR4 = AR[:, :, 0, :]
            E("dve", lambda e, kk0=kk0, dlt=dlt: e.scalar_tensor_tensor(out=AR[:, :, 0, :], in0=kk0[:, :].rearrange("p (n c) -> p n c", c=64), scalar=-1.0,
                                                                          in1=dlt[:, :].rearrange("p (n c) -> p n c", c=64), op0=ALU.mult, op1=ALU.mult), [rkk0, rdlt], [rAR])
            if full:
                E("dve", lambda e, rs_=rs_, ec=ec: e.tensor_tensor(out=AR[:, :, 1, :], in0=rs_[:, :].rearrange("p (n c) -> p n c", c=64),
                                                                    in1=ec[:, :].rearrange("p (n c) -> p n c", c=64), op=ALU.mult), [rrs, rec], [rAR])
            bv, rbv = tf.next()
            E("dve", lambda e, bv=bv, kk0=kk0, asg=asg: e.tensor_tensor(out=bv[:, :], in0=kk0[:, :], in1=asg[:, :], op=ALU.mult), [rkk0, rasg], [rbv])
            BT, rBT = tb_.next()
            E("dve", lambda e, BT=BT, bv=bv, enc=enc: e.tensor_tensor(out=BT[:, :], in0=bv[:, :], in1=enc[:, :], op=ALU.mult), [rbv, renc], [rBT])
            KT, rKT = tb_.next()
            E("dve", lambda e, KT=KT, km=km, enc=enc: e.tensor_tensor(out=KT[:, :], in0=km[:, :], in1=enc[:, :], op=ALU.mult), [rkm, renc], [rKT])
            ecl = bcast_last(ec[:, :].rearrange("p (n c) -> p n c", c=64)[:, :, 63:64], 64)
            BH, rBH = tb_.next()
            E("dve", lambda e, BH=BH, BT=BT, ecl=ecl: e.tensor_tensor(out=BH[:, :].rearrange("p (n c) -> p n c", c=64), in0=BT[:, :].rearrange("p (n c) -> p n c", c=64), in1=ecl, op=ALU.mult), [rBT, rec], [rBH])
            KH_, rKH_ = tb_.next()
            E("dve", lambda e, KH_=KH_, KT=KT, ecl=ecl: e.tensor_tensor(out=KH_[:, :].rearrange("p (n c) -> p n c", c=64), in0=KT[:, :].rearrange("p (n c) -> p n c", c=64), in1=ecl, op=ALU.mult), [rKT, rec], [rKH_])
            VTr, rVTr = tb_.next()
            E("dve", lambda e, VTr=VTr, vs_=vs_: e.tensor_tensor(out=VTr[:, :], in0=vs_[:, :], in1=src(mkg[:, :]), op=ALU.mult), [rvs, rmkg], [rVTr])
            tap("AR", AR[:].rearrange("p n a c -> p (n a c)"), rAR)
            tap("BT", BT[:, :], rBT)
            tap("KT", KT[:, :], rKT)
            tap("ec", ec[:, :], rec)
            tap("enc", enc[:, :], renc)
            tap("kkn", kk0[:, :], rkk0)
            tap("asg", asg[:, :], rasg)
            tap("ks", ks_[:, :], rks)
            E("act", lambda e, ec=ec: acopy(e, out=PCt[:, :], in_=ec[:, 63:G:64]), [rec], [rPCt])
            E("act", lambda e: acopy(e, out=PC[:, 0, :], in_=PCt[0:64, :]), [rPCt], [rPC])
            dma(ph, "sp", PC[:, 1, :], PCt[64:128, :], reads=[rPCt], writes=[rPC])
            if full:
                E("act", lambda e: acopy(e, out=RT0[:, 0, :].rearrange("p (n c) -> p n c", c=64), in_=AR[0:64, :, 1, :]), [rAR], [rRT0])
                dma(ph, "sp", RT0[:, 1, :].rearrange("p (n c) -> p n c", c=64), AR[64:128, :, 1, :], reads=[rAR], writes=[rRT0])
            if final:
                tq, rtq = tb_.next()
                E("dve", lambda e, tq=tq, rs_=rs_, km=km, c=c: e.scalar_tensor_tensor(out=tq[:, :], in0=rs_[:, :], scalar=cst[:, C_RK + c:C_RK + c + 1], in1=km[:, :], op0=ALU.mult, op1=ALU.mult), [rrs, rkm, rcst], [rtq])
                pbn, rpbn = FB.next()
                E("pe", lambda e, pbn=pbn, tq=tq: e.matmul(pbn[:, :], lhsT=blk64[:, :], rhs=tq[:, :], start=True, stop=True), [rblk64, rtq], [rpbn])
                bon, rbon = tb_.next()
                E("dve", lambda e, bon=bon, pbn=pbn, vs_=vs_: e.tensor_tensor(out=bon[:, :], in0=pbn[:, :], in1=vs_[:, :], op=ALU.mult), [rpbn, rvs], [rbon])
            for nm, srcT, rsrc in (("ATM", None, rAR), ("BHM", BH, rBH), ("KHM", KH_, rKH_), ("VTM", VTr, rVTr)):
                ptb, rptb = TB.next()

                def trr(e, ptb=ptb, srcT=srcT):
                    ins = None
                    for n in range(8):
                        in_ = AR[:, n, 0, :] if srcT is None else srcT[:, n * 64:(n + 1) * 64]
                        ins = e.transpose(ptb[0:64, n * P:(n + 1) * P], in_, ident[:, :])
                    return ins
                E("pe", trr, [rsrc, rident], [rptb])
                dst, rdst = tmq[nm]
                E("act" if nm in ("ATM", "KHM") else "dve",
                  (lambda e, dst=dst, ptb=ptb: acopy(e, out=dst[:, :, :], in_=ptb[0:64, :].rearrange("p (n c) -> p n c", c=P))) if nm in ("ATM", "KHM") else
                  (lambda e, dst=dst, ptb=ptb: e.tensor_copy(out=dst[:, :, :], in_=ptb[0:64, :].rearrange("p (n c) -> p n c", c=P))),
                  [rptb], [rdst])
            ATM, rATM = tmq["ATM"]
            BHM, rBHM = tmq["BHM"]
            KHM, rKHM = tmq["KHM"]
            VTM, rVTM = tmq["VTM"]
            if full:
                dma(ph, "sp", AR1[:, :, :, :], AR[64:128, :, :, :], reads=[rAR], writes=[rAR1])
            else:
                dma(ph, "sp", AR1[:, :, 0, :], AR[64:128, :, 0, :], reads=[rAR], writes=[rAR1])
            dma(ph, "sp", BK1[:, 0, :], BT[64:128, :], reads=[rBT], writes=[rBK1])
            dma(ph, "sp", BK1[:, 1, :], KT[64:128, :], reads=[rKT], writes=[rBK1])
            BT3 = [BT[0:64, :].rearrange("p (n c) -> p n c", c=64), BK1[:, 0, :].rearrange("p (n c) -> p n c", c=64)]
            KT3 = [KT[0:64, :].rearrange("p (n c) -> p n c", c=64), BK1[:, 1, :].rearrange("p (n c) -> p n c", c=64)]
            ARh = [AR[0:64, :, :, :], AR1[:, :, :, :]]
            ncol = P if full else 64

            def hs_(hh):
                return slice(hh * 64, hh * 64 + 64)
            for (lhs3, rl, Mdst, rMd) in ((BT3, rBT, M1, rM1), (KT3, rKT, M2, rM2)):
                rl = [rl, rBK1, rAR1]
                for q4 in range(4):
                    pm_, rpm_ = FB.next()

                    def mmM(e, pm_=pm_, lhs3=lhs3, q4=q4):
                        ins = None
                        for j in range(4):
                            b = q4 * 4 + j
                            n, hh = b // 2, b % 2
                            rhs = ARh[hh][:, n, :, :] if full else ARh[hh][:, n, 0, :]
                            ins = e.matmul(pm_[0:64, j * P:j * P + ncol], lhsT=lhs3[hh][:, n, :], rhs=rhs, start=True, stop=True)
                        return ins
                    E("pe", mmM, rl + [rAR], [rpm_])
                    E("dve", lambda e, pm_=pm_, Mdst=Mdst, q4=q4: e.tensor_tensor(out=Mdst[:, q4 * 4:q4 * 4 + 4, 0:ncol], in0=pm_[0:64, :].rearrange("p (j c) -> p j c", c=P)[:, :, 0:ncol],
                                                                                   in1=mskM[:, :].rearrange("p (j c) -> p j c", c=P)[:, :, 0:ncol], op=ALU.mult), [rpm_, rmskM], [rMd])
            for q8 in range(2):
                px, rpx = FB.next()

                def mmX(e, px=px, q8=q8):
                    ins = None
                    for j in range(8):
                        b = q8 * 8 + j
                        n, hh = b // 2, b % 2
                        ins = e.matmul(px[0:64, j * 64:(j + 1) * 64], lhsT=ARh[hh][:, n, 0, :], rhs=BT3[hh][:, n, :], start=True, stop=True)
                    return ins
                E("pe", mmX, [rAR, rBT, rAR1, rBK1], [rpx])
                E("dve", lambda e, px=px, q8=q8: e.tensor_tensor(out=Xs[0][:, q8 * 8:q8 * 8 + 8, :], in0=px[0:64, :].rearrange("p (j c) -> p j c", c=64),
                                                                   in1=mskX[:, :].rearrange("p (j c) -> p j c", c=64), op=ALU.mult), [rpx, rmskX], [rXs[0]])
            E("pool", lambda e: e.tensor_copy(out=XTs[0][:, :, :], in_=M1[:, :, 0:64]), [rM1], [rXTs[0]])
            E("pool", lambda e: e.tensor_copy(out=Z[:, :, 0:64].rearrange("p (n h) c -> p n h c", h=2), in_=ATM[:, :, :].rearrange("p n (h c) -> p n h c", h=2)), [rATM], [rZ])
            for q8 in range(2):
                pz0, rpz0 = FB.next()

                def mmZ0(e, pz0=pz0, q8=q8):
                    ins = None
                    for j in range(8):
                        b = q8 * 8 + j
                        n, hh = b // 2, b % 2
                        ins = e.matmul(pz0[0:64, j * 64:(j + 1) * 64], lhsT=M2[:, b, 0:64], rhs=VTM[:, n, hs_(hh)], start=True, stop=True)
                    return ins
                E("pe", mmZ0, [rM2, rVTM], [rpz0])
                E("act", lambda e, pz0=pz0, q8=q8: acopy(e, out=Z[:, q8 * 8:q8 * 8 + 8, 64:P], in_=pz0[0:64, :].rearrange("p (j c) -> p j c", c=64)), [rpz0], [rZ])
            tap("M1", M1[:].rearrange("p b c -> p (b c)"), rM1)
            tap("M2", M2[:].rearrange("p b c -> p (b c)"), rM2)
            tap("X0", Xs[0][:].rearrange("p b c -> p (b c)"), rXs[0])
            tap("Z0", Z[:].rearrange("p b c -> p (b c)"), rZ)
            cur = 0
            for it in range(6):
                if it < 5:
                    nxt = 1 - cur
                    for q8 in range(2):
                        p1_, rp1_ = FB.next()
                        p2_, rp2_ = FB.next()

                        def mmS(e, p1_=p1_, p2_=p2_, q8=q8, cur=cur):
                            ins = None
                            for j in range(8):
                                b = q8 * 8 + j
                                e.matmul(p1_[0:64, j * 64:(j + 1) * 64], lhsT=Xs[cur][:, b, :], rhs=XTs[cur][:, b, :], start=True, stop=True)
                                ins = e.matmul(p2_[0:64, j * 64:(j + 1) * 64], lhsT=XTs[cur][:, b, :], rhs=Xs[cur][:, b, :], start=True, stop=True)
                            return ins
                        E("pe", mmS, [rXs[cur], rXTs[cur]], [rp1_, rp2_])
                        E("act", lambda e, p1_=p1_, q8=q8, nxt=nxt: acopy(e, out=XTs[nxt][:, q8 * 8:q8 * 8 + 8, :], in_=p1_[0:64, :].rearrange("p (j c) -> p j c", c=64)), [rp1_], [rXTs[nxt]])
                        E("act", lambda e, p2_=p2_, q8=q8, nxt=nxt: acopy(e, out=Xs[nxt][:, q8 * 8:q8 * 8 + 8, :], in_=p2_[0:64, :].rearrange("p (j c) -> p j c", c=64)), [rp2_], [rXs[nxt]])
                for q4 in range(4):
                    pa_, rpa_ = FB.next()

                    def mmA(e, pa_=pa_, q4=q4, cur=cur):
                        ins = None
                        for j in range(4):
                            b = q4 * 4 + j
                            ins = e.matmul(pa_[0:64, j * P:(j + 1) * P], lhsT=XTs[cur][:, b, :], rhs=Z[:, b, :], start=True, stop=True)
                        return ins
                    E("pe", mmA, [rXTs[cur], rZ], [rpa_])
                    E("dve", lambda e, pa_=pa_, q4=q4: e.tensor_tensor(out=Z[:, q4 * 4:q4 * 4 + 4, :], in0=pa_[0:64, :].rearrange("p (j c) -> p j c", c=P),
                                                                        in1=Z[:, q4 * 4:q4 * 4 + 4, :], op=ALU.add), [rpa_, rZ], [rZ])
                if it < 5:
                    cur = nxt
            if full:
                for q8 in range(2):
                    pq_, rpq_ = FB.next()

                    def mmQ(e, pq_=pq_, q8=q8):
                        ins = None
                        for j in range(8):
                            b = q8 * 8 + j
                            ins = e.matmul(pq_[0:64, j * 64:(j + 1) * 64], lhsT=Z[:, b, 0:64], rhs=M1[:, b, 64:P], start=True, stop=True)
                        return ins
                    E("pe", mmQ, [rZ, rM1], [rpq_])
                    E("dve", lambda e, pq_=pq_, q8=q8: e.tensor_tensor(out=QE[:, q8 * 8:q8 * 8 + 8, :].rearrange("p (n h) c -> p n h c", h=2),
                                                                        in0=pq_[0:64, :].rearrange("p (n h c) -> p n h c", h=2, c=64),
                                                                        in1=RT0[:, :, q8 * 256:(q8 + 1) * 256].rearrange("p h (n c) -> p n h c", c=64), op=ALU.add), [rpq_, rRT0], [rQE])
            for q8 in range(2):
                pg_, rpg_ = FB.next()
                ph2, rph2 = FB.next()

                def mmG(e, pg_=pg_, ph2=ph2, q8=q8):
                    ins = None
                    for j in range(8):
                        b = q8 * 8 + j
                        n, hh = b // 2, b % 2
                        e.matmul(pg_[0:64, j * 64:(j + 1) * 64], lhsT=Z[:, b, 0:64], rhs=BHM[:, n, hs_(hh)], start=True, stop=True)
                        e.matmul(ph2[0:64, j * 64:(j + 1) * 64], lhsT=BHM[:, n, hs_(hh)], rhs=Z[:, b, 64:P], start=True, stop=False)
                        ins = e.matmul(ph2[0:64, j * 64:(j + 1) * 64], lhsT=KHM[:, n, hs_(hh)], rhs=VTM[:, n, hs_(hh)], start=False, stop=True)
                    return ins
                E("pe", mmG, [rZ, rBHM, rKHM, rVTM], [rpg_, rph2])
                E("act", lambda e, pg_=pg_, q8=q8: acopy(e, out=GT[:, q8 * 8:q8 * 8 + 8, :], in_=pg_[0:64, :].rearrange("p (j c) -> p j c", c=64)), [rpg_], [rGT])
                E("dve", lambda e, ph2=ph2, q8=q8: e.tensor_copy(out=HI[:, q8 * 8:q8 * 8 + 8, :], in_=ph2[0:64, :].rearrange("p (j c) -> p j c", c=64)), [rph2], [rHI])
            if full:
                pys = PY
            for n in range(8):
                for hh in range(2):
                    b = n * 2 + hh
                    h = 2 * c + hh
                    curh = hbp[h]
                    if full:
                        py, rpy = pys[b // 8]
                        j = b % 8

                        def mmY(e, py=py, j=j, b=b, n=n, hh=hh, h=h, curh=curh):
                            e.matmul(py[0:64, j * 64:(j + 1) * 64], lhsT=Z[:, b, 64:P], rhs=M1[:, b, 64:P], start=True, stop=False)
                            e.matmul(py[0:64, j * 64:(j + 1) * 64], lhsT=VTM[:, n, hs_(hh)], rhs=M2[:, b, 64:P], start=False, stop=False)
                            return e.matmul(py[0:64, j * 64:(j + 1) * 64], lhsT=Hbf[curh][:, h, :], rhs=QE[:, b, :], start=False, stop=True)
                        E("pe", mmY, [rZ, rM1, rM2, rVTM, rQE, rHbf[curh][h]], [rpy])
                    pn_, rpn_ = FB.next()

                    def mmH(e, pn_=pn_, b=b, h=h, curh=curh):
                        e.matmul(pn_[0:64, 0:64], lhsT=GT[:, b, :], rhs=Hbf[curh][:, h, :], start=True, stop=False)
                        return e.matmul(pn_[0:64, 0:64], lhsT=ident[0:64, 0:64], rhs=HI[:, b, :], start=False, stop=True)
                    E("pe", mmH, [rGT, rHI, rident, rHbf[curh][h]], [rpn_])
                    E("dve", lambda e, pn_=pn_, h=h, hh=hh, n=n: e.scalar_tensor_tensor(out=H32[:, h, :], in0=H32[:, h, :], scalar=PC[:, hh, n:n + 1], in1=pn_[0:64, 0:64], op0=ALU.mult, op1=ALU.add),
                      [rH32[h], rPC, rpn_], [rH32[h]])
                    nxh = 1 - curh
                    E("act", lambda e, nxh=nxh, h=h: acopy(e, out=Hbf[nxh][:, h, :], in_=H32[:, h, :]), [rH32[h]], [rHbf[nxh][h]])
                    hbp[h] = nxh
            if full:
                ysr, rysr = ysrs.next()

                def ysr_view(hh, q8, ysr=ysr):
                    return ysr[:, hh, q8 * 256:(q8 + 1) * 256].rearrange("p (n c) -> p n c", c=64)
                for q8 in range(2):
                    py, rpy = pys[q8]
                    for hh in range(2):
                        pyv = py[0:64, :].rearrange("p (n h c) -> p n h c", h=2, c=64)[:, :, hh, :]
                        if final:
                            pbuf, rpbuf = parts.next()
                            lo = g0 + (1 - q8) * 256 if bwd else g0 + q8 * 256
                            dma(ph, "sp", pbuf[0:64, 0:256], dr["yb_part"][2 * c + hh][:, lo:lo + 256], writes=[rpbuf])
                            E("dve", lambda e, pyv=pyv, pbuf=pbuf, hh=hh, q8=q8, ysr_view=ysr_view: e.tensor_tensor(
                                out=ysr_view(hh, q8), in0=pyv, in1=src(pbuf[0:64, 0:256]).rearrange("p (n c) -> p n c", c=64), op=ALU.add), [rpy, rpbuf], [rysr])
                        else:
                            E("act", lambda e, pyv=pyv, hh=hh, q8=q8, ysr_view=ysr_view: acopy(e, out=ysr_view(hh, q8), in_=pyv), [rpy], [rysr])
                if not final:
                    for hh in range(2):
                        dma(ph, "sp", dr["yb_part"][2 * c + hh][:, g0:g0 + G], src(ysr[:, hh, :]) if bwd else ysr[:, hh, :], reads=[rysr])
                if final:
                    dma(ph, "sp", bon1[:, :], bon[64:128, :], reads=[rbon], writes=[rbon1])
                    ybT, rybT = ybTs.next()
                    for hh in range(2):
                        h = 2 * c + hh
                        yv = ysr[:, hh, :]
                        ybf, rybf = tb_.next()
                        E("act", lambda e, ybf=ybf, yv=yv: acopy(e, out=ybf[0:64, :], in_=yv), [rysr], [rybf])
                        pm, rpm = FB.next()
                        E("pe", lambda e, pm=pm, ybf=ybf: e.matmul(pm[0:64, :], lhsT=ones64[:, :], rhs=ybf[0:64, :], start=True, stop=True), [rones64, rybf], [rpm])
                        dd, rdd = tf.next()
                        E("dve", lambda e, dd=dd, yv=yv, pm=pm: e.tensor_tensor(out=dd[0:64, :], in0=yv, in1=pm[0:64, :], op=ALU.subtract), [rysr, rpm], [rdd])
                        dsq, rdsq = tb_.next()
                        E("act", lambda e, dsq=dsq, dd=dd: e.activation(out=dsq[0:64, :], in_=dd[0:64, :], func=AF.Square), [rdd], [rdsq])
                        pv2, rpv2 = FB.next()
                        E("pe", lambda e, pv2=pv2, dsq=dsq: e.matmul(pv2[0:64, :], lhsT=ones64[:, :], rhs=dsq[0:64, :], start=True, stop=True), [rones64, rdsq], [rpv2])
                        vr, rvr = tf.next()
                        E("dve", lambda e, vr=vr, pv2=pv2: e.tensor_scalar(out=vr[0:64, :], in0=pv2[0:64, :], scalar1=64e-5, scalar2=None, op0=ALU.add), [rpv2], [rvr])
                        E("act", lambda e, vr=vr: e.activation(out=vr[0:64, :], in_=vr[0:64, :], func=AF.Sqrt), [rvr], [rvr])
                        E("dve", lambda e, vr=vr: e.reciprocal(out=vr[0:64, :], in_=vr[0:64, :]), [rvr], [rvr])
                        E("dve", lambda e, dd=dd, vr=vr: e.tensor_tensor(out=dd[0:64, :], in0=dd[0:64, :], in1=vr[0:64, :], op=ALU.mult), [rdd, rvr], [rdd])
                        E("act", lambda e, dd=dd, h=h: e.activation(out=dd[0:64, :], in_=dd[0:64, :], func=AF.Identity, scale=lnc[:, h:h + 1], bias=lnc[:, 16 + h:17 + h]), [rdd, rlnc], [rdd])
                        if hh == 0:
                            E("dve", lambda e, dd=dd, bon=bon: e.tensor_tensor(out=dd[0:64, :], in0=dd[0:64, :], in1=bon[0:64, :], op=ALU.add), [rdd, rbon], [rdd])
                        else:
                            E("dve", lambda e, dd=dd: e.tensor_tensor(out=dd[0:64, :], in0=dd[0:64, :], in1=bon1[:, :], op=ALU.add), [rdd, rbon1], [rdd])
                        pgg, rpgg = FB.next()

                        def mmg(e, pgg=pgg, h=h):
                            e.matmul(pgg[0:64, :], lhsT=g2a[:, h * 64:(h + 1) * 64], rhs=sgd[:, 0, :], start=True, stop=False)
                            return e.matmul(pgg[0:64, :], lhsT=g2b[0:32, h * 64:(h + 1) * 64], rhs=sgd[0:32, 1, :], start=False, stop=True)
                        E("pe", mmg, [rg2a, rg2b, rsgd], [rpgg])
                        E("dve", lambda e, dd=dd, pgg=pgg, hh=hh, ybT=ybT: e.tensor_tensor(out=src(ybT[:, hh, :]), in0=dd[0:64, :], in1=pgg[0:64, :], op=ALU.mult), [rdd, rpgg], [rybT])
                    dma(ph, "sp", dr["ybT"].rearrange("h p t -> p h t")[:, 2 * c:2 * c + 2, g0:g0 + G], ybT[:, :, :], reads=[rybT])
    if dr.get("st_out") is not None:
        dma(ph, "sp", dr["st_out"][0][:, :], S32[:].rearrange("p h e -> p (h e)"), reads=rS32)
        dma(ph, "sp", dr["st_out"][1][:, :], H32[:].rearrange("p h e -> p (h e)"), reads=rH32)
    ph.close()


def outproj_phase(nc, tag, groups, x_d, maskcol_d, yaT_d, ybT_d, woA_d, woB_d, xout_d):
    ph = Phase(nc, tag)
    S = ph.S
    mc, rmc = load_plain(ph, maskcol_d[:, :], [P, maskcol_d.shape[1]], F32, "maskcol")
    yas = RR(ph, 2, [P, 8, G], BF16, "ya")
    ybs = RR(ph, 2, [64, 16, G], BF16, "yb")
    was = RR(ph, 2, [P, 8, 512], BF16, "woA")
    wbs = RR(ph, 2, [64, 16, 512], BF16, "woB")
    xss = RR(ph, 2, [P, 4, 512], F32, "xs")
    pws = RR(ph, 2, [P, G], F32, "pw", psum=True)
    ya_v = yaT_d.rearrange("h p t -> p h t")
    yb_v = ybT_d.rearrange("h p t -> p h t")
    for g in groups:
        g0 = g * G
        ya, rya = yas.next()
        dma(ph, "sp", ya[:], ya_v[:, :, g0:g0 + G], writes=[rya])
        yb, ryb = ybs.next()
        dma(ph, "sp", yb[:], yb_v[:, :, g0:g0 + G], writes=[ryb])
        for cbk in range(4):
            wa, rwa = was.next()
            dma(ph, "sp", wa[:], woA_d[cbk], writes=[rwa])
            wb, rwb = wbs.next()
            dma(ph, "sp", wb[:], woB_d[cbk], writes=[rwb])
            xs, rxs = xss.next()
            cs = slice(cbk * 512, (cbk + 1) * 512)
            dma(ph, "sp", xs[:], x_d[g0:g0 + G, cs].rearrange("(t p) d -> p t d", p=P), writes=[rxs])
            for t in range(4):
                pw, rpw = pws.next()

                def mmo(e, pw=pw, ya=ya, yb=yb, wa=wa, wb=wb, t=t):
                    for kc in range(8):
                        e.matmul(pw[:, :], lhsT=ya[:, kc, t * P:(t + 1) * P], rhs=wa[:, kc, :], start=(kc == 0), stop=False)
                    ins = None
                    for h in range(16):
                        ins = e.matmul(pw[:, :], lhsT=yb[0:64, h, t * P:(t + 1) * P], rhs=wb[0:64, h, :], start=False, stop=(h == 15))
                    return ins
                S.op("pe", mmo, reads=[rya, ryb, rwa, rwb], writes=[rpw])
                S.op("dve", lambda e, xs=xs, pw=pw, t=t: e.tensor_tensor(out=xs[:, t, :], in0=pw[:, :], in1=xs[:, t, :], op=ALU.add), reads=[rpw, rxs], writes=[rxs])
                S.op("act", lambda e, xs=xs, t=t, g=g: e.activation(out=xs[:, t, :], in_=xs[:, t, :], func=AF.Identity, scale=mc[:, g * 4 + t:g * 4 + t + 1]),
                     reads=[rxs, rmc], writes=[rxs])
            dma(ph, "sp", xout_d[g0:g0 + G, cs].rearrange("(t p) d -> p t d", p=P), xs[:], reads=[rxs])
    ph.close()


def final_phase(nc, tag, groups, xin_d, g_d, y_d):
    ph = Phase(nc, tag)
    S = ph.S
    gbc, rg = load_bcast(ph, g_d, D, "gbc")
    nx = NormCtx(ph, None, None, with_T=False)
    xts = RR(ph, 2, [P, 4, D], F32, "xt")
    for g in groups:
        g0 = g * G
        xt, rxt = xts.next()
        dma(ph, "sp", xt[:], xin_d[g0:g0 + G, :].rearrange("(t p) d -> p t d", p=P), writes=[rxt])
        for t in range(4):
            st, rst = rms_stats(ph, nx, xt[:, t, :], rxt)
            S.op("dve", lambda e, xt=xt, t=t, st=st: e.scalar_tensor_tensor(out=xt[:, t, :], in0=xt[:, t, :], scalar=st[:, 3:4], in1=gbc[:, :],
                                                                            op0=ALU.mult, op1=ALU.mult), reads=[rxt, rst, rg], writes=[rxt])
        dma(ph, "sp", y_d[g0:g0 + G, :].rearrange("(t p) d -> p t d", p=P), xt[:], reads=[rxt])
    ph.close()


def mix_masks():
    u = np.arange(G)
    rmask = np.broadcast_to((u % 64 != 0).astype(np.float32)[None, :], (P, G)).copy()
    j = np.arange(P)[:, None]
    i = np.arange(P)[None, :]
    trimask = ((j // 64 == i // 64) & (j <= i)).astype(np.float32)
    s_ = np.arange(64)[:, None]
    t_ = np.arange(64)[None, :]
    mM = np.concatenate([(s_ < t_), (s_ <= t_)], axis=1).astype(np.float32)
    maskM = np.tile(mM, (1, 4))
    mX = (t_.T > s_.T).astype(np.float32)
    mX = (np.arange(64)[:, None] > np.arange(64)[None, :]).astype(np.float32)
    maskX = np.tile(mX, (1, 8))
    ones = np.ones((P, P), np.float32)
    blk64 = (np.arange(P)[:, None] // 64 == np.arange(P)[None, :] // 64).astype(np.float32)
    shift = np.zeros((P, 64), np.float32)
    shift[64 + np.arange(64), np.arange(64)] = 1.0
    ones64 = np.full((64, 64), 1.0 / 64, np.float32)
    return dict(rmask=rmask, trimask=trimask, maskM=maskM, maskX=maskX, ones=ones, blk64=blk64, shift=shift, ones64=ones64,
                ident=np.eye(P, dtype=np.float32))


def lay_mix(inp, d0, d1):
    w = inp["ab_w_in"][0]
    HWd = 1024
    pa = [w[:, i * HWd:(i + 1) * HWd] for i in range(5)]
    pb = w[:, 5 * HWd:]
    r_, k_, v_ = pb[:, 0:1024], pb[:, 1024:2048], pb[:, 2048:3072]
    wd = [pb[:, 3072:3136], pb[:, 3136:3200]]
    ad = pb[:, 3200:3264]
    gd = pb[:, 3264:3424]
    z64 = np.zeros((D, 64), np.float32)
    cols = [pa[0], pa[1], pa[2 + d0], pa[2 + d1], pa[4], r_, k_, v_,
            np.concatenate([wd[d0], wd[d1]], 1), np.concatenate([ad, z64], 1), gd[:, 0:128],
            np.concatenate([gd[:, 128:160], np.zeros((D, 96), np.float32)], 1)]
    w_in = lay_ws(np.concatenate(cols, axis=1)).reshape(NCH * P, KC * P)
    mu = inp["rwkv_mu"][0]
    z = np.zeros(64, np.float32)
    mus = [mu[0:1024], mu[1024:2048], mu[2048:3072]]
    mul = [np.concatenate([mu[3072 + d0 * 64:3136 + d0 * 64], mu[3072 + d1 * 64:3136 + d1 * 64]]), np.concatenate([mu[3200:3264], z]),
           mu[3264:3392], np.concatenate([mu[3392:3424], np.zeros(96, np.float32)])]
    lb = inp["hgrn_lb"]
    lbt = np.concatenate([lay_cols(lb[r]) for r in range(3)], axis=1)
    mixc = np.concatenate([lbt, lay_cols(inp["hgrn_onorm"][0])] + [lay_cols(m) for m in mus] + [m.reshape(P, 1) for m in mul] +
                          [lay_cols(inp["rwkv_w0"][0][d0]), lay_cols(inp["rwkv_w0"][0][d1]), lay_cols(inp["rwkv_a0"][0]),
                           lay_cols(inp["rwkv_kk"][0]), lay_cols(inp["rwkv_ka"][0]), lay_cols(inp["rwkv_rk"][0].reshape(-1))], axis=1)
    assert mixc.shape == (P, 108), mixc.shape
    w2 = np.concatenate([inp["rwkv_w2"][0][d0], inp["rwkv_w2"][0][d1]], axis=0)
    lnc = np.concatenate([inp["rwkv_ln_w"][0].reshape(16, 64).T, inp["rwkv_ln_b"][0].reshape(16, 64).T], axis=1)
    wo = inp["ab_w_out"][0]
    woA = lay_as(wo[:1024], 512)
    woB = np.ascontiguousarray(wo[1024:].reshape(16, 64, 4, 512).transpose(2, 1, 0, 3))
    return dict(w_in=w_in, mixc=np.ascontiguousarray(mixc), w2=np.ascontiguousarray(w2), a2=np.ascontiguousarray(inp["rwkv_a2"][0]),
                g2=np.ascontiguousarray(inp["rwkv_g2"][0]), lnc=np.ascontiguousarray(lnc),
                woA=woA.reshape(4 * P, 8 * 512), woB=woB.reshape(4 * 64, 16 * 512))


IN_SHAPES = None


def input_shapes(NTOT, L):
    return {
        "xs": [NTOT, D], "maskrow": [1, NTOT], "maskcol": [P, NTOT // P], "cos": [P, L], "sin": [P, L], "kbias": [1, L + 256],
        "mixn0": [1, D], "ffnn0": [1, D], "mixn1": [1, D], "ffnn1": [1, D], "finn": [1, D],
        "w_in": [NCH * P, KC * P], "mixc": [P, 108], "w2": [P, 1024], "a2": [64, 1024], "g2": [160, 1024], "lnc": [64, 32],
        "woA": [4 * P, 8 * 512], "woB": [4 * 64, 16 * 512],
        "rmask": [P, G], "trimask": [P, P], "maskM": [64, 512], "maskX": [64, 512], "ones": [P, P], "blk64": [P, P], "ones64": [64, 64], "ident": [P, P],
        "wqk": [40 * P, KC * P], "wv": [P, KC * 512], "wo": [4 * P, KC * 512], "band": [P, 384], "sink": [1, 16],
        "up0": [2 * FC * P, KC * P], "up1": [2 * FC * P, KC * P], "dn0": [8 * P, FC * 256], "dn1": [8 * P, FC * 256],
        "conv0": [P, 4 * FC], "conv1": [P, 4 * FC],
    }


def build(NTOT, L, NOUT):
    nc = bass.Bass("TRN2", target_bir_lowering=False)
    dr = {k: nc.dram_tensor(k, list(v), F32, kind="ExternalInput").ap() for k, v in input_shapes(NTOT, L).items()}
    y = nc.dram_tensor("y", [NOUT, D], F32, kind="ExternalOutput").ap()

    def scr(name, shape, dt=BF16):
        return nc.dram_tensor(name, list(shape), dt, kind="Internal").ap()
    wb = {k: scr(k + "_b", dr[k].shape) for k in ("w_in", "woA", "woB", "wqk", "wv", "wo", "up0", "up1", "dn0", "dn1")}
    hT0 = scr("hT0", [KC, P, NTOT])
    ya_part = scr("ya_part", [8, P, L], F32)
    yb_part = scr("yb_part", [16, 64, L], F32)
    yaT = scr("yaT", [8, P, L])
    ybT = scr("ybT", [16, 64, L])
    st_hg = scr("st_hg", [P, 1024], F32)
    st_rw = scr("st_rw", [64, 1024], F32)
    x1 = scr("x1", [L, D], F32)
    x2 = scr("x2", [L, D], F32)
    x3 = scr("x3", [L, D], F32)
    x4 = scr("x4", [L, D], F32)
    hT1 = scr("hT1", [KC, P, L])
    hT3 = scr("hT3", [KC, P, L])
    qT = scr("qT", [16, P, L])
    kT = scr("kT", [4, P, L + 256])
    vv = scr("vv", [L + 256, 512])
    NG, NGL, NGO = NTOT // G, L // G, NOUT // G

    cast_phase(nc, "pw", [(wb[k], dr[k]) for k in wb])
    normT_phase(nc, "p0", range(NG), dr["xs"], dr["mixn0"], hT0, dr["ident"])
    d2 = dict(dr)
    d2.update(hT0=hT0, w_in=wb["w_in"].rearrange("(o p) (k c) -> o p k c", p=P, c=P),
              ya_part=[ya_part[h] for h in range(8)], yb_part=[yb_part[h] for h in range(16)], yaT=yaT, ybT=ybT)
    if NG > NGL:
        d1 = dict(d2)
        d1["st_out"] = (st_hg, st_rw)
        mix_phase(nc, "p1", list(range(NGL, NG))[::-1], True, False, False, NTOT, d1)
    mix_phase(nc, "p2", list(range(NGL)), False, True, False, NTOT, d2)
    d3 = dict(d2)
    if NG > NGL:
        d3["st_in"] = (st_hg, st_rw)
    mix_phase(nc, "p3", list(range(NGL))[::-1], True, True, True, NTOT, d3)
    outproj_phase(nc, "p3b", range(NGL), dr["xs"], dr["maskcol"], yaT, ybT,
                  wb["woA"].rearrange("(n p) (k c) -> n p k c", p=P, c=512), wb["woB"].rearrange("(n p) (k c) -> n p k c", p=64, c=512), x1)
    normT_phase(nc, "p3c", range(NGL), x1, dr["ffnn0"], hT1, dr["ident"])
    ffn_phase(nc, "p4", range(NGL), x1, x2, hT1, wb["up0"].rearrange("(o p) (k c) -> o p k c", p=P, c=P),
              wb["dn0"].rearrange("(o p) (k c) -> o p k c", p=P, c=256), dr["conv0"])
    qkv_phase(nc, "p4b", range(NGL), x2, dr["mixn1"], dr["ident"], wb["wqk"].rearrange("(o p) (k c) -> o p k c", p=P, c=P),
              wb["wv"].rearrange("p (k c) -> p k c", c=512), dr["cos"], dr["sin"], qT, kT, vv)
    attn_phase(nc, "p5", L // P, x2, x3, qT, kT, vv, wb["wo"].rearrange("(n p) (k c) -> n p k c", p=P, c=512), dr["band"], dr["kbias"],
               dr["sink"], dr["maskcol"], dr["ident"])
    normT_phase(nc, "p5b", range(NGL), x3, dr["ffnn1"], hT3, dr["ident"])
    ffn_phase(nc, "p6", range(NGO), x3, x4, hT3, wb["up1"].rearrange("(o p) (k c) -> o p k c", p=P, c=P),
              wb["dn1"].rearrange("(o p) (k c) -> o p k c", p=P, c=256), dr["conv1"])
    final_phase(nc, "p7", range(NGO), x4, dr["finn"], y)
    return nc


def host_shared(inp):
    sh = dict(mix_masks())
    sh.pop("shift", None)
    sh["band"] = band_mask()
    sh["sink"] = np.ascontiguousarray(inp["att_sink"][0][None, :])
    for nm, key, i in (("mixn0", "mix_norm", 0), ("ffnn0", "ffn_norm", 0), ("mixn1", "mix_norm", 1), ("ffnn1", "ffn_norm", 1)):
        sh[nm] = np.ascontiguousarray(inp[key][i][None, :])
    sh["finn"] = np.ascontiguousarray(inp["final_norm"][None, :])
    wqk, wv = lay_qk(inp["att_w_qkv"][0])
    sh["wqk"], sh["wv"] = wqk, wv
    sh["wo"] = lay_as(inp["att_w_o"][0], 512).reshape(4 * P, KC * 512)
    for l in range(2):
        sh[f"up{l}"] = lay_up(inp["ffn_w_up"][l])
        sh[f"dn{l}"] = lay_as(inp["ffn_w_down"][l], 256).reshape(8 * P, FC * 256)
    orient = {}
    for flip in (0, 1):
        o = dict(lay_mix(inp, flip, 1 - flip))
        for l in range(2):
            cw = inp["ffn_conv_w"][l]
            o[f"conv{l}"] = lay_conv(cw[::-1] if flip else cw, inp["ffn_conv_b"][l])
        orient[flip] = o
    return sh, orient


def core_inputs(sh, orient, x_local, mask, pos, flip, NTOT, L):
    m = dict(sh)
    m.update(orient[flip])
    m["xs"] = np.ascontiguousarray(x_local, dtype=np.float32)
    m["maskrow"] = np.ascontiguousarray(mask[None, :], dtype=np.float32)
    m["maskcol"] = np.ascontiguousarray(mask.reshape(-1, P).T, dtype=np.float32)
    c, s_ = rope_tables(pos[:L])
    m["cos"], m["sin"] = c, s_
    kb = np.full((1, L + 256), NEG, np.float32)
    kb[0, P:P + L] = np.where(mask[:L] > 0, 0.0, NEG)
    m["kbias"] = kb
    return m


_NC_CACHE = {}


def run_cores(inp_w, jobs, NTOT, L, NOUT):
    sh, orient = host_shared(inp_w)
    in_maps = [core_inputs(sh, orient, x, mk, pos, fl, NTOT, L) for (x, mk, pos, fl) in jobs]
    key = (NTOT, L, NOUT)
    nc = build(NTOT, L, NOUT)
    res = run_bass_kernel_spmd(nc, in_maps, core_ids=list(range(len(jobs))))
    return [r["y"] for r in res.results]


def kernel(x_prompt, x_sample, **w):
    NTOT, L, NOUT = 16384, 8704, 8192
    inp_w = {k: np.asarray(v, dtype=np.float32) for k, v in w.items()}
    x_prompt = np.asarray(x_prompt, dtype=np.float32)
    x_sample = np.asarray(x_sample, dtype=np.float32)
    jobs = []
    ar = np.arange(NTOT)
    ones = np.ones(NTOT, np.float32)
    for b in range(2):
        jobs.append((x_prompt[b], ones, ar, 0))
        jobs.append((x_prompt[b][::-1], ones, NTOT - 1 - ar, 1))
    smask = np.concatenate([np.ones(NOUT, np.float32), np.zeros(NTOT - NOUT, np.float32)])
    for i in range(4):
        xl = np.zeros((NTOT, D), np.float32)
        xl[:NOUT] = x_sample[i]
        jobs.append((xl, smask, ar, 0))
    outs = run_cores(inp_w, jobs, NTOT, L, NOUT)
    y_prompt = np.empty((2, 2 * NOUT, D), np.float32)
    y_sample = np.empty((4, NOUT, D), np.float32)
    for b in range(2):
        y_prompt[b, :NOUT] = outs[2 * b]
        y_prompt[b, NOUT:] = outs[2 * b + 1][::-1]
    for i in range(4):
        y_sample[i] = outs[4 + i]
    return (y_prompt, y_sample)
```

```python
import types
import numpy as np
from contextlib import ExitStack
import concourse.bass as bass
import concourse.mybir as mybir
from concourse.bass_utils import run_bass_kernel_spmd

F32 = mybir.dt.float32
BF16 = mybir.dt.bfloat16
AF = mybir.ActivationFunctionType
ALU = mybir.AluOpType
AX = mybir.AxisListType

P = 128
D = 2048
KC = 16
G = 512
DFF = 5632
FC = 44
HW_ = 1024
NEG = -30000.0


def freeze(fn):
    if fn.__closure__ is None:
        return fn
    cells = []
    for c in fn.__closure__:
        try:
            cells.append(types.CellType(c.cell_contents))
        except ValueError:
            cells.append(c)
    g = types.FunctionType(fn.__code__, fn.__globals__, fn.__name__, fn.__defaults__, tuple(cells))
    g.__kwdefaults__ = fn.__kwdefaults__
    return g


class Res:
    __slots__ = ("name", "last_w", "readers")

    def __init__(self, name):
        self.name = name
        self.last_w = None
        self.readers = []


class Sched:
    ENG = ("pe", "act", "dve", "pool", "sp")
    NDMA = 12

    _pool = {}

    def __init__(self, nc, stack, tag):
        self.nc = nc
        self.tag = tag
        self.ops = {e: [] for e in self.ENG}
        pool = getattr(nc, "_sched_pool", None)
        if pool is None:
            pool = {"sem": {e: nc.alloc_semaphore(name=f"s_{e}") for e in ("pe", "act", "dve", "pool")},
                    "cnt": {e: 0 for e in ("pe", "act", "dve", "pool")},
                    "dsem": {q: [nc.alloc_semaphore(name=f"d_{q}{i}") for i in range(self.NDMA)] for q in ("sp", "pool")},
                    "dcnt": {q: [0] * self.NDMA for q in ("sp", "pool")},
                    "dnext": {"sp": 0, "pool": 0},
                    "waited": {e: {} for e in self.ENG},
                    "semobj": {}}
            nc._sched_pool = pool
        self.sem = pool["sem"]
        self.cnt = pool["cnt"]
        self.dsem = pool["dsem"]
        self.dcnt = pool["dcnt"]
        self.dnext = pool["dnext"]
        self.waited = pool["waited"]
        self.semobj = pool["semobj"]
        self.nres = 0

    def res(self, name=None):
        self.nres += 1
        return Res(name or f"r{self.nres}")

    def _need(self, eng, tok, waits):
        if tok is None:
            return
        key, val = tok
        if eng == "pe" and key == ("c", "pe"):
            return
        if self.waited[eng].get(key, 0) >= val:
            return
        waits[key] = max(waits.get(key, 0), val)

    def op(self, eng, fn, reads=(), writes=(), dma=False):
        import os
        self.nop = getattr(self, "nop", 0) + 1
        if os.environ.get("MAXOPS") and self.tag == os.environ.get("MAXTAG", "p2") and self.nop > int(os.environ["MAXOPS"]):
            return None
        fn = freeze(fn)
        waits = {}
        for r in list(reads) + list(writes):
            self._need(eng, r.last_w, waits)
        for w in writes:
            for tok in w.readers:
                self._need(eng, tok, waits)
        if dma:
            q = eng
            i = self.dnext[q]
            self.dnext[q] = (i + 1) % self.NDMA
            key = ("d", q, i)
            if self.dcnt[q][i] > 0:
                self._need(eng, (key, self.dcnt[q][i]), waits)
            self.dcnt[q][i] += 16
            tok = (key, self.dcnt[q][i])
            semo = self.dsem[q][i]
            inc = 16
        else:
            self.cnt[eng] += 1
            key = ("c", eng)
            tok = (key, self.cnt[eng])
            semo = self.sem[eng]
            inc = 1
        self.semobj[key] = semo
        for k, v in waits.items():
            self.waited[eng][k] = max(self.waited[eng].get(k, 0), v)
        self.ops[eng].append(([(self.semobj[k], v) for k, v in waits.items()], fn, semo, inc))
        for w in writes:
            w.last_w = tok
            w.readers = []
        for r in reads:
            r.readers.append(tok)
        return tok

    def emit(self, block):
        def run(engobj, lst, final_waits):
            for waits, fn, semo, inc in lst:
                for s, v in waits:
                    engobj.wait_ge(s, v)
                ins = fn(engobj)
                ins.then_inc(semo, inc)
            for s, v in final_waits:
                engobj.wait_ge(s, v)

        fin = []
        for q in ("sp", "pool"):
            for i in range(self.NDMA):
                if self.dcnt[q][i] > 0:
                    fin.append((self.dsem[q][i], self.dcnt[q][i]))
        for e in ("pe", "act", "dve", "pool"):
            if self.cnt[e] > 0:
                fin.append((self.sem[e], self.cnt[e]))

        @block.tensor
        def _(e):
            run(e, self.ops["pe"], [])

        @block.scalar
        def _(e):
            run(e, self.ops["act"], [])

        @block.vector
        def _(e):
            run(e, self.ops["dve"], [])

        @block.gpsimd
        def _(e):
            run(e, self.ops["pool"], [])

        @block.sync
        def _(e):
            run(e, self.ops["sp"], fin)


def acopy(e, out, in_):
    return e.activation(out=out, in_=in_, func=AF.Identity)


def rev(ap):
    apl = [list(d) for d in ap.ap]
    st, n = apl[-1]
    apl[-1] = [-st, n]
    return bass.AP(tensor=ap.tensor, offset=ap.offset + st * (n - 1), ap=apl)


def bcast_last(ap, n):
    apl = [list(d) for d in ap.ap]
    assert apl[-1][1] == 1
    apl[-1] = [0, n]
    return bass.AP(tensor=ap.tensor, offset=ap.offset, ap=apl)


class Phase:
    def __init__(self, nc, tag):
        self.nc = nc
        self.tag = tag
        self.stack = ExitStack()
        self.S = Sched(nc, self.stack, tag)
        self.n = 0

    def sb(self, shape, dt, name=None):
        self.n += 1
        return self.stack.enter_context(self.nc.sbuf_tensor(f"{self.tag}_{name or 'sb'}{self.n}", list(shape), dt))

    def ps(self, shape, dt, name=None):
        self.n += 1
        return self.stack.enter_context(self.nc.psum_tensor(f"{self.tag}_{name or 'ps'}{self.n}", list(shape), dt))

    def res(self, name=None):
        return self.S.res(name)

    def close(self):
        with self.nc.Block() as block:
            self.S.emit(block)
        self.stack.close()


class RR:
    def __init__(self, ph, n, shape, dt, name, psum=False):
        self.items = [((ph.ps if psum else ph.sb)(shape, dt, name), ph.res(name)) for _ in range(n)]
        self.i = 0

    def next(self):
        it = self.items[self.i]
        self.i = (self.i + 1) % len(self.items)
        return it


def dma(ph, q, out, in_, reads=(), writes=()):
    return ph.S.op(q, lambda e: e.dma_start(out=out, in_=in_), reads, writes, dma=True)


def col2(ap3, kc, a, b):
    v = ap3[:, kc, a:a + 1]
    apl = [list(d) for d in v.ap]
    apl[-1] = [(b - a) * apl[-1][0] if apl[-1][0] != 0 else (b - a), 2]
    return bass.AP(tensor=v.tensor, offset=v.offset, ap=apl)


class NormCtx:
    def __init__(self, ph, ident, rident, with_T=True):
        self.ph = ph
        self.junk = ph.sb([P, D], BF16, "junk")
        self.rjunk = ph.res()
        self.st = RR(ph, 2, [P, 4], F32, "nst")
        self.xn = RR(ph, 2, [P, D], BF16, "xn")
        self.ident = ident
        self.rident = rident
        if with_T:
            self.pT = RR(ph, 1, [P, D], BF16, "pT", psum=True)


def rms_stats(ph, nx, xt, rx):
    S = ph.S
    st, rst = nx.st.next()
    npart = xt.shape[0]
    S.op("act", lambda e: e.activation(out=nx.junk[0:npart, :], in_=xt, func=AF.Square, accum_out=st[0:npart, 0:1]),
         reads=[rx], writes=[nx.rjunk, rst])
    S.op("dve", lambda e: e.tensor_scalar(out=st[:, 1:2], in0=st[:, 0:1], scalar1=1.0 / D, scalar2=1e-6,
                                           op0=ALU.mult, op1=ALU.add), reads=[rst], writes=[rst])
    S.op("act", lambda e: e.activation(out=st[:, 2:3], in_=st[:, 1:2], func=AF.Sqrt), reads=[rst], writes=[rst])
    S.op("dve", lambda e: e.reciprocal(out=st[:, 3:4], in_=st[:, 2:3]), reads=[rst], writes=[rst])
    return st, rst


def norm_T(ph, nx, xt, rx, gbc, rg, hT, rhT, c0, ncol=P, npart=P):
    S = ph.S
    st, rst = rms_stats(ph, nx, xt, rx)
    xn, rxn = nx.xn.next()
    S.op("dve", lambda e: e.scalar_tensor_tensor(out=xn[0:npart, :], in0=xt, scalar=st[0:npart, 3:4], in1=gbc[0:npart, :],
                                                  op0=ALU.mult, op1=ALU.mult), reads=[rx, rst, rg], writes=[rxn])
    pT, rpT = nx.pT.next()

    def tr(e):
        ins = None
        for kc in range(KC):
            ins = e.transpose(pT[:, kc * P:kc * P + npart], xn[0:npart, kc * P:(kc + 1) * P], nx.ident[0:npart, 0:npart])
        return ins
    S.op("pe", tr, reads=[rxn, nx.rident], writes=[rpT])
    src = pT[:, :].rearrange("p (k c) -> p k c", k=KC)[:, :, 0:npart]
    S.op("act", lambda e: acopy(e, out=hT[:, :, c0:c0 + npart], in_=src), reads=[rpT], writes=[rhT])


def load_bcast(ph, vec_ap, n, name):
    t = ph.sb([P, n], F32, name)
    r = ph.res(name)
    dma(ph, "sp", t[:], vec_ap[0, :].partition_broadcast(P), writes=[r])
    return t, r


def load_plain(ph, ap, shape, dt, name, q="sp"):
    t = ph.sb(shape, dt, name)
    r = ph.res(name)
    dma(ph, q, t[:], ap, writes=[r])
    return t, r


class WStream:
    def __init__(self, ph, nbuf, shape, name):
        self.ph = ph
        self.rr = RR(ph, nbuf, shape, BF16, name)

    def load(self, src_ap, dst_slice=None):
        t, r = self.rr.next()
        dst = t[:] if dst_slice is None else dst_slice(t)
        dma(self.ph, "sp", dst, src_ap, writes=[r])
        return t, r


def ffn_phase(nc, tag, groups, xin_d, xout_d, hT_d, wup_d, wdn_d, convc_d):
    ph = Phase(nc, tag)
    S = ph.S
    cw, rcw = load_plain(ph, convc_d[:, :], [P, 4 * FC], F32, "cw")
    hxs = RR(ph, 2, [P, KC, G + 2], BF16, "hx")
    xss = RR(ph, 2, [P, 4, 256], F32, "xs")
    wus = WStream(ph, 3, [P, 2, KC, P], "wu")
    wds = WStream(ph, 2, [P, FC, 256], "wd")
    act = ph.sb([P, FC, G], BF16, "act")
    ract = [ph.res() for _ in range(FC)]
    pgs = RR(ph, 2, [P, G], F32, "pg", psum=True)
    pvs = RR(ph, 2, [P, G], F32, "pv", psum=True)
    phl = ph.ps([P, G], F32, "phalo")
    _r = ph.res()
    rphl = [_r, _r]
    pds = [ph.ps([P, G], F32, "pd"), ph.ps([P, G], F32, "pd")]
    rpd = [ph.res(), ph.res()]
    tmps = RR(ph, 2, [P, G], F32, "tmp")
    sls = RR(ph, 2, [P, G], F32, "sl")
    hT_v = hT_d.rearrange("k p t -> p k t")
    ntok = hT_d.shape[2]
    wup_v = wup_d.rearrange("o p k c -> p o k c")
    nh = 0
    nd = 0
    for g in groups:
        g0 = g * G
        hx, rhx = hxs.next()
        lo = max(g0 - 1, 0)
        hi = min(g0 + G + 1, ntok)
        dma(ph, "sp", hx[:, :, lo - (g0 - 1):hi - (g0 - 1)], hT_v[:, :, lo:hi], writes=[rhx])
        if lo > g0 - 1:
            S.op("pool", lambda e, hx=hx: e.memset(hx[:, :, 0:1], 0.0), writes=[rhx])
        if hi < g0 + G + 1:
            S.op("pool", lambda e, hx=hx: e.memset(hx[:, :, G + 1:G + 2], 0.0), writes=[rhx])
        for j in range(FC):
            wt, rwt = wus.load(wup_v[:, 2 * j:2 * j + 2, :, :])
            pg, rpg = pgs.next()
            pv, rpv = pvs.next()
            hs = nh % 2
            nh += 1

            def mm_gate(e, wt=wt, pg=pg, hx=hx, hs=hs):
                ins = None
                for kc in range(KC):
                    e.matmul(pg[:, :], lhsT=wt[:, 0, kc, :], rhs=hx[:, kc, 1:G + 1], start=(kc == 0), stop=(kc == KC - 1))
                for kc in range(KC):
                    ins = e.matmul(phl[:, hs * 2:hs * 2 + 2], lhsT=wt[:, 0, kc, :], rhs=col2(hx, kc, 0, G + 1),
                                   start=(kc == 0), stop=(kc == KC - 1))
                return ins
            S.op("pe", mm_gate, reads=[rwt, rhx], writes=[rpg, rphl[hs]])

            def mm_val(e, wt=wt, pv=pv, hx=hx):
                ins = None
                for kc in range(KC):
                    ins = e.matmul(pv[:, :], lhsT=wt[:, 1, kc, :], rhs=hx[:, kc, 1:G + 1], start=(kc == 0), stop=(kc == KC - 1))
                return ins
            S.op("pe", mm_val, reads=[rwt, rhx], writes=[rpv])
            tmp, rtmp = tmps.next()
            sl, rsl = sls.next()
            c0 = cw[:, 0 * FC + j:0 * FC + j + 1]
            c1 = cw[:, 1 * FC + j:1 * FC + j + 1]
            c2 = cw[:, 2 * FC + j:2 * FC + j + 1]
            cb = cw[:, 3 * FC + j:3 * FC + j + 1]
            S.op("act", lambda e, tmp=tmp, pg=pg, c1=c1, cb=cb: e.activation(out=tmp[:, :], in_=pg[:, :], func=AF.Identity, scale=c1, bias=cb),
                 reads=[rpg, rcw], writes=[rtmp])
            S.op("dve", lambda e, tmp=tmp, pg=pg, c0=c0: e.scalar_tensor_tensor(out=tmp[:, 1:G], in0=pg[:, 0:G - 1], scalar=c0, in1=tmp[:, 1:G],
                                                                                  op0=ALU.mult, op1=ALU.add), reads=[rpg, rcw, rtmp], writes=[rtmp])
            S.op("dve", lambda e, tmp=tmp, pg=pg, c2=c2: e.scalar_tensor_tensor(out=tmp[:, 0:G - 1], in0=pg[:, 1:G], scalar=c2, in1=tmp[:, 0:G - 1],
                                                                                  op0=ALU.mult, op1=ALU.add), reads=[rpg, rcw, rtmp], writes=[rtmp])
            S.op("dve", lambda e, tmp=tmp, c0=c0, hs=hs: e.scalar_tensor_tensor(out=tmp[:, 0:1], in0=phl[:, hs * 2:hs * 2 + 1], scalar=c0, in1=tmp[:, 0:1],
                                                                                  op0=ALU.mult, op1=ALU.add), reads=[rphl[hs], rcw, rtmp], writes=[rtmp])
            S.op("dve", lambda e, tmp=tmp, c2=c2, hs=hs: e.scalar_tensor_tensor(out=tmp[:, G - 1:G], in0=phl[:, hs * 2 + 1:hs * 2 + 2], scalar=c2, in1=tmp[:, G - 1:G],
                                                                                  op0=ALU.mult, op1=ALU.add), reads=[rphl[hs], rcw, rtmp], writes=[rtmp])
            S.op("act", lambda e, tmp=tmp, sl=sl: e.activation(out=sl[:, :], in_=tmp[:, :], func=AF.Silu), reads=[rtmp], writes=[rsl])
            S.op("dve", lambda e, sl=sl, pv=pv, j=j: e.tensor_tensor(out=act[:, j, :], in0=sl[:, :], in1=pv[:, :], op=ALU.mult),
                 reads=[rsl, rpv], writes=[ract[j]])
        for cbk in range(D // 256):
            wd, rwd = wds.load(wdn_d[cbk])
            xs, rxs = xss.next()
            cs = slice(cbk * 256, (cbk + 1) * 256)
            dma(ph, "sp", xs[:], xin_d[g0:g0 + G, cs].rearrange("(t p) d -> p t d", p=P), writes=[rxs])
            for t in range(4):
                hb = nd % 2
                nd += 1

                def mm_dn(e, wd=wd, t=t, hb=hb):
                    ins = None
                    for kc in range(FC):
                        ins = e.matmul(pds[hb][:, 0:256], lhsT=act[:, kc, t * P:(t + 1) * P], rhs=wd[:, kc, :], start=(kc == 0), stop=(kc == FC - 1))
                    return ins
                S.op("pe", mm_dn, reads=[rwd] + ract, writes=[rpd[hb]])
                S.op("dve", lambda e, xs=xs, t=t, hb=hb: e.tensor_tensor(out=xs[:, t, :], in0=pds[hb][:, 0:256], in1=xs[:, t, :], op=ALU.add),
                     reads=[rpd[hb], rxs], writes=[rxs])
            dma(ph, "sp", xout_d[g0:g0 + G, cs].rearrange("(t p) d -> p t d", p=P), xs[:], reads=[rxs])
    ph.close()


def normT_phase(nc, tag, groups, xin_d, g_d, hT_d, ident_d):
    ph = Phase(nc, tag)
    ident, rident = load_plain(ph, ident_d[:, :], [P, P], BF16, "ident", q="pool")
    gbc, rg = load_bcast(ph, g_d, D, "gbc")
    nx = NormCtx(ph, ident, rident)
    xts = RR(ph, 2, [P, 4, D], F32, "xt")
    hts = RR(ph, 2, [P, KC, G], BF16, "hT")
    hT_v = hT_d.rearrange("k p t -> p k t")
    for g in groups:
        g0 = g * G
        xt, rxt = xts.next()
        dma(ph, "sp", xt[:], xin_d[g0:g0 + G, :].rearrange("(t p) d -> p t d", p=P), writes=[rxt])
        hT, rhT = hts.next()
        for t in range(4):
            norm_T(ph, nx, xt[:, t, :], rxt, gbc, rg, hT, rhT, t * P)
        dma(ph, "sp", hT_v[:, :, g0:g0 + G], hT[:], reads=[rhT])
    ph.close()


def cast_phase(nc, tag, pairs):
    ph = Phase(nc, tag)
    for dst, src in pairs:
        rows, cols = src.shape
        step = max(1, (1 << 20) // cols)
        for r0 in range(0, rows, step):
            r1 = min(rows, r0 + step)
            dma(ph, "pool", dst[r0:r1, :], src[r0:r1, :])
    ph.close()


def lay_ws(w):
    K_, N = w.shape
    return np.ascontiguousarray(w.reshape(K_ // P, P, N // P, P).transpose(2, 1, 0, 3))


def lay_as(w, cb):
    K_, N = w.shape
    return np.ascontiguousarray(w.reshape(K_ // P, P, N // cb, cb).transpose(2, 1, 0, 3))


def lay_up(w_up):
    ws = lay_ws(w_up)
    o = np.empty_like(ws)
    o[0::2] = ws[:FC]
    o[1::2] = ws[FC:]
    return o.reshape(2 * FC * P, KC * P)


def lay_cols(v):
    return np.ascontiguousarray(v.reshape(-1, P).T)


def lay_conv(cw, cb):
    return np.ascontiguousarray(np.concatenate([lay_cols(cw[0]), lay_cols(cw[1]), lay_cols(cw[2]), lay_cols(cb)], axis=1))


def qkv_phase(nc, tag, groups, xin_d, g_d, ident_d, wqk_d, wv_d, cos_d, sin_d, qT_d, kT_d, v_d):
    ph = Phase(nc, tag)
    S = ph.S
    ident, rident = load_plain(ph, ident_d[:, :], [P, P], BF16, "ident", q="pool")
    gbc, rg = load_bcast(ph, g_d, D, "gbc")
    wv, rwv = load_plain(ph, wv_d, [P, KC, 512], BF16, "wv")
    nx = NormCtx(ph, ident, rident)
    xts = RR(ph, 1, [P, 4, D], F32, "xt")
    hts = RR(ph, 1, [P, KC, G], BF16, "hT")
    css = RR(ph, 2, [P, 2, G], F32, "cs")
    wqs = WStream(ph, 3, [P, 2, KC, P], "wq")
    pas = RR(ph, 2, [P, G], F32, "pa", psum=True)
    pbs = RR(ph, 2, [P, G], F32, "pb", psum=True)
    t1s = RR(ph, 2, [P, G], F32, "t1")
    t2s = RR(ph, 2, [P, G], F32, "t2")
    qos = RR(ph, 1, [P, 20, G], BF16, "qo")
    vos = RR(ph, 1, [P, 4, 512], BF16, "vo")
    zt = ph.sb([P, 4 * 512], BF16, "zt")
    rz = ph.res()
    S.op("pool", lambda e: e.memset(zt[:], 0.0), writes=[rz])
    Lk = kT_d.shape[2]
    kT_v = kT_d.rearrange("h p t -> p h t")
    qT_v = qT_d.rearrange("h p t -> p h t")
    wqk_v = wqk_d.rearrange("o p k c -> p o k c")
    ztk = zt[:, 0:4 * P].rearrange("p (h t) -> p h t", h=4)
    dma(ph, "sp", kT_v[:, :, 0:P], ztk, reads=[rz])
    dma(ph, "sp", kT_v[:, :, Lk - P:Lk], ztk, reads=[rz])
    dma(ph, "sp", v_d[0:P, :], zt[:, 0:512], reads=[rz])
    dma(ph, "sp", v_d[Lk - P:Lk, :], zt[:, 0:512], reads=[rz])
    for g in groups:
        g0 = g * G
        xt, rxt = xts.next()
        dma(ph, "sp", xt[:], xin_d[g0:g0 + G, :].rearrange("(t p) d -> p t d", p=P), writes=[rxt])
        cs, rcs = css.next()
        dma(ph, "sp", cs[:, 0, :], cos_d[:, g0:g0 + G], writes=[rcs])
        dma(ph, "sp", cs[:, 1, :], sin_d[:, g0:g0 + G], writes=[rcs])
        hT, rhT = hts.next()
        for t in range(4):
            norm_T(ph, nx, xt[:, t, :], rxt, gbc, rg, hT, rhT, t * P)
        qo, rqo = qos.next()
        for hh in range(20):
            wt, rwt = wqs.load(wqk_v[:, 2 * hh:2 * hh + 2, :, :])
            pa, rpa = pas.next()
            pb, rpb = pbs.next()

            def mm(e, wt=wt, pa=pa, pb=pb, hT=hT):
                ins = None
                for kc in range(KC):
                    e.matmul(pa[:, :], lhsT=wt[:, 0, kc, :], rhs=hT[:, kc, :], start=(kc == 0), stop=(kc == KC - 1))
                for kc in range(KC):
                    ins = e.matmul(pb[:, :], lhsT=wt[:, 1, kc, :], rhs=hT[:, kc, :], start=(kc == 0), stop=(kc == KC - 1))
                return ins
            S.op("pe", mm, reads=[rwt, rhT], writes=[rpa, rpb])
            t1, rt1 = t1s.next()
            t2, rt2 = t2s.next()
            S.op("dve", lambda e, t1=t1, pa=pa, cs=cs: e.tensor_tensor(out=t1[:, :], in0=pa[:, :], in1=cs[:, 0, :], op=ALU.mult),
                 reads=[rpa, rcs], writes=[rt1])
            S.op("dve", lambda e, t2=t2, pb=pb, cs=cs: e.tensor_tensor(out=t2[:, :], in0=pb[:, :], in1=cs[:, 1, :], op=ALU.mult),
                 reads=[rpb, rcs], writes=[rt2])
            S.op("pool", lambda e, t1=t1, t2=t2, qo=qo, hh=hh: e.tensor_tensor(out=qo[:, hh, :], in0=t1[:, :], in1=t2[:, :], op=ALU.add),
                 reads=[rt1, rt2], writes=[rqo])
        dma(ph, "sp", qT_v[:, :, g0:g0 + G], qo[:, 0:16, :], reads=[rqo])
        dma(ph, "sp", kT_v[:, :, P + g0:P + g0 + G], qo[:, 16:20, :], reads=[rqo])
        vo, rvo = vos.next()
        for t in range(4):
            pa, rpa = pas.next()

            def mmv(e, pa=pa, hT=hT, t=t):
                ins = None
                for kc in range(KC):
                    ins = e.matmul(pa[:, :], lhsT=hT[:, kc, t * P:(t + 1) * P], rhs=wv[:, kc, :], start=(kc == 0), stop=(kc == KC - 1))
                return ins
            S.op("pe", mmv, reads=[rwv, rhT], writes=[rpa])
            S.op("act", lambda e, vo=vo, pa=pa, t=t: acopy(e, out=vo[:, t, :], in_=pa[:, :]), reads=[rpa], writes=[rvo])
        dma(ph, "sp", v_d[P + g0:P + g0 + G, :].rearrange("(t p) c -> p t c", p=P), vo[:], reads=[rvo])
    ph.close()


def attn_phase(nc, tag, nblk, xin_d, xout_d, qT_d, kT_d, v_d, wo_d, band_d, kbias_d, sink_d, maskcol_d, ident_d):
    ph = Phase(nc, tag)
    S = ph.S
    SC = 128.0 ** -0.5
    ident, rident = load_plain(ph, ident_d[:, :], [P, P], BF16, "ident", q="pool")
    wo, rwo = load_plain(ph, wo_d.rearrange("n p k c -> p n k c"), [P, 4, KC, 512], BF16, "wo")
    band, rband = load_plain(ph, band_d[:, :], [P, 384], F32, "band")
    Lk = kbias_d.shape[1]
    kb, rkb = load_bcast(ph, kbias_d, Lk, "kbias")
    sk, rsk = load_bcast(ph, sink_d, 16, "sink")
    mc, rmc = load_plain(ph, maskcol_d[:, :], [P, maskcol_d.shape[1]], F32, "maskcol")
    qT_v = qT_d.rearrange("h p t -> p h t")
    kT_v = kT_d.rearrange("h p t -> p h t")
    qs = RR(ph, 2, [P, 16, P], BF16, "q")
    ks = RR(ph, 2, [P, 4, 384], BF16, "k")
    vs = RR(ph, 2, [P, 3, 512], BF16, "v")
    xts = RR(ph, 2, [P, D], F32, "xt")
    bns = RR(ph, 2, [P, 384], F32, "bn")
    pss = RR(ph, 2, [P, G], F32, "psc", psum=True)
    pts = RR(ph, 1, [P, 3, P], BF16, "ppt", psum=True)
    pos_ = RR(ph, 2, [P, G], F32, "po", psum=True)
    pws = RR(ph, 2, [P, G], F32, "pw", psum=True)
    ss = RR(ph, 2, [P, 384], F32, "s")
    pps = RR(ph, 2, [P, 384], F32, "p")
    pns = RR(ph, 2, [P, 384], BF16, "pn")
    pTs = RR(ph, 2, [P, 3, P], BF16, "pT")
    sts = RR(ph, 4, [P, 8], F32, "st")
    oTs = RR(ph, 2, [P, 16, P], BF16, "oT")
    for n in range(nblk):
        n0 = n * P
        q, rq = qs.next()
        dma(ph, "sp", q[:], qT_v[:, :, n0:n0 + P], writes=[rq])
        k, rk = ks.next()
        dma(ph, "sp", k[:], kT_v[:, :, n0:n0 + 384], writes=[rk])
        v, rv = vs.next()
        dma(ph, "sp", v[:], v_d[n0:n0 + 384, :].rearrange("(j p) c -> p j c", p=P), writes=[rv])
        xt, rxt = xts.next()
        dma(ph, "sp", xt[:], xin_d[n0:n0 + P, :], writes=[rxt])
        bn, rbn = bns.next()
        S.op("pool", lambda e, bn=bn, n0=n0: e.tensor_tensor(out=bn[:, :], in0=band[:, :], in1=kb[:, n0:n0 + 384], op=ALU.add),
             reads=[rband, rkb], writes=[rbn])
        oT, roT = oTs.next()
        for h in range(16):
            kv = h // 4
            psc, rpsc = pss.next()
            S.op("pe", lambda e, psc=psc, q=q, k=k, h=h, kv=kv: e.matmul(psc[:, 0:384], lhsT=q[:, h, :], rhs=k[:, kv, :], start=True, stop=True),
                 reads=[rq, rk], writes=[rpsc])
            s, rs = ss.next()
            st, rst = sts.next()
            S.op("dve", lambda e, s=s, psc=psc, bn=bn: e.scalar_tensor_tensor(out=s[:, :], in0=psc[:, 0:384], scalar=SC, in1=bn[:, :],
                                                                                op0=ALU.mult, op1=ALU.add), reads=[rpsc, rbn], writes=[rs])
            S.op("dve", lambda e, s=s, st=st: e.reduce_max(out=st[:, 0:1], in_=s[:, :], axis=AX.X), reads=[rs], writes=[rst])
            S.op("dve", lambda e, st=st, h=h: e.tensor_scalar(out=st[:, 1:2], in0=st[:, 0:1], scalar1=sk[:, h:h + 1], scalar2=-1.0,
                                                               op0=ALU.max, op1=ALU.mult), reads=[rst, rsk], writes=[rst])
            pp, rpp = pps.next()
            S.op("act", lambda e, pp=pp, s=s, st=st: e.activation(out=pp[:, :], in_=s[:, :], func=AF.Exp, bias=st[:, 1:2], accum_out=st[:, 2:3]),
                 reads=[rs, rst], writes=[rpp, rst])
            S.op("act", lambda e, st=st, h=h: e.activation(out=st[:, 3:4], in_=sk[:, h:h + 1], func=AF.Exp, bias=st[:, 1:2]),
                 reads=[rsk, rst], writes=[rst])
            S.op("dve", lambda e, st=st: e.tensor_tensor(out=st[:, 4:5], in0=st[:, 2:3], in1=st[:, 3:4], op=ALU.add), reads=[rst], writes=[rst])
            S.op("dve", lambda e, st=st: e.reciprocal(out=st[:, 5:6], in_=st[:, 4:5]), reads=[rst], writes=[rst])
            pn, rpn = pns.next()
            S.op("act", lambda e, pn=pn, pp=pp, st=st: e.activation(out=pn[:, :], in_=pp[:, :], func=AF.Identity, scale=st[:, 5:6]),
                 reads=[rpp, rst], writes=[rpn])
            ppt, rppt = pts.next()

            def trs(e, ppt=ppt, pn=pn):
                ins = None
                for j in range(3):
                    ins = e.transpose(ppt[:, j, :], pn[:, j * P:(j + 1) * P], ident[:, :])
                return ins
            S.op("pe", trs, reads=[rpn, rident], writes=[rppt])
            pT, rpT = pTs.next()
            S.op("dve", lambda e, pT=pT, ppt=ppt: e.tensor_copy(out=pT[:], in_=ppt[:]), reads=[rppt], writes=[rpT])
            po, rpo = pos_.next()

            def pv(e, po=po, v=v, pT=pT, kv=kv):
                ins = None
                for j in range(3):
                    ins = e.matmul(po[:, 0:P], lhsT=v[:, j, kv * P:(kv + 1) * P], rhs=pT[:, j, :], start=(j == 0), stop=(j == 2))
                return ins
            S.op("pe", pv, reads=[rv, rpT], writes=[rpo])
            S.op("act", lambda e, oT=oT, po=po, h=h: acopy(e, out=oT[:, h, :], in_=po[:, 0:P]), reads=[rpo], writes=[roT])
        for cbk in range(4):
            pw, rpw = pws.next()

            def mmo(e, pw=pw, oT=oT, cbk=cbk):
                ins = None
                for h in range(16):
                    ins = e.matmul(pw[:, :], lhsT=oT[:, h, :], rhs=wo[:, cbk, h, :], start=(h == 0), stop=(h == 15))
                return ins
            S.op("pe", mmo, reads=[roT, rwo], writes=[rpw])
            S.op("dve", lambda e, xt=xt, pw=pw, cbk=cbk: e.tensor_tensor(out=xt[:, cbk * 512:(cbk + 1) * 512], in0=pw[:, :],
                                                                          in1=xt[:, cbk * 512:(cbk + 1) * 512], op=ALU.add),
                 reads=[rpw, rxt], writes=[rxt])
        S.op("act", lambda e, xt=xt, n=n: e.activation(out=xt[:, :], in_=xt[:, :], func=AF.Identity, scale=mc[:, n:n + 1]),
             reads=[rxt, rmc], writes=[rxt])
        dma(ph, "sp", xout_d[n0:n0 + P, :], xt[:], reads=[rxt])
    ph.close()


def lay_qk(w_qkv):
    tiles = []
    for hh in range(20):
        blk = w_qkv[:, hh * P:(hh + 1) * P]
        perm = np.concatenate([blk[:, 64:], blk[:, :64]], axis=1)
        tiles.append(lay_ws(blk)[0])
        tiles.append(lay_ws(perm)[0])
    wqk = np.ascontiguousarray(np.stack(tiles))
    wv = lay_as(w_qkv[:, 20 * P:], 512)[0]
    return wqk.reshape(40 * P, KC * P), np.ascontiguousarray(wv.reshape(P, KC * 512))


def rope_tables(pos):
    inv = (np.float32(10000.0) ** (-(np.arange(64, dtype=np.float32) / np.float32(64)))).astype(np.float32)
    ang = (pos.astype(np.float32)[:, None] * inv[None, :]).astype(np.float32).astype(np.float64)
    c = np.cos(ang).T.astype(np.float32)
    s_ = np.sin(ang).T.astype(np.float32)
    return np.ascontiguousarray(np.concatenate([c, c], 0)), np.ascontiguousarray(np.concatenate([-s_, s_], 0))


def band_mask():
    q = np.arange(P)[:, None]
    s_ = np.arange(384)[None, :] - P
    return np.where(np.abs(s_ - q) <= 128, 0.0, NEG).astype(np.float32)


K0 = float(np.exp(-0.5))
NCH = 68


def mix_phase(nc, tag, groups, bwd, full, final, ntot, dr):
    ph = Phase(nc, tag)
    S = ph.S
    dirn = 1 if bwd else 0

    def src(ap):
        return rev(ap) if bwd else ap

    def E(eng, fn, reads=(), writes=()):
        return S.op(eng, fn, reads, writes)

    dbg = dr.get("dbg") or {}
    tapped = set()

    def tap(name, ap, r):
        if name in dbg and name not in tapped:
            tapped.add(name)
            dma(ph, "sp", dbg[name], ap, reads=[r])

    ident, rident = load_plain(ph, dr["ident"][:, :], [P, P], BF16, "ident", q="pool")
    cst, rcst = load_plain(ph, dr["mixc"][:, :], [P, dr["mixc"].shape[1]], F32, "mixc")
    C_LBT, C_ONORM, C_MU, C_W0, C_A0, C_KK, C_KA, C_RK = 0, 24, 32, 60, 76, 84, 92, 100
    rmask, rrm = load_plain(ph, dr["rmask"][:, :], [P, G], F32, "rmask")
    trim, rtrim = load_plain(ph, dr["trimask"][:, :], [P, P], F32, "trimask")
    mskM, rmskM = load_plain(ph, dr["maskM"][:, :], [64, 4 * P], F32, "maskM")
    mskX, rmskX = load_plain(ph, dr["maskX"][:, :], [64, 8 * 64], F32, "maskX")
    ones_b, rones = load_plain(ph, dr["ones"][:, :], [P, P], BF16, "ones", q="pool")
    blk64, rblk64 = load_plain(ph, dr["blk64"][:, :], [P, P], BF16, "blk64", q="pool")
    w2sb, rw2 = load_plain(ph, dr["w2"][:, :], [P, 1024], BF16, "w2", q="pool")
    a2sb, ra2 = load_plain(ph, dr["a2"][:, :], [64, 1024], BF16, "a2", q="pool")
    if final:
        g2a, rg2a = load_plain(ph, dr["g2"][0:P, :], [P, 1024], BF16, "g2a", q="pool")
        g2b, rg2b = load_plain(ph, dr["g2"][P:160, :], [32, 1024], BF16, "g2b", q="pool")
        lnc, rlnc = load_plain(ph, dr["lnc"][:, :], [64, 32], F32, "lnc")
        ones64, rones64 = load_plain(ph, dr["ones64"][:, :], [64, 64], BF16, "ones64", q="pool")
    dc = ph.sb([P, 128], F32, "dconst")
    rdc = ph.res()
    lbt = cst[:, C_LBT:C_LBT + 24].rearrange("p (r h) -> p r h", r=3)
    D_LB, D_OML, D_NOML, D_OMM, D_HM, D_T = 0, 8, 16, 24, 52, 80
    E("dve", lambda e: e.tensor_tensor(out=dc[:, D_T:D_T + 8], in0=lbt[:, 0, :], in1=lbt[:, 1, :], op=ALU.max), [rcst], [rdc])
    E("dve", lambda e: e.tensor_tensor(out=dc[:, D_T:D_T + 8], in0=dc[:, D_T:D_T + 8], in1=lbt[:, 2, :], op=ALU.max), [rcst, rdc], [rdc])
    for r_ in range(3):
        E("dve", lambda e, r_=r_: e.tensor_tensor(out=dc[:, D_T + 8 + 8 * r_:D_T + 16 + 8 * r_], in0=lbt[:, r_, :], in1=dc[:, D_T:D_T + 8], op=ALU.subtract),
          [rcst, rdc], [rdc])
    E("act", lambda e: e.activation(out=dc[:, D_T + 8:D_T + 32], in_=dc[:, D_T + 8:D_T + 32], func=AF.Exp), [rdc], [rdc])
    E("dve", lambda e: e.tensor_tensor(out=dc[:, D_T:D_T + 8], in0=dc[:, D_T + 8:D_T + 16], in1=dc[:, D_T + 16:D_T + 24], op=ALU.add), [rdc], [rdc])
    E("dve", lambda e: e.tensor_tensor(out=dc[:, D_T:D_T + 8], in0=dc[:, D_T:D_T + 8], in1=dc[:, D_T + 24:D_T + 32], op=ALU.add), [rdc], [rdc])
    E("dve", lambda e: e.reciprocal(out=dc[:, D_T:D_T + 8], in_=dc[:, D_T:D_T + 8]), [rdc], [rdc])
    E("dve", lambda e: e.tensor_tensor(out=dc[:, D_LB:D_LB + 8], in0=dc[:, D_T + 8:D_T + 16], in1=dc[:, D_T:D_T + 8], op=ALU.mult), [rdc], [rdc])
    E("dve", lambda e: e.tensor_scalar(out=dc[:, D_OML:D_OML + 8], in0=dc[:, D_LB:D_LB + 8], scalar1=-1.0, scalar2=1.0, op0=ALU.mult, op1=ALU.add), [rdc], [rdc])
    E("dve", lambda e: e.tensor_scalar(out=dc[:, D_NOML:D_NOML + 8], in0=dc[:, D_OML:D_OML + 8], scalar1=-1.0, scalar2=None, op0=ALU.mult), [rdc], [rdc])
    E("dve", lambda e: e.tensor_scalar(out=dc[:, D_OMM:D_OMM + 28], in0=cst[:, C_MU:C_MU + 28], scalar1=-1.0, scalar2=1.0, op0=ALU.mult, op1=ALU.add), [rcst], [rdc])
    E("dve", lambda e: e.tensor_scalar(out=dc[:, D_HM:D_HM + 28], in0=cst[:, C_MU:C_MU + 28], scalar1=0.5, scalar2=None, op0=ALU.mult), [rcst], [rdc])

    hTs = RR(ph, 2, [P, KC, G + 2], BF16, "hT")
    hT_v = dr["hT0"].rearrange("k p t -> p k t")
    mkg = ph.sb([P, G], F32, "maskg")
    rmkg = ph.res()
    FB = RR(ph, 4, [P, G], F32, "fb", psum=True)
    PY = [(ph.ps([P, G], F32, "py"), ph.res()) for _ in range(2)]
    TB = RR(ph, 2, [P, 8 * P], BF16, "tb", psum=True)
    wis = WStream(ph, 3, [P, KC, P], "wi")
    win_v = dr["w_in"].rearrange("o p k c -> p o k c")
    tf = RR(ph, 15, [P, G], F32, "tf")
    tb_ = RR(ph, 20, [P, G], BF16, "tbf")
    th = ph.sb([P, G], BF16, "th")
    rth = ph.res()
    adb = ph.sb([64, G], BF16, "adb")
    radb = ph.res()
    if final:
        sgd = ph.sb([P, 2, G], BF16, "sgd")
        rsgd = ph.res()
    S32 = ph.sb([P, 8, P], F32, "S32")
    rS32 = [ph.res() for _ in range(8)]
    Sbf = [ph.sb([P, 8, P], BF16, "Sbf") for _ in range(2)]
    rSbf = [[ph.res() for _ in range(8)] for _ in range(2)]
    sbp = [0] * 8
    H32 = ph.sb([64, 16, 64], F32, "H32")
    rH32 = [ph.res() for _ in range(16)]
    Hbf = [ph.sb([64, 16, 64], BF16, "Hbf") for _ in range(2)]
    rHbf = [[ph.res() for _ in range(16)] for _ in range(2)]
    hbp = [0] * 16
    if dr.get("st_in") is not None:
        dma(ph, "sp", S32[:].rearrange("p h e -> p (h e)"), dr["st_in"][0][:, :], writes=rS32)
        dma(ph, "sp", H32[:].rearrange("p h e -> p (h e)"), dr["st_in"][1][:, :], writes=rH32)
    else:
        E("pool", lambda e: e.memset(S32[:], 0.0), [], rS32)
        E("pool", lambda e: e.memset(H32[:], 0.0), [], rH32)
    E("act", lambda e: acopy(e, out=Sbf[0][:], in_=S32[:]), rS32, rSbf[0])
    E("act", lambda e: acopy(e, out=Hbf[0][:], in_=H32[:]), rH32, rHbf[0])
    if full:
        yst = RR(ph, 2, [P, G], F32, "yst")
    if full:
        ysrs = RR(ph, 1, [64, 2, G], F32, "ysr")
    if final:
        yaTs = RR(ph, 2, [P, G], BF16, "yaT")
        ybTs = RR(ph, 2, [64, 2, G], BF16, "ybT")
        parts = RR(ph, 2, [P, G], F32, "part")
        bon1 = ph.sb([64, G], BF16, "bon1")
        rbon1 = ph.res()
    AR = ph.sb([P, 8, 2, 64], BF16, "AR")
    rAR = ph.res()
    tmq = {nm: (ph.sb([64, 8, P], BF16, nm), ph.res()) for nm in ("ATM", "BHM", "KHM", "VTM")}
    M1 = ph.sb([64, 16, P], BF16, "M1")
    rM1 = ph.res()
    M2 = ph.sb([64, 16, P], BF16, "M2")
    rM2 = ph.res()
    Xs = [ph.sb([64, 16, 64], BF16, "X") for _ in range(2)]
    rXs = [ph.res(), ph.res()]
    XTs = [ph.sb([64, 16, 64], BF16, "XT") for _ in range(2)]
    rXTs = [ph.res(), ph.res()]
    Z = ph.sb([64, 16, P], BF16, "Z")
    rZ = ph.res()
    QE = ph.sb([64, 16, 64], BF16, "QE")
    rQE = ph.res()
    GT = ph.sb([64, 16, 64], BF16, "GT")
    rGT = ph.res()
    HI = ph.sb([64, 16, 64], BF16, "HI")
    rHI = ph.res()
    PC = ph.sb([64, 2, 8], F32, "PC")
    rPC = ph.res()
    PCt = ph.sb([P, 8], F32, "PCt")
    rPCt = ph.res()
    AR1 = ph.sb([64, 8, 2, 64], BF16, "AR1")
    rAR1 = ph.res()
    BK1 = ph.sb([64, 2, G], BF16, "BK1")
    rBK1 = ph.res()
    RT0 = ph.sb([64, 2, G], BF16, "RT0")
    rRT0 = ph.res()

    cur_h = {}

    def proj(chunk, halo):
        hT, rhT = cur_h["hT"]
        wt, rwt = wis.load(win_v[:, chunk, :, :])
        pp, rpp = FB.next()
        if halo:
            phh, rph = FB.next()
        else:
            phh, rph = None, None

        def mm(e):
            ins = None
            for kc in range(KC):
                ins = e.matmul(pp[:, :], lhsT=wt[:, kc, :], rhs=hT[:, kc, 1:G + 1], start=(kc == 0), stop=(kc == KC - 1))
            if halo:
                for kc in range(KC):
                    ins = e.matmul(phh[:, 0:2], lhsT=wt[:, kc, :], rhs=col2(hT, kc, 0, G + 1), start=(kc == 0), stop=(kc == KC - 1))
            return ins
        E("pe", mm, [rwt, rhT], [rpp] + ([rph] if halo else []))
        return pp, rpp, phh, rph

    def shift(pp, rpp, phh, rph, mi, np_=P):
        o, ro = tf.next()
        omm = dc[0:np_, D_OMM + mi:D_OMM + mi + 1]
        hm = dc[0:np_, D_HM + mi:D_HM + mi + 1]
        sp_ = src(pp[0:np_, :])
        E("act", lambda e: e.activation(out=o[0:np_, :], in_=sp_, func=AF.Identity, scale=omm), [rpp, rdc], [ro])
        if bwd:
            a_ = rev(pp[0:np_, 1:G])
            b_ = rev(pp[0:np_, 0:G - 1])
            hf, hl = phh[0:np_, 1:2], phh[0:np_, 0:1]
        else:
            a_ = pp[0:np_, 0:G - 1]
            b_ = pp[0:np_, 1:G]
            hf, hl = phh[0:np_, 0:1], phh[0:np_, 1:2]
        E("dve", lambda e: e.scalar_tensor_tensor(out=o[0:np_, 1:G], in0=a_, scalar=hm, in1=o[0:np_, 1:G], op0=ALU.mult, op1=ALU.add), [rpp, rdc, ro], [ro])
        E("dve", lambda e: e.scalar_tensor_tensor(out=o[0:np_, 0:G - 1], in0=b_, scalar=hm, in1=o[0:np_, 0:G - 1], op0=ALU.mult, op1=ALU.add), [rpp, rdc, ro], [ro])
        E("dve", lambda e: e.scalar_tensor_tensor(out=o[0:np_, 0:1], in0=hf, scalar=hm, in1=o[0:np_, 0:1], op0=ALU.mult, op1=ALU.add), [rph, rdc, ro], [ro])
        E("dve", lambda e: e.scalar_tensor_tensor(out=o[0:np_, G - 1:G], in0=hl, scalar=hm, in1=o[0:np_, G - 1:G], op0=ALU.mult, op1=ALU.add), [rph, rdc, ro], [ro])
        return o, ro

    for g in groups:
        g0 = g * G
        hT, rhT = hTs.next()
        cur_h["hT"] = (hT, rhT)
        lo = max(g0 - 1, 0)
        hi = min(g0 + G + 1, ntot)
        dma(ph, "sp", hT[:, :, lo - (g0 - 1):hi - (g0 - 1)], hT_v[:, :, lo:hi], writes=[rhT])
        if lo > g0 - 1:
            E("pool", lambda e, hT=hT: e.memset(hT[:, :, 0:1], 0.0), [], [rhT])
        if hi < g0 + G + 1:
            E("pool", lambda e, hT=hT: e.memset(hT[:, :, G + 1:G + 2], 0.0), [], [rhT])
        dma(ph, "sp", mkg[:], dr["maskrow"][0, g0:g0 + G].partition_broadcast(P), writes=[rmkg])

        pp, rpp, phh, rph = proj(64, True)
        o, ro = shift(pp, rpp, phh, rph, 24)
        E("act", lambda e, o=o: e.activation(out=th[:, :], in_=o[:, :], func=AF.Tanh), [ro], [rth])
        pp, rpp, phh, rph = proj(65, True)
        o, ro = shift(pp, rpp, phh, rph, 25, np_=64)
        E("act", lambda e, o=o: acopy(e, out=adb[:, :], in_=o[0:64, :]), [ro], [radb])
        if final:
            pp, rpp, phh, rph = proj(66, True)
            o, ro = shift(pp, rpp, phh, rph, 26)
            E("act", lambda e, o=o: e.activation(out=sgd[:, 0, :], in_=o[:, :], func=AF.Sigmoid), [ro], [rsgd])
            pp, rpp, phh, rph = proj(67, True)
            o, ro = shift(pp, rpp, phh, rph, 27, np_=32)
            E("act", lambda e, o=o: e.activation(out=sgd[0:32, 1, :], in_=o[0:32, :], func=AF.Sigmoid), [ro], [rsgd])

        def hg_A(h):
            lb = dc[:, D_LB + h:D_LB + h + 1]
            oml = dc[:, D_OML + h:D_OML + h + 1]
            noml = dc[:, D_NOML + h:D_NOML + h + 1]
            if full:
                pq, rpq, _, _ = proj(h, False)
            pi, rpi, _, _ = proj(8 + h, False)
            pz, rpz, _, _ = proj(16 + dirn * 8 + h, False)
            if final:
                pgt, rpgt, _, _ = proj(32 + h, False)
            sg, rsg = tf.next()
            E("act", lambda e, sg=sg, pz=pz: e.activation(out=sg[:, :], in_=src(pz[:, :]), func=AF.Sigmoid), [rpz], [rsg])
            f_, rf = tf.next()
            E("dve", lambda e, f_=f_, sg=sg, oml=oml, lb=lb: e.tensor_scalar(out=f_[:, :], in0=sg[:, :], scalar1=oml, scalar2=lb, op0=ALU.mult, op1=ALU.add), [rsg, rdc], [rf])
            kk, rkk = tf.next()
            E("dve", lambda e, kk=kk, sg=sg, oml=oml, noml=noml: e.tensor_scalar(out=kk[:, :], in0=sg[:, :], scalar1=noml, scalar2=oml, op0=ALU.mult, op1=ALU.add), [rsg, rdc], [rkk])
            lf, rlf = tf.next()
            E("act", lambda e, lf=lf, f_=f_: e.activation(out=lf[:, :], in_=f_[:, :], func=AF.Ln), [rf], [rlf])
            bcs, rbcs = tf.next()
            E("dve", lambda e, bcs=bcs, lf=lf: e.tensor_tensor_scan(out=bcs[:, :], data0=rmask[:, :], data1=lf[:, :], initial=0.0, op0=ALU.mult, op1=ALU.add), [rlf, rrm], [rbcs])
            eb, reb = tf.next()
            E("act", lambda e, eb=eb, bcs=bcs: e.activation(out=eb[:, :], in_=bcs[:, :], func=AF.Exp), [rbcs], [reb])
            enb, renb = tf.next()
            E("act", lambda e, enb=enb, bcs=bcs: e.activation(out=enb[:, :], in_=bcs[:, :], func=AF.Exp, scale=-1.0), [rbcs], [renb])
            if full:
                QD, rQD = tb_.next()
                E("dve", lambda e, QD=QD, pq=pq, eb=eb: e.tensor_tensor(out=QD[:, :], in0=src(pq[:, :]), in1=eb[:, :], op=ALU.mult), [rpq, reb], [rQD])
                KD, rKD = tb_.next()
                E("pool", lambda e, KD=KD, kk=kk, enb=enb: e.tensor_tensor(out=KD[:, :], in0=kk[:, :], in1=enb[:, :], op=ALU.mult), [rkk, renb], [rKD])
            khf, rkhf = tf.next()
            E("dve", lambda e, khf=khf, kk=kk, enb=enb: e.tensor_tensor(out=khf[:, :], in0=kk[:, :], in1=enb[:, :], op=ALU.mult), [rkk, renb], [rkhf])
            KH, rKH = tb_.next()
            ebl = bcast_last(eb[:, :].rearrange("p (n c) -> p n c", c=64)[:, :, 63:64], 64)
            E("dve", lambda e, KH=KH, khf=khf, ebl=ebl: e.tensor_tensor(out=KH[:, :].rearrange("p (n c) -> p n c", c=64), in0=khf[:, :].rearrange("p (n c) -> p n c", c=64),
                                                                          in1=ebl, op=ALU.mult), [rkhf, reb], [rKH])
            VT, rVT = tb_.next()
            E("act", lambda e, VT=VT, pi=pi: acopy(e, out=VT[:, :], in_=src(pi[:, :])), [rpi], [rVT])
            if final:
                GSh, rGSh = tb_.next()
                E("act", lambda e, pgt=pgt, GSh=GSh: e.activation(out=GSh[:, :], in_=src(pgt[:, :]), func=AF.Silu), [rpgt], [rGSh])
            ptb, rptb = TB.next()

            def trh(e, ptb=ptb, VT=VT, KH=KH):
                ins = None
                for b in range(4):
                    e.transpose(ptb[:, b * P:(b + 1) * P], VT[:, b * P:(b + 1) * P], ident[:, :])
                for b in range(4):
                    ins = e.transpose(ptb[:, (4 + b) * P:(5 + b) * P], KH[:, b * P:(b + 1) * P], ident[:, :])
                return ins
            E("pe", trh, [rVT, rKH, rident], [rptb])
            VK, rVK = tb_.next()
            E("dve", lambda e, VK=VK, ptb=ptb: e.tensor_copy(out=VK[:, :], in_=ptb[:, 0:4 * P]), [rptb], [rVK])
            VK2, rVK2 = tb_.next()
            E("dve", lambda e, VK2=VK2, ptb=ptb: e.tensor_copy(out=VK2[:, :], in_=ptb[:, 4 * P:8 * P]), [rptb], [rVK2])
            return dict(locals())

        def hg_B(h, V):
            QD, rQD, KD, rKD = V.get("QD"), V.get("rQD"), V.get("KD"), V.get("rKD")
            VK, rVK, VK2, rVK2 = V["VK"], V["rVK"], V["VK2"], V["rVK2"]
            eb, reb = V["eb"], V["reb"]
            GSh, rGSh = V.get("GSh"), V.get("rGSh")
            if full:
                ys, rys = yst.next()
            for b in range(4):
                bs = slice(b * P, (b + 1) * P)
                if full:
                    pat, rpat = FB.next()
                    E("pe", lambda e, pat=pat, KD=KD, QD=QD, bs=bs: e.matmul(pat[:, 0:P], lhsT=KD[:, bs], rhs=QD[:, bs], start=True, stop=True), [rKD, rQD], [rpat])
                    ATT, rATT = tb_.next()
                    E("dve", lambda e, ATT=ATT, pat=pat: e.tensor_tensor(out=ATT[:, 0:P], in0=pat[:, 0:P], in1=trim[:, :], op=ALU.mult), [rpat, rtrim], [rATT])
                    po, rpo = FB.next()
                    E("pe", lambda e, po=po, VK=VK, ATT=ATT, b=b: e.matmul(po[:, 0:P], lhsT=VK[:, b * P:(b + 1) * P], rhs=ATT[:, 0:P], start=True, stop=False), [rVK, rATT], [rpo])
                for c in range(2):
                    cp = slice(64 * c, 64 * c + 64)
                    cur = sbp[h]
                    if full:
                        E("pe", lambda e, po=po, cur=cur, h=h, QD=QD, b=b, c=c: e.matmul(po[:, 64 * c:64 * c + 64], lhsT=Sbf[cur][:, h, :], rhs=QD[:, b * P + 64 * c:b * P + 64 * c + 64],
                                                                                         start=False, stop=(c == 1)), [rSbf[cur][h], rQD], [rpo])
                    pu, rpu = FB.next()
                    E("pe", lambda e, pu=pu, VK=VK, VK2=VK2, b=b, cp=cp: e.matmul(pu[:, 0:P], lhsT=VK2[cp, b * P:(b + 1) * P], rhs=VK[cp, b * P:(b + 1) * P], start=True, stop=True),
                      [rVK, rVK2], [rpu])
                    dcol = eb[:, b * P + 64 * c + 63:b * P + 64 * c + 64]
                    E("dve", lambda e, pu=pu, h=h, dcol=dcol: e.scalar_tensor_tensor(out=S32[:, h, :], in0=S32[:, h, :], scalar=dcol, in1=pu[:, 0:P], op0=ALU.mult, op1=ALU.add),
                      [rS32[h], reb, rpu], [rS32[h]])
                    nxt = 1 - cur
                    E("act", lambda e, nxt=nxt, h=h: acopy(e, out=Sbf[nxt][:, h, :], in_=S32[:, h, :]), [rS32[h]], [rSbf[nxt][h]])
                    sbp[h] = nxt
                if full:
                    if final:
                        pt_, rpt_ = parts.next()
                        dma(ph, "sp", pt_[:, 0:P], dr["ya_part"][h][:, g0 + (3 - b) * P:g0 + (4 - b) * P] if bwd else dr["ya_part"][h][:, g0 + b * P:g0 + (b + 1) * P], writes=[rpt_])
                        E("dve", lambda e, ys=ys, po=po, pt_=pt_, bs=bs: e.tensor_tensor(out=ys[:, bs], in0=po[:, 0:P], in1=src(pt_[:, 0:P]), op=ALU.add), [rpo, rpt_], [rys])
                    else:
                        E("act", lambda e, ys=ys, po=po, bs=bs: acopy(e, out=ys[:, bs], in_=po[:, 0:P]), [rpo], [rys])
            if full and not final:
                dma(ph, "sp", dr["ya_part"][h][:, g0:g0 + G], src(ys[:, :]) if bwd else ys[:, :], reads=[rys])
            if final:
                sq, rsq = tb_.next()
                E("act", lambda e, sq=sq, ys=ys: e.activation(out=sq[:, :], in_=ys[:, :], func=AF.Square), [rys], [rsq])
                pm, rpm = FB.next()
                E("pe", lambda e, pm=pm, sq=sq: e.matmul(pm[:, :], lhsT=ones_b[:, :], rhs=sq[:, :], start=True, stop=True), [rones, rsq], [rpm])
                vr, rvr = tf.next()
                E("dve", lambda e, vr=vr, pm=pm: e.tensor_scalar(out=vr[:, :], in0=pm[:, :], scalar1=1.0 / 128, scalar2=1e-6, op0=ALU.mult, op1=ALU.add), [rpm], [rvr])
                E("act", lambda e, vr=vr: e.activation(out=vr[:, :], in_=vr[:, :], func=AF.Sqrt), [rvr], [rvr])
                E("dve", lambda e, vr=vr: e.reciprocal(out=vr[:, :], in_=vr[:, :]), [rvr], [rvr])
                on = cst[:, C_ONORM + h:C_ONORM + h + 1]
                t3, rt3 = tf.next()
                E("dve", lambda e, t3=t3, ys=ys, on=on, vr=vr: e.scalar_tensor_tensor(out=t3[:, :], in0=ys[:, :], scalar=on, in1=vr[:, :], op0=ALU.mult, op1=ALU.mult), [rys, rcst, rvr], [rt3])
                yaT, ryaT = yaTs.next()
                E("pool", lambda e, t3=t3, yaT=yaT, GSh=GSh: e.tensor_tensor(out=src(yaT[:, :]), in0=t3[:, :], in1=GSh[:, :], op=ALU.mult), [rt3, rGSh], [ryaT])
                dma(ph, "sp", dr["yaT"][h][:, g0:g0 + G], yaT[:, :], reads=[ryaT])


        nh = 0 if dr.get("skip_hgrn") else 8
        VA = {}
        if nh:
            VA[0] = hg_A(0)
        for h in range(nh):
            if h + 1 < nh:
                VA[h + 1] = hg_A(h + 1)
            hg_B(h, VA.pop(h))

        for c in range(0 if dr.get("skip_rwkv") else 8):
            if full:
                pr, rpr, phr, rphr = proj(40 + c, True)
                rs_, rrs = shift(pr, rpr, phr, rphr, c)
            pk, rpk, phk, rphk = proj(48 + c, True)
            ks_, rks = shift(pk, rpk, phk, rphk, 8 + c)
            pv, rpv, phv, rphv = proj(56 + c, True)
            vs_, rvs = shift(pv, rpv, phv, rphv, 16 + c)
            cs128 = slice(c * P, (c + 1) * P)
            pzw, rpzw = FB.next()
            d0 = dirn * 64
            E("pe", lambda e, pzw=pzw, cs128=cs128: e.matmul(pzw[:, :], lhsT=w2sb[d0:d0 + 64, cs128], rhs=th[d0:d0 + 64, :], start=True, stop=True), [rw2, rth], [rpzw])
            sgw, rsgw = tf.next()
            E("act", lambda e, sgw=sgw, pzw=pzw, c=c: e.activation(out=sgw[:, :], in_=pzw[:, :], func=AF.Sigmoid, bias=cst[:, C_W0 + dirn * 8 + c:C_W0 + dirn * 8 + c + 1]), [rpzw, rcst], [rsgw])
            css, rcss = tf.next()
            E("dve", lambda e, css=css, sgw=sgw: e.tensor_tensor_scan(out=css[:, :], data0=rmask[:, :], data1=sgw[:, :], initial=0.0, op0=ALU.mult, op1=ALU.add), [rsgw, rrm], [rcss])
            ec, rec = tf.next()
            E("act", lambda e, ec=ec, css=css: e.activation(out=ec[:, :], in_=css[:, :], func=AF.Exp, scale=-K0), [rcss], [rec])
            enc, renc = tf.next()
            E("act", lambda e, enc=enc, css=css: e.activation(out=enc[:, :], in_=css[:, :], func=AF.Exp, scale=K0), [rcss], [renc])
            dlt, rdlt = tf.next()
            E("dve", lambda e, dlt=dlt, css=css, sgw=sgw: e.tensor_tensor(out=dlt[:, :], in0=css[:, :], in1=sgw[:, :], op=ALU.subtract), [rcss, rsgw], [rdlt])
            E("act", lambda e, dlt=dlt: e.activation(out=dlt[:, :], in_=dlt[:, :], func=AF.Exp, scale=-K0), [rdlt], [rdlt])
            pza, rpza = FB.next()
            E("pe", lambda e, pza=pza, cs128=cs128: e.matmul(pza[:, :], lhsT=a2sb[0:64, cs128], rhs=adb[0:64, :], start=True, stop=True), [ra2, radb], [rpza])
            asg, rasg = tf.next()
            E("act", lambda e, asg=asg, pza=pza, c=c: e.activation(out=asg[:, :], in_=pza[:, :], func=AF.Sigmoid, bias=cst[:, C_A0 + c:C_A0 + c + 1]), [rpza, rcst], [rasg])
            kk0, rkk0 = tf.next()
            E("act", lambda e, kk0=kk0, ks_=ks_, c=c: e.activation(out=kk0[:, :], in_=ks_[:, :], func=AF.Identity, scale=cst[:, C_KK + c:C_KK + c + 1]), [rks, rcst], [rkk0])
            sq, rsq = tb_.next()
            E("act", lambda e, sq=sq, kk0=kk0: e.activation(out=sq[:, :], in_=kk0[:, :], func=AF.Square), [rkk0], [rsq])
            pss, rpss = FB.next()
            E("pe", lambda e, pss=pss, sq=sq: e.matmul(pss[:, :], lhsT=blk64[:, :], rhs=sq[:, :], start=True, stop=True), [rblk64, rsq], [rpss])
            rn, rrn = tf.next()
            E("act", lambda e, rn=rn, pss=pss: e.activation(out=rn[:, :], in_=pss[:, :], func=AF.Sqrt), [rpss], [rrn])
            E("dve", lambda e, rn=rn: e.tensor_scalar(out=rn[:, :], in0=rn[:, :], scalar1=1e-12, scalar2=None, op0=ALU.max), [rrn], [rrn])
            E("dve", lambda e, rn=rn: e.reciprocal(out=rn[:, :], in_=rn[:, :]), [rrn], [rrn])
            E("dve", lambda e, kk0=kk0, rn=rn: e.tensor_tensor(out=kk0[:, :], in0=kk0[:, :], in1=rn[:, :], op=ALU.mult), [rkk0, rrn], [rkk0])
            km, rkm = tf.next()
            E("dve", lambda e, km=km, asg=asg, c=c: e.tensor_scalar(out=km[:, :], in0=asg[:, :], scalar1=-1.0, scalar2=cst[:, C_KA + c:C_KA + c + 1], op0=ALU.add, op1=ALU.mult), [rasg, rcst], [rkm])
            E("dve", lambda e, km=km, ks_=ks_: e.scalar_tensor_tensor(out=km[:, :], in0=km[:, :], scalar=1.0, in1=ks_[:, :], op0=ALU.add, op1=ALU.mult), [rkm, rks], [rkm])
            AR4 = AR[:, :, 0, :]
            E("dve", lambda e, kk0=kk0, dlt=dlt: e.scalar_tensor_tensor(out=AR[:, :, 0, :], in0=kk0[:, :].rearrange("p (n c) -> p n c", c=64), scalar=-1.0,
                                                                          in1=dlt[:, :].rearrange("p (n c) -> p n c", c=64), op0=ALU.mult, op1=ALU.mult), [rkk0, rdlt], [rAR])
            if full:
                E("dve", lambda e, rs_=rs_, ec=ec: e.tensor_tensor(out=AR[:, :, 1, :], in0=rs_[:, :].rearrange("p (n c) -> p n c", c=64),
                                                                    in1=ec[:, :].rearrange("p (n c) -> p n c", c=64), op=ALU.mult), [rrs, rec], [rAR])
            bv, rbv = tf.next()
            E("dve", lambda e, bv=bv, kk0=kk0, asg=asg: e.tensor_tensor(out=bv[:, :], in0=kk0[:, :], in1=asg[:, :], op=ALU.mult), [rkk0, rasg], [rbv])
            BT, rBT = tb_.next()
            E("dve", lambda e, BT=BT, bv=bv, enc=enc: e.tensor_tensor(out=BT[:, :], in0=bv[:, :], in1=enc[:, :], op=ALU.mult), [rbv, renc], [rBT])
            KT, rKT = tb_.next()
            E("dve", lambda e, KT=KT, km=km, enc=enc: e.tensor_tensor(out=KT[:, :], in0=km[:, :], in1=enc[:, :], op=ALU.mult), [rkm, renc], [rKT])
            ecl = bcast_last(ec[:, :].rearrange("p (n c) -> p n c", c=64)[:, :, 63:64], 64)
            BH, rBH = tb_.next()
            E("dve", lambda e, BH=BH, BT=BT, ecl=ecl: e.tensor_tensor(out=BH[:, :].rearrange("p (n c) -> p n c", c=64), in0=BT[:, :].rearrange("p (n c) -> p n c", c=64), in1=ecl, op=ALU.mult), [rBT, rec], [rBH])
            KH_, rKH_ = tb_.next()
            E("dve", lambda e, KH_=KH_, KT=KT, ecl=ecl: e.tensor_tensor(out=KH_[:, :].rearrange("p (n c) -> p n c", c=64), in0=KT[:, :].rearrange("p (n c) -> p n c", c=64), in1=ecl, op=ALU.mult), [rKT, rec], [rKH_])
            VTr, rVTr = tb_.next()
            E("dve", lambda e, VTr=VTr, vs_=vs_: e.tensor_tensor(out=VTr[:, :], in0=vs_[:, :], in1=src(mkg[:, :]), op=ALU.mult), [rvs, rmkg], [rVTr])
            tap("AR", AR[:].rearrange("p n a c -> p (n a c)"), rAR)
            tap("BT", BT[:, :], rBT)
            tap("KT", KT[:, :], rKT)
            tap("ec", ec[:, :], rec)
            tap("enc", enc[:, :], renc)
            tap("kkn", kk0[:, :], rkk0)
            tap("asg", asg[:, :], rasg)
            tap("ks", ks_[:, :], rks)
            E("act", lambda e, ec=ec: acopy(e, out=PCt[:, :], in_=ec[:, 63:G:64]), [rec], [rPCt])
            E("act", lambda e: acopy(e, out=PC[:, 0, :], in_=PCt[0:64, :]), [rPCt], [rPC])
            dma(ph, "sp", PC[:, 1, :], PCt[64:128, :], reads=[rPCt], writes=[rPC])
            if full:
                E("act", lambda e: acopy(e, out=RT0[:, 0, :].rearrange("p (n c) -> p n c", c=64), in_=AR[0:64, :, 1, :]), [rAR], [rRT0])
                dma(ph, "sp", RT0[:, 1, :].rearrange("p (n c) -> p n c", c=64), AR[64:128, :, 1, :], reads=[rAR], writes=[rRT0])
            if final:
                tq, rtq = tb_.next()
                E("dve", lambda e, tq=tq, rs_=rs_, km=km, c=c: e.scalar_tensor_tensor(out=tq[:, :], in0=rs_[:, :], scalar=cst[:, C_RK + c:C_RK + c + 1], in1=km[:, :], op0=ALU.mult, op1=ALU.mult), [rrs, rkm, rcst], [rtq])
                pbn, rpbn = FB.next()
                E("pe", lambda e, pbn=pbn, tq=tq: e.matmul(pbn[:, :], lhsT=blk64[:, :], rhs=tq[:, :], start=True, stop=True), [rblk64, rtq], [rpbn])
                bon, rbon = tb_.next()
                E("dve", lambda e, bon=bon, pbn=pbn, vs_=vs_: e.tensor_tensor(out=bon[:, :], in0=pbn[:, :], in1=vs_[:, :], op=ALU.mult), [rpbn, rvs], [rbon])
            for nm, srcT, rsrc in (("ATM", None, rAR), ("BHM", BH, rBH), ("KHM", KH_, rKH_), ("VTM", VTr, rVTr)):
                ptb, rptb = TB.next()

                def trr(e, ptb=ptb, srcT=srcT):
                    ins = None
                    for n in range(8):
                        in_ = AR[:, n, 0, :] if srcT is None else srcT[:, n * 64:(n + 1) * 64]
                        ins = e.transpose(ptb[0:64, n * P:(n + 1) * P], in_, ident[:, :])
                    return ins
                E("pe", trr, [rsrc, rident], [rptb])
                dst, rdst = tmq[nm]
                E("act" if nm in ("ATM", "KHM") else "dve",
                  (lambda e, dst=dst, ptb=ptb: acopy(e, out=dst[:, :, :], in_=ptb[0:64, :].rearrange("p (n c) -> p n c", c=P))) if nm in ("ATM", "KHM") else
                  (lambda e, dst=dst, ptb=ptb: e.tensor_copy(out=dst[:, :, :], in_=ptb[0:64, :].rearrange("p (n c) -> p n c", c=P))),
                  [rptb], [rdst])
            ATM, rATM = tmq["ATM"]
            BHM, rBHM = tmq["BHM"]
            KHM, rKHM = tmq["KHM"]
            VTM, rVTM = tmq["VTM"]
            if full:
                dma(ph, "sp", AR1[:, :, :, :], AR[64:128, :, :, :], reads=[rAR], writes=[rAR1])
            else:
                dma(ph, "sp", AR1[:, :, 0, :], AR[64:128, :, 0, :], reads=[rAR], writes=[rAR1])
            dma(ph, "sp", BK1[:, 0, :], BT[64:128, :], reads=[rBT], writes=[rBK1])
            dma(ph, "sp", BK1[:, 1, :], KT[64:128, :], reads=[rKT], writes=[rBK1])
            BT3 = [BT[0:64, :].rearrange("p (n c) -> p n c", c=64), BK1[:, 0, :].rearrange("p (n c) -> p n c", c=64)]
            KT3 = [KT[0:64, :].rearrange("p (n c) -> p n c", c=64), BK1[:, 1, :].rearrange("p (n c) -> p n c", c=64)]
            ARh = [AR[0:64, :, :, :], AR1[:, :, :, :]]
            ncol = P if full else 64

            def hs_(hh):
                return slice(hh * 64, hh * 64 + 64)
            for (lhs3, rl, Mdst, rMd) in ((BT3, rBT, M1, rM1), (KT3, rKT, M2, rM2)):
                rl = [rl, rBK1, rAR1]
                for q4 in range(4):
                    pm_, rpm_ = FB.next()

                    def mmM(e, pm_=pm_, lhs3=lhs3, q4=q4):
                        ins = None
                        for j in range(4):
                            b = q4 * 4 + j
                            n, hh = b // 2, b % 2
                            rhs = ARh[hh][:, n, :, :] if full else ARh[hh][:, n, 0, :]
                            ins = e.matmul(pm_[0:64, j * P:j * P + ncol], lhsT=lhs3[hh][:, n, :], rhs=rhs, start=True, stop=True)
                        return ins
                    E("pe", mmM, rl + [rAR], [rpm_])
                    E("dve", lambda e, pm_=pm_, Mdst=Mdst, q4=q4: e.tensor_tensor(out=Mdst[:, q4 * 4:q4 * 4 + 4, 0:ncol], in0=pm_[0:64, :].rearrange("p (j c) -> p j c", c=P)[:, :, 0:ncol],
                                                                                   in1=mskM[:, :].rearrange("p (j c) -> p j c", c=P)[:, :, 0:ncol], op=ALU.mult), [rpm_, rmskM], [rMd])
            for q8 in range(2):
                px, rpx = FB.next()

                def mmX(e, px=px, q8=q8):
                    ins = None
                    for j in range(8):
                        b = q8 * 8 + j
                        n, hh = b // 2, b % 2
                        ins = e.matmul(px[0:64, j * 64:(j + 1) * 64], lhsT=ARh[hh][:, n, 0, :], rhs=BT3[hh][:, n, :], start=True, stop=True)
                    return ins
                E("pe", mmX, [rAR, rBT, rAR1, rBK1], [rpx])
                E("dve", lambda e, px=px, q8=q8: e.tensor_tensor(out=Xs[0][:, q8 * 8:q8 * 8 + 8, :], in0=px[0:64, :].rearrange("p (j c) -> p j c", c=64),
                                                                   in1=mskX[:, :].rearrange("p (j c) -> p j c", c=64), op=ALU.mult), [rpx, rmskX], [rXs[0]])
            E("pool", lambda e: e.tensor_copy(out=XTs[0][:, :, :], in_=M1[:, :, 0:64]), [rM1], [rXTs[0]])
            E("pool", lambda e: e.tensor_copy(out=Z[:, :, 0:64].rearrange("p (n h) c -> p n h c", h=2), in_=ATM[:, :, :].rearrange("p n (h c) -> p n h c", h=2)), [rATM], [rZ])
            for q8 in range(2):
                pz0, rpz0 = FB.next()

                def mmZ0(e, pz0=pz0, q8=q8):
                    ins = None
                    for j in range(8):
                        b = q8 * 8 + j
                        n, hh = b // 2, b % 2
                        ins = e.matmul(pz0[0:64, j * 64:(j + 1) * 64], lhsT=M2[:, b, 0:64], rhs=VTM[:, n, hs_(hh)], start=True, stop=True)
                    return ins
                E("pe", mmZ0, [rM2, rVTM], [rpz0])
                E("act", lambda e, pz0=pz0, q8=q8: acopy(e, out=Z[:, q8 * 8:q8 * 8 + 8, 64:P], in_=pz0[0:64, :].rearrange("p (j c) -> p j c", c=64)), [rpz0], [rZ])
            tap("M1", M1[:].rearrange("p b c -> p (b c)"), rM1)
            tap("M2", M2[:].rearrange("p b c -> p (b c)"), rM2)
            tap("X0", Xs[0][:].rearrange("p b c -> p (b c)"), rXs[0])
            tap("Z0", Z[:].rearrange("p b c -> p (b c)"), rZ)
            cur = 0
            for it in range(6):
                if it < 5:
                    nxt = 1 - cur
                    for q8 in range(2):
                        p1_, rp1_ = FB.next()
                        p2_, rp2_ = FB.next()

                        def mmS(e, p1_=p1_, p2_=p2_, q8=q8, cur=cur):
                            ins = None
                            for j in range(8):
                                b = q8 * 8 + j
                                e.matmul(p1_[0:64, j * 64:(j + 1) * 64], lhsT=Xs[cur][:, b, :], rhs=XTs[cur][:, b, :], start=True, stop=True)
                                ins = e.matmul(p2_[0:64, j * 64:(j + 1) * 64], lhsT=XTs[cur][:, b, :], rhs=Xs[cur][:, b, :], start=True, stop=True)
                            return ins
                        E("pe", mmS, [rXs[cur], rXTs[cur]], [rp1_, rp2_])
                        E("act", lambda e, p1_=p1_, q8=q8, nxt=nxt: acopy(e, out=XTs[nxt][:, q8 * 8:q8 * 8 + 8, :], in_=p1_[0:64, :].rearrange("p (j c) -> p j c", c=64)), [rp1_], [rXTs[nxt]])
                        E("act", lambda e, p2_=p2_, q8=q8, nxt=nxt: acopy(e, out=Xs[nxt][:, q8 * 8:q8 * 8 + 8, :], in_=p2_[0:64, :].rearrange("p (j c) -> p j c", c=64)), [rp2_], [rXs[nxt]])
                for q4 in range(4):
                    pa_, rpa_ = FB.next()

                    def mmA(e, pa_=pa_, q4=q4, cur=cur):
                        ins = None
                        for j in range(4):
                            b = q4 * 4 + j
                            ins = e.matmul(pa_[0:64, j * P:(j + 1) * P], lhsT=XTs[cur][:, b, :], rhs=Z[:, b, :], start=True, stop=True)
                        return ins
                    E("pe", mmA, [rXTs[cur], rZ], [rpa_])
                    E("dve", lambda e, pa_=pa_, q4=q4: e.tensor_tensor(out=Z[:, q4 * 4:q4 * 4 + 4, :], in0=pa_[0:64, :].rearrange("p (j c) -> p j c", c=P),
                                                                        in1=Z[:, q4 * 4:q4 * 4 + 4, :], op=ALU.add), [rpa_, rZ], [rZ])
                if it < 5:
                    cur = nxt
            if full:
                for q8 in range(2):
                    pq_, rpq_ = FB.next()

                    def mmQ(e, pq_=pq_, q8=q8):
                        ins = None
                        for j in range(8):
                            b = q8 * 8 + j
                            ins = e.matmul(pq_[0:64, j * 64:(j + 1) * 64], lhsT=Z[:, b, 0:64], rhs=M1[:, b, 64:P], start=True, stop=True)
                        return ins
                    E("pe", mmQ, [rZ, rM1], [rpq_])
                    E("dve", lambda e, pq_=pq_, q8=q8: e.tensor_tensor(out=QE[:, q8 * 8:q8 * 8 + 8, :].rearrange("p (n h) c -> p n h c", h=2),
                                                                        in0=pq_[0:64, :].rearrange("p (n h c) -> p n h c", h=2, c=64),
                                                                        in1=RT0[:, :, q8 * 256:(q8 + 1) * 256].rearrange("p h (n c) -> p n h c", c=64), op=ALU.add), [rpq_, rRT0], [rQE])
            for q8 in range(2):
                pg_, rpg_ = FB.next()
                ph2, rph2 = FB.next()

                def mmG(e, pg_=pg_, ph2=ph2, q8=q8):
                    ins = None
                    for j in range(8):
                        b = q8 * 8 + j
                        n, hh = b // 2, b % 2
                        e.matmul(pg_[0:64, j * 64:(j + 1) * 64], lhsT=Z[:, b, 0:64], rhs=BHM[:, n, hs_(hh)], start=True, stop=True)
                        e.matmul(ph2[0:64, j * 64:(j + 1) * 64], lhsT=BHM[:, n, hs_(hh)], rhs=Z[:, b, 64:P], start=True, stop=False)
                        ins = e.matmul(ph2[0:64, j * 64:(j + 1) * 64], lhsT=KHM[:, n, hs_(hh)], rhs=VTM[:, n, hs_(hh)], start=False, stop=True)
                    return ins
                E("pe", mmG, [rZ, rBHM, rKHM, rVTM], [rpg_, rph2])
                E("act", lambda e, pg_=pg_, q8=q8: acopy(e, out=GT[:, q8 * 8:q8 * 8 + 8, :], in_=pg_[0:64, :].rearrange("p (j c) -> p j c", c=64)), [rpg_], [rGT])
                E("dve", lambda e, ph2=ph2, q8=q8: e.tensor_copy(out=HI[:, q8 * 8:q8 * 8 + 8, :], in_=ph2[0:64, :].rearrange("p (j c) -> p j c", c=64)), [rph2], [rHI])
            if full:
                pys = PY
            for n in range(8):
                for hh in range(2):
                    b = n * 2 + hh
                    h = 2 * c + hh
                    curh = hbp[h]
                    if full:
                        py, rpy = pys[b // 8]
                        j = b % 8

                        def mmY(e, py=py, j=j, b=b, n=n, hh=hh, h=h, curh=curh):
                            e.matmul(py[0:64, j * 64:(j + 1) * 64], lhsT=Z[:, b, 64:P], rhs=M1[:, b, 64:P], start=True, stop=False)
                            e.matmul(py[0:64, j * 64:(j + 1) * 64], lhsT=VTM[:, n, hs_(hh)], rhs=M2[:, b, 64:P], start=False, stop=False)
                            return e.matmul(py[0:64, j * 64:(j + 1) * 64], lhsT=Hbf[curh][:, h, :], rhs=QE[:, b, :], start=False, stop=True)
                        E("pe", mmY, [rZ, rM1, rM2, rVTM, rQE, rHbf[curh][h]], [rpy])
                    pn_, rpn_ = FB.next()

                    def mmH(e, pn_=pn_, b=b, h=h, curh=curh):
                        e.matmul(pn_[0:64, 0:64], lhsT=GT[:, b, :], rhs=Hbf[curh][:, h, :], start=True, stop=False)
                        return e.matmul(pn_[0:64, 0:64], lhsT=ident[0:64, 0:64], rhs=HI[:, b, :], start=False, stop=True)
                    E("pe", mmH, [rGT, rHI, rident, rHbf[curh][h]], [rpn_])
                    nxh = 1 - curh
                    E("dve", lambda e, pn_=pn_, h=h, hh=hh, n=n, nxh=nxh: e.scalar_tensor_tensor(out=Hbf[nxh][:, h, :], in0=H32[:, h, :], scalar=PC[:, hh, n:n + 1], in1=pn_[0:64, 0:64], op0=ALU.mult, op1=ALU.add),
                      [rH32[h], rPC, rpn_], [rHbf[nxh][h]])
                    E("dve", lambda e, pn_=pn_, h=h, hh=hh, n=n: e.scalar_tensor_tensor(out=H32[:, h, :], in0=H32[:, h, :], scalar=PC[:, hh, n:n + 1], in1=pn_[0:64, 0:64], op0=ALU.mult, op1=ALU.add),
                      [rH32[h], rPC, rpn_], [rH32[h]])
                    hbp[h] = nxh
            if full:
                ysr, rysr = ysrs.next()

                def ysr_view(hh, q8, ysr=ysr):
                    return ysr[:, hh, q8 * 256:(q8 + 1) * 256].rearrange("p (n c) -> p n c", c=64)
                for q8 in range(2):
                    py, rpy = pys[q8]
                    for hh in range(2):
                        pyv = py[0:64, :].rearrange("p (n h c) -> p n h c", h=2, c=64)[:, :, hh, :]
                        if final:
                            pbuf, rpbuf = parts.next()
                            lo = g0 + (1 - q8) * 256 if bwd else g0 + q8 * 256
                            dma(ph, "sp", pbuf[0:64, 0:256], dr["yb_part"][2 * c + hh][:, lo:lo + 256], writes=[rpbuf])
                            E("dve", lambda e, pyv=pyv, pbuf=pbuf, hh=hh, q8=q8, ysr_view=ysr_view: e.tensor_tensor(
                                out=ysr_view(hh, q8), in0=pyv, in1=src(pbuf[0:64, 0:256]).rearrange("p (n c) -> p n c", c=64), op=ALU.add), [rpy, rpbuf], [rysr])
                        else:
                            E("act", lambda e, pyv=pyv, hh=hh, q8=q8, ysr_view=ysr_view: acopy(e, out=ysr_view(hh, q8), in_=pyv), [rpy], [rysr])
                if not final:
                    for hh in range(2):
                        dma(ph, "sp", dr["yb_part"][2 * c + hh][:, g0:g0 + G], src(ysr[:, hh, :]) if bwd else ysr[:, hh, :], reads=[rysr])
                if final:
                    dma(ph, "sp", bon1[:, :], bon[64:128, :], reads=[rbon], writes=[rbon1])
                    ybT, rybT = ybTs.next()
                    for hh in range(2):
                        h = 2 * c + hh
                        yv = ysr[:, hh, :]
                        ybf, rybf = tb_.next()
                        E("act", lambda e, ybf=ybf, yv=yv: acopy(e, out=ybf[0:64, :], in_=yv), [rysr], [rybf])
                        pm, rpm = FB.next()
                        E("pe", lambda e, pm=pm, ybf=ybf: e.matmul(pm[0:64, :], lhsT=ones64[:, :], rhs=ybf[0:64, :], start=True, stop=True), [rones64, rybf], [rpm])
                        dd, rdd = tf.next()
                        E("dve", lambda e, dd=dd, yv=yv, pm=pm: e.tensor_tensor(out=dd[0:64, :], in0=yv, in1=pm[0:64, :], op=ALU.subtract), [rysr, rpm], [rdd])
                        dsq, rdsq = tb_.next()
                        E("act", lambda e, dsq=dsq, dd=dd: e.activation(out=dsq[0:64, :], in_=dd[0:64, :], func=AF.Square), [rdd], [rdsq])
                        pv2, rpv2 = FB.next()
                        E("pe", lambda e, pv2=pv2, dsq=dsq: e.matmul(pv2[0:64, :], lhsT=ones64[:, :], rhs=dsq[0:64, :], start=True, stop=True), [rones64, rdsq], [rpv2])
                        vr, rvr = tf.next()
                        E("dve", lambda e, vr=vr, pv2=pv2: e.tensor_scalar(out=vr[0:64, :], in0=pv2[0:64, :], scalar1=64e-5, scalar2=None, op0=ALU.add), [rpv2], [rvr])
                        E("act", lambda e, vr=vr: e.activation(out=vr[0:64, :], in_=vr[0:64, :], func=AF.Sqrt), [rvr], [rvr])
                        E("dve", lambda e, vr=vr: e.reciprocal(out=vr[0:64, :], in_=vr[0:64, :]), [rvr], [rvr])
                        E("dve", lambda e, dd=dd, vr=vr: e.tensor_tensor(out=dd[0:64, :], in0=dd[0:64, :], in1=vr[0:64, :], op=ALU.mult), [rdd, rvr], [rdd])
                        E("act", lambda e, dd=dd, h=h: e.activation(out=dd[0:64, :], in_=dd[0:64, :], func=AF.Identity, scale=lnc[:, h:h + 1], bias=lnc[:, 16 + h:17 + h]), [rdd, rlnc], [rdd])
                        if hh == 0:
                            E("dve", lambda e, dd=dd, bon=bon: e.tensor_tensor(out=dd[0:64, :], in0=dd[0:64, :], in1=bon[0:64, :], op=ALU.add), [rdd, rbon], [rdd])
                        else:
                            E("dve", lambda e, dd=dd: e.tensor_tensor(out=dd[0:64, :], in0=dd[0:64, :], in1=bon1[:, :], op=ALU.add), [rdd, rbon1], [rdd])
                        pgg, rpgg = FB.next()

                        def mmg(e, pgg=pgg, h=h):
                            e.matmul(pgg[0:64, :], lhsT=g2a[:, h * 64:(h + 1) * 64], rhs=sgd[:, 0, :], start=True, stop=False)
                            return e.matmul(pgg[0:64, :], lhsT=g2b[0:32, h * 64:(h + 1) * 64], rhs=sgd[0:32, 1, :], start=False, stop=True)
                        E("pe", mmg, [rg2a, rg2b, rsgd], [rpgg])
                        E("dve", lambda e, dd=dd, pgg=pgg, hh=hh, ybT=ybT: e.tensor_tensor(out=src(ybT[:, hh, :]), in0=dd[0:64, :], in1=pgg[0:64, :], op=ALU.mult), [rdd, rpgg], [rybT])
                    dma(ph, "sp", dr["ybT"].rearrange("h p t -> p h t")[:, 2 * c:2 * c + 2, g0:g0 + G], ybT[:, :, :], reads=[rybT])
    if dr.get("st_out") is not None:
        dma(ph, "sp", dr["st_out"][0][:, :], S32[:].rearrange("p h e -> p (h e)"), reads=rS32)
        dma(ph, "sp", dr["st_out"][1][:, :], H32[:].rearrange("p h e -> p (h e)"), reads=rH32)
    ph.close()


def outproj_phase(nc, tag, groups, x_d, maskcol_d, yaT_d, ybT_d, woA_d, woB_d, xout_d):
    ph = Phase(nc, tag)
    S = ph.S
    mc, rmc = load_plain(ph, maskcol_d[:, :], [P, maskcol_d.shape[1]], F32, "maskcol")
    yas = RR(ph, 2, [P, 8, G], BF16, "ya")
    ybs = RR(ph, 2, [64, 16, G], BF16, "yb")
    was = RR(ph, 2, [P, 8, 512], BF16, "woA")
    wbs = RR(ph, 2, [64, 16, 512], BF16, "woB")
    xss = RR(ph, 2, [P, 4, 512], F32, "xs")
    pws = RR(ph, 2, [P, G], F32, "pw", psum=True)
    ya_v = yaT_d.rearrange("h p t -> p h t")
    yb_v = ybT_d.rearrange("h p t -> p h t")
    for g in groups:
        g0 = g * G
        ya, rya = yas.next()
        dma(ph, "sp", ya[:], ya_v[:, :, g0:g0 + G], writes=[rya])
        yb, ryb = ybs.next()
        dma(ph, "sp", yb[:], yb_v[:, :, g0:g0 + G], writes=[ryb])
        for cbk in range(4):
            wa, rwa = was.next()
            dma(ph, "sp", wa[:], woA_d[cbk], writes=[rwa])
            wb, rwb = wbs.next()
            dma(ph, "sp", wb[:], woB_d[cbk], writes=[rwb])
            xs, rxs = xss.next()
            cs = slice(cbk * 512, (cbk + 1) * 512)
            dma(ph, "sp", xs[:], x_d[g0:g0 + G, cs].rearrange("(t p) d -> p t d", p=P), writes=[rxs])
            for t in range(4):
                pw, rpw = pws.next()

                def mmo(e, pw=pw, ya=ya, yb=yb, wa=wa, wb=wb, t=t):
                    for kc in range(8):
                        e.matmul(pw[:, :], lhsT=ya[:, kc, t * P:(t + 1) * P], rhs=wa[:, kc, :], start=(kc == 0), stop=False)
                    ins = None
                    for h in range(16):
                        ins = e.matmul(pw[:, :], lhsT=yb[0:64, h, t * P:(t + 1) * P], rhs=wb[0:64, h, :], start=False, stop=(h == 15))
                    return ins
                S.op("pe", mmo, reads=[rya, ryb, rwa, rwb], writes=[rpw])
                S.op("dve", lambda e, xs=xs, pw=pw, t=t: e.tensor_tensor(out=xs[:, t, :], in0=pw[:, :], in1=xs[:, t, :], op=ALU.add), reads=[rpw, rxs], writes=[rxs])
                S.op("act", lambda e, xs=xs, t=t, g=g: e.activation(out=xs[:, t, :], in_=xs[:, t, :], func=AF.Identity, scale=mc[:, g * 4 + t:g * 4 + t + 1]),
                     reads=[rxs, rmc], writes=[rxs])
            dma(ph, "sp", xout_d[g0:g0 + G, cs].rearrange("(t p) d -> p t d", p=P), xs[:], reads=[rxs])
    ph.close()


def final_phase(nc, tag, groups, xin_d, g_d, y_d):
    ph = Phase(nc, tag)
    S = ph.S
    gbc, rg = load_bcast(ph, g_d, D, "gbc")
    nx = NormCtx(ph, None, None, with_T=False)
    xts = RR(ph, 2, [P, 4, D], F32, "xt")
    for g in groups:
        g0 = g * G
        xt, rxt = xts.next()
        dma(ph, "sp", xt[:], xin_d[g0:g0 + G, :].rearrange("(t p) d -> p t d", p=P), writes=[rxt])
        for t in range(4):
            st, rst = rms_stats(ph, nx, xt[:, t, :], rxt)
            S.op("dve", lambda e, xt=xt, t=t, st=st: e.scalar_tensor_tensor(out=xt[:, t, :], in0=xt[:, t, :], scalar=st[:, 3:4], in1=gbc[:, :],
                                                                            op0=ALU.mult, op1=ALU.mult), reads=[rxt, rst, rg], writes=[rxt])
        dma(ph, "sp", y_d[g0:g0 + G, :].rearrange("(t p) d -> p t d", p=P), xt[:], reads=[rxt])
    ph.close()


def mix_masks():
    u = np.arange(G)
    rmask = np.broadcast_to((u % 64 != 0).astype(np.float32)[None, :], (P, G)).copy()
    j = np.arange(P)[:, None]
    i = np.arange(P)[None, :]
    trimask = ((j // 64 == i // 64) & (j <= i)).astype(np.float32)
    s_ = np.arange(64)[:, None]
    t_ = np.arange(64)[None, :]
    mM = np.concatenate([(s_ < t_), (s_ <= t_)], axis=1).astype(np.float32)
    maskM = np.tile(mM, (1, 4))
    mX = (t_.T > s_.T).astype(np.float32)
    mX = (np.arange(64)[:, None] > np.arange(64)[None, :]).astype(np.float32)
    maskX = np.tile(mX, (1, 8))
    ones = np.ones((P, P), np.float32)
    blk64 = (np.arange(P)[:, None] // 64 == np.arange(P)[None, :] // 64).astype(np.float32)
    shift = np.zeros((P, 64), np.float32)
    shift[64 + np.arange(64), np.arange(64)] = 1.0
    ones64 = np.full((64, 64), 1.0 / 64, np.float32)
    return dict(rmask=rmask, trimask=trimask, maskM=maskM, maskX=maskX, ones=ones, blk64=blk64, shift=shift, ones64=ones64,
                ident=np.eye(P, dtype=np.float32))


def lay_mix(inp, d0, d1):
    w = inp["ab_w_in"][0]
    HWd = 1024
    pa = [w[:, i * HWd:(i + 1) * HWd] for i in range(5)]
    pb = w[:, 5 * HWd:]
    r_, k_, v_ = pb[:, 0:1024], pb[:, 1024:2048], pb[:, 2048:3072]
    wd = [pb[:, 3072:3136], pb[:, 3136:3200]]
    ad = pb[:, 3200:3264]
    gd = pb[:, 3264:3424]
    z64 = np.zeros((D, 64), np.float32)
    cols = [pa[0], pa[1], pa[2 + d0], pa[2 + d1], pa[4], r_, k_, v_,
            np.concatenate([wd[d0], wd[d1]], 1), np.concatenate([ad, z64], 1), gd[:, 0:128],
            np.concatenate([gd[:, 128:160], np.zeros((D, 96), np.float32)], 1)]
    w_in = lay_ws(np.concatenate(cols, axis=1)).reshape(NCH * P, KC * P)
    mu = inp["rwkv_mu"][0]
    z = np.zeros(64, np.float32)
    mus = [mu[0:1024], mu[1024:2048], mu[2048:3072]]
    mul = [np.concatenate([mu[3072 + d0 * 64:3136 + d0 * 64], mu[3072 + d1 * 64:3136 + d1 * 64]]), np.concatenate([mu[3200:3264], z]),
           mu[3264:3392], np.concatenate([mu[3392:3424], np.zeros(96, np.float32)])]
    lb = inp["hgrn_lb"]
    lbt = np.concatenate([lay_cols(lb[r]) for r in range(3)], axis=1)
    mixc = np.concatenate([lbt, lay_cols(inp["hgrn_onorm"][0])] + [lay_cols(m) for m in mus] + [m.reshape(P, 1) for m in mul] +
                          [lay_cols(inp["rwkv_w0"][0][d0]), lay_cols(inp["rwkv_w0"][0][d1]), lay_cols(inp["rwkv_a0"][0]),
                           lay_cols(inp["rwkv_kk"][0]), lay_cols(inp["rwkv_ka"][0]), lay_cols(inp["rwkv_rk"][0].reshape(-1))], axis=1)
    assert mixc.shape == (P, 108), mixc.shape
    w2 = np.concatenate([inp["rwkv_w2"][0][d0], inp["rwkv_w2"][0][d1]], axis=0)
    lnc = np.concatenate([inp["rwkv_ln_w"][0].reshape(16, 64).T, inp["rwkv_ln_b"][0].reshape(16, 64).T], axis=1)
    wo = inp["ab_w_out"][0]
    woA = lay_as(wo[:1024], 512)
    woB = np.ascontiguousarray(wo[1024:].reshape(16, 64, 4, 512).transpose(2, 1, 0, 3))
    return dict(w_in=w_in, mixc=np.ascontiguousarray(mixc), w2=np.ascontiguousarray(w2), a2=np.ascontiguousarray(inp["rwkv_a2"][0]),
                g2=np.ascontiguousarray(inp["rwkv_g2"][0]), lnc=np.ascontiguousarray(lnc),
                woA=woA.reshape(4 * P, 8 * 512), woB=woB.reshape(4 * 64, 16 * 512))


IN_SHAPES = None


def input_shapes(NTOT, L):
    return {
        "xs": [NTOT, D], "maskrow": [1, NTOT], "maskcol": [P, NTOT // P], "cos": [P, L], "sin": [P, L], "kbias": [1, L + 256],
        "mixn0": [1, D], "ffnn0": [1, D], "mixn1": [1, D], "ffnn1": [1, D], "finn": [1, D],
        "w_in": [NCH * P, KC * P], "mixc": [P, 108], "w2": [P, 1024], "a2": [64, 1024], "g2": [160, 1024], "lnc": [64, 32],
        "woA": [4 * P, 8 * 512], "woB": [4 * 64, 16 * 512],
        "rmask": [P, G], "trimask": [P, P], "maskM": [64, 512], "maskX": [64, 512], "ones": [P, P], "blk64": [P, P], "ones64": [64, 64], "ident": [P, P],
        "wqk": [40 * P, KC * P], "wv": [P, KC * 512], "wo": [4 * P, KC * 512], "band": [P, 384], "sink": [1, 16],
        "up0": [2 * FC * P, KC * P], "up1": [2 * FC * P, KC * P], "dn0": [8 * P, FC * 256], "dn1": [8 * P, FC * 256],
        "conv0": [P, 4 * FC], "conv1": [P, 4 * FC],
    }


def build(NTOT, L, NOUT):
    nc = bass.Bass("TRN2", target_bir_lowering=False)
    dr = {k: nc.dram_tensor(k, list(v), F32, kind="ExternalInput").ap() for k, v in input_shapes(NTOT, L).items()}
    y = nc.dram_tensor("y", [NOUT, D], F32, kind="ExternalOutput").ap()

    def scr(name, shape, dt=BF16):
        return nc.dram_tensor(name, list(shape), dt, kind="Internal").ap()
    wb = {k: scr(k + "_b", dr[k].shape) for k in ("w_in", "woA", "woB", "wqk", "wv", "wo", "up0", "up1", "dn0", "dn1")}
    hT0 = scr("hT0", [KC, P, NTOT])
    ya_part = scr("ya_part", [8, P, L], F32)
    yb_part = scr("yb_part", [16, 64, L], F32)
    yaT = scr("yaT", [8, P, L])
    ybT = scr("ybT", [16, 64, L])
    st_hg = scr("st_hg", [P, 1024], F32)
    st_rw = scr("st_rw", [64, 1024], F32)
    x1 = scr("x1", [L, D], F32)
    x2 = scr("x2", [L, D], F32)
    x3 = scr("x3", [L, D], F32)
    x4 = scr("x4", [L, D], F32)
    hT1 = scr("hT1", [KC, P, L])
    hT3 = scr("hT3", [KC, P, L])
    qT = scr("qT", [16, P, L])
    kT = scr("kT", [4, P, L + 256])
    vv = scr("vv", [L + 256, 512])
    NG, NGL, NGO = NTOT // G, L // G, NOUT // G

    cast_phase(nc, "pw", [(wb[k], dr[k]) for k in wb])
    normT_phase(nc, "p0", range(NG), dr["xs"], dr["mixn0"], hT0, dr["ident"])
    d2 = dict(dr)
    d2.update(hT0=hT0, w_in=wb["w_in"].rearrange("(o p) (k c) -> o p k c", p=P, c=P),
              ya_part=[ya_part[h] for h in range(8)], yb_part=[yb_part[h] for h in range(16)], yaT=yaT, ybT=ybT)
    if NG > NGL:
        d1 = dict(d2)
        d1["st_out"] = (st_hg, st_rw)
        mix_phase(nc, "p1", list(range(NGL, NG))[::-1], True, False, False, NTOT, d1)
    mix_phase(nc, "p2", list(range(NGL)), False, True, False, NTOT, d2)
    d3 = dict(d2)
    if NG > NGL:
        d3["st_in"] = (st_hg, st_rw)
    mix_phase(nc, "p3", list(range(NGL))[::-1], True, True, True, NTOT, d3)
    outproj_phase(nc, "p3b", range(NGL), dr["xs"], dr["maskcol"], yaT, ybT,
                  wb["woA"].rearrange("(n p) (k c) -> n p k c", p=P, c=512), wb["woB"].rearrange("(n p) (k c) -> n p k c", p=64, c=512), x1)
    normT_phase(nc, "p3c", range(NGL), x1, dr["ffnn0"], hT1, dr["ident"])
    ffn_phase(nc, "p4", range(NGL), x1, x2, hT1, wb["up0"].rearrange("(o p) (k c) -> o p k c", p=P, c=P),
              wb["dn0"].rearrange("(o p) (k c) -> o p k c", p=P, c=256), dr["conv0"])
    qkv_phase(nc, "p4b", range(NGL), x2, dr["mixn1"], dr["ident"], wb["wqk"].rearrange("(o p) (k c) -> o p k c", p=P, c=P),
              wb["wv"].rearrange("p (k c) -> p k c", c=512), dr["cos"], dr["sin"], qT, kT, vv)
    attn_phase(nc, "p5", L // P, x2, x3, qT, kT, vv, wb["wo"].rearrange("(n p) (k c) -> n p k c", p=P, c=512), dr["band"], dr["kbias"],
               dr["sink"], dr["maskcol"], dr["ident"])
    normT_phase(nc, "p5b", range(NGL), x3, dr["ffnn1"], hT3, dr["ident"])
    ffn_phase(nc, "p6", range(NGO), x3, x4, hT3, wb["up1"].rearrange("(o p) (k c) -> o p k c", p=P, c=P),
              wb["dn1"].rearrange("(o p) (k c) -> o p k c", p=P, c=256), dr["conv1"])
    final_phase(nc, "p7", range(NGO), x4, dr["finn"], y)
    return nc


def host_shared(inp):
    sh = dict(mix_masks())
    sh.pop("shift", None)
    sh["band"] = band_mask()
    sh["sink"] = np.ascontiguousarray(inp["att_sink"][0][None, :])
    for nm, key, i in (("mixn0", "mix_norm", 0), ("ffnn0", "ffn_norm", 0), ("mixn1", "mix_norm", 1), ("ffnn1", "ffn_norm", 1)):
        sh[nm] = np.ascontiguousarray(inp[key][i][None, :])
    sh["finn"] = np.ascontiguousarray(inp["final_norm"][None, :])
    wqk, wv = lay_qk(inp["att_w_qkv"][0])
    sh["wqk"], sh["wv"] = wqk, wv
    sh["wo"] = lay_as(inp["att_w_o"][0], 512).reshape(4 * P, KC * 512)
    for l in range(2):
        sh[f"up{l}"] = lay_up(inp["ffn_w_up"][l])
        sh[f"dn{l}"] = lay_as(inp["ffn_w_down"][l], 256).reshape(8 * P, FC * 256)
    orient = {}
    for flip in (0, 1):
        o = dict(lay_mix(inp, flip, 1 - flip))
        for l in range(2):
            cw = inp["ffn_conv_w"][l]
            o[f"conv{l}"] = lay_conv(cw[::-1] if flip else cw, inp["ffn_conv_b"][l])
        orient[flip] = o
    return sh, orient


def core_inputs(sh, orient, x_local, mask, pos, flip, NTOT, L):
    m = dict(sh)
    m.update(orient[flip])
    m["xs"] = np.ascontiguousarray(x_local, dtype=np.float32)
    m["maskrow"] = np.ascontiguousarray(mask[None, :], dtype=np.float32)
    m["maskcol"] = np.ascontiguousarray(mask.reshape(-1, P).T, dtype=np.float32)
    c, s_ = rope_tables(pos[:L])
    m["cos"], m["sin"] = c, s_
    kb = np.full((1, L + 256), NEG, np.float32)
    kb[0, P:P + L] = np.where(mask[:L] > 0, 0.0, NEG)
    m["kbias"] = kb
    return m


_NC_CACHE = {}


def run_cores(inp_w, jobs, NTOT, L, NOUT):
    sh, orient = host_shared(inp_w)
    in_maps = [core_inputs(sh, orient, x, mk, pos, fl, NTOT, L) for (x, mk, pos, fl) in jobs]
    key = (NTOT, L, NOUT)
    nc = build(NTOT, L, NOUT)
    res = run_bass_kernel_spmd(nc, in_maps, core_ids=list(range(len(jobs))))
    return [r["y"] for r in res.results]


def kernel(x_prompt, x_sample, **w):
    NTOT, L, NOUT = 16384, 8704, 8192
    inp_w = {k: np.asarray(v, dtype=np.float32) for k, v in w.items()}
    x_prompt = np.asarray(x_prompt, dtype=np.float32)
    x_sample = np.asarray(x_sample, dtype=np.float32)
    jobs = []
    ar = np.arange(NTOT)
    ones = np.ones(NTOT, np.float32)
    for b in range(2):
        jobs.append((x_prompt[b], ones, ar, 0))
        jobs.append((x_prompt[b][::-1], ones, NTOT - 1 - ar, 1))
    smask = np.concatenate([np.ones(NOUT, np.float32), np.zeros(NTOT - NOUT, np.float32)])
    for i in range(4):
        xl = np.zeros((NTOT, D), np.float32)
        xl[:NOUT] = x_sample[i]
        jobs.append((xl, smask, ar, 0))
    outs = run_cores(inp_w, jobs, NTOT, L, NOUT)
    y_prompt = np.empty((2, 2 * NOUT, D), np.float32)
    y_sample = np.empty((4, NOUT, D), np.float32)
    for b in range(2):
        y_prompt[b, :NOUT] = outs[2 * b]
        y_prompt[b, NOUT:] = outs[2 * b + 1][::-1]
    for i in range(4):
        y_sample[i] = outs[4 + i]
    return (y_prompt, y_sample)
```
